# Optimizing a Trainium2 kernel written in Bass

```python
import jax, jax.numpy as jnp
from jax import lax
import numpy as np

D_MODEL = 1024
BATCH = 16
SEQ = 256
DEPTH = 4
DEC_BATCH = 8
DEC_SEQ = 1024
PAST_LEN = 512

GRID_W = 64
MIX_W = D_MODEL
CONV_W = MIX_W // 4
ATTN_W = MIX_W // 2
POOL_W = MIX_W - CONV_W - ATTN_W
HEAD_DIM = 64
N_HEADS = ATTN_W // HEAD_DIM
N_KV_HEADS = 2
GQA_GROUP = N_HEADS // N_KV_HEADS
KV_W = N_KV_HEADS * HEAD_DIM
WINDOW = 128
BLOCK = 128
CONV_K = 31
POOL_WINDOWS = (2, 4, 8, 16)
N_POOL_GROUPS = len(POOL_WINDOWS)
POOL_GROUP_W = POOL_W // N_POOL_GROUPS
ROPE_BASE = 10000.0
EPS = 1e-6
NEG_INF = -1e30

OFF_A_VAL = 0
OFF_A_GLU = OFF_A_VAL + CONV_W
OFF_A_GATE = OFF_A_GLU + CONV_W
OFF_Q = OFF_A_GATE + CONV_W
OFF_K = OFF_Q + ATTN_W
OFF_V = OFF_K + KV_W
OFF_B_GATE = OFF_V + KV_W
OFF_C_IN = OFF_B_GATE + ATTN_W
OFF_C_GATE = OFF_C_IN + POOL_W
IN_W = OFF_C_GATE + POOL_W

kernel_name = "hybrid_dit_conv_swa_pool_step"


def rms_norm(x, g):
    xf = x.astype(jnp.float32)
    xf = xf * lax.rsqrt(jnp.mean(xf * xf, axis=-1, keepdims=True) + EPS)
    return xf.astype(x.dtype) * g


def rope_2d(x, rows, cols):
    half = HEAD_DIM // 2
    quarter = half // 2
    freqs = ROPE_BASE ** (-jnp.arange(quarter, dtype=jnp.float32) / quarter)

    def rot(xp, pos):
        ang = pos.astype(jnp.float32)[:, None] * freqs[None, :]
        cos = jnp.cos(ang)[None, :, None, :]
        sin = jnp.sin(ang)[None, :, None, :]
        x1 = xp[..., :quarter].astype(jnp.float32)
        x2 = xp[..., quarter:].astype(jnp.float32)
        return jnp.concatenate([x1 * cos - x2 * sin, x1 * sin + x2 * cos], axis=-1)

    out = jnp.concatenate([rot(x[..., :half], rows), rot(x[..., half:], cols)], axis=-1)
    return out.astype(x.dtype)


def softmax_with_sink(s, sink_kg):
    sb = jnp.broadcast_to(sink_kg[None, :, :, None, None], s.shape[:-1] + (1,))
    p = jax.nn.softmax(jnp.concatenate([s, sb], axis=-1), axis=-1)
    return p[..., :-1]


def context_attention(q, k, v, sink):
    B, S = q.shape[0], q.shape[1]
    nq = S // BLOCK
    scale = HEAD_DIM ** -0.5
    sink_kg = sink.reshape(N_KV_HEADS, GQA_GROUP).astype(jnp.float32)
    qb = q.reshape(B, nq, BLOCK, N_KV_HEADS, GQA_GROUP, HEAD_DIM).transpose(1, 0, 2, 3, 4, 5)

    def one(qblk):
        s = jnp.einsum('bqkgd,btkd->bkgqt', qblk, k, preferred_element_type=jnp.float32) * scale
        p = softmax_with_sink(s, sink_kg)
        return jnp.einsum('bkgqt,btkd->bqkgd', p.astype(v.dtype), v)

    o = lax.map(one, qb)
    return o.transpose(1, 0, 2, 3, 4, 5).reshape(B, S, ATTN_W)


def windowed_attention(q, k, v, ck, cv, sink):
    B, L = q.shape[0], q.shape[1]
    nb = L // BLOCK
    scale = HEAD_DIM ** -0.5
    sink_kg = sink.reshape(N_KV_HEADS, GQA_GROUP).astype(jnp.float32)
    qb = q.reshape(B, nb, BLOCK, N_KV_HEADS, GQA_GROUP, HEAD_DIM).transpose(1, 0, 2, 3, 4, 5)
    pad = ((0, 0), (BLOCK, BLOCK), (0, 0), (0, 0))
    kp = jnp.pad(k, pad)
    vp = jnp.pad(v, pad)

    def band(t):
        parts = [t[:, j * BLOCK: j * BLOCK + L].reshape(B, nb, BLOCK, N_KV_HEADS, HEAD_DIM) for j in range(3)]
        return jnp.concatenate(parts, axis=2).transpose(1, 0, 2, 3, 4)

    kb, vb = band(kp), band(vp)
    blk = jnp.arange(nb)[:, None]
    qpos = blk * BLOCK + jnp.arange(BLOCK)[None, :]
    kpos = (blk - 1) * BLOCK + jnp.arange(3 * BLOCK)[None, :]
    rel = kpos[:, None, :] - qpos[:, :, None]
    valid = (jnp.abs(rel) <= WINDOW) & (kpos[:, None, :] >= 0) & (kpos[:, None, :] < L)

    def one(args):
        qblk, kblk, vblk, vmask = args
        s_loc = jnp.einsum('bqkgd,bjkd->bkgqj', qblk, kblk, preferred_element_type=jnp.float32) * scale
        s_loc = jnp.where(vmask[None, None, None], s_loc, NEG_INF)
        s_ctx = jnp.einsum('bqkgd,bpkd->bkgqp', qblk, ck, preferred_element_type=jnp.float32) * scale
        p = softmax_with_sink(jnp.concatenate([s_loc, s_ctx], axis=-1), sink_kg)
        p_loc = p[..., :3 * BLOCK].astype(v.dtype)
        p_ctx = p[..., 3 * BLOCK:].astype(v.dtype)
        return (jnp.einsum('bkgqj,bjkd->bqkgd', p_loc, vblk)
                + jnp.einsum('bkgqp,bpkd->bqkgd', p_ctx, cv))

    o = lax.map(one, (qb, kb, vb, valid))
    return o.transpose(1, 0, 2, 3, 4, 5).reshape(B, L, ATTN_W)


def conv_branch(a_val, a_glu, dw, db, ln_g, ln_b, w_pw):
    u = a_val * jax.nn.sigmoid(a_glu)
    y = lax.conv_general_dilated(u, dw[:, None, :], window_strides=(1,),
                                 padding=[(CONV_K // 2, CONV_K // 2)],
                                 dimension_numbers=('NWC', 'WIO', 'NWC'),
                                 feature_group_count=CONV_W) + db
    yf = y.astype(jnp.float32)
    mu = jnp.mean(yf, axis=-1, keepdims=True)
    var = jnp.mean(jnp.square(yf - mu), axis=-1, keepdims=True)
    yn = ((yf - mu) * lax.rsqrt(var + EPS)).astype(y.dtype) * ln_g + ln_b
    return jax.nn.silu(yn) @ w_pw


def pool_branch(u, pool_w, pool_scale):
    B, L, _ = u.shape
    cs = jnp.concatenate([jnp.zeros((B, 1, POOL_W), jnp.float32),
                          jnp.cumsum(u.astype(jnp.float32), axis=1)], axis=1)
    t = jnp.arange(L)
    outs = []
    for gi, w in enumerate(POOL_WINDOWS):
        lo = jnp.clip(t - w // 2, 0, L)
        hi = jnp.clip(t + w - w // 2, 0, L)
        c0, c1 = gi * POOL_GROUP_W, (gi + 1) * POOL_GROUP_W
        c = cs[..., c0:c1]
        mean = (jnp.take(c, hi, axis=1) - jnp.take(c, lo, axis=1)) / (hi - lo).astype(jnp.float32)[None, :, None]
        outs.append((mean.astype(u.dtype) - u[..., c0:c1]) @ pool_w[gi])
    return jnp.concatenate(outs, axis=-1) * pool_scale


def mixer_layer(x, mod, attend, norm_w, w_in, conv_dw, conv_b, conv_ln_g, conv_ln_b, conv_pw,
                pool_w, pool_scale, w_out):
    B, L = x.shape[0], x.shape[1]
    shift, scl, gate = jnp.split(mod, 3, axis=-1)
    h = rms_norm(x, norm_w) * (1 + scl) + shift
    u = h @ w_in
    a_out = conv_branch(u[..., OFF_A_VAL:OFF_A_GLU], u[..., OFF_A_GLU:OFF_A_GATE],
                        conv_dw, conv_b, conv_ln_g, conv_ln_b, conv_pw) * jax.nn.silu(u[..., OFF_A_GATE:OFF_Q])
    q = u[..., OFF_Q:OFF_K].reshape(B, L, N_HEADS, HEAD_DIM)
    k = u[..., OFF_K:OFF_V].reshape(B, L, N_KV_HEADS, HEAD_DIM)
    v = u[..., OFF_V:OFF_B_GATE].reshape(B, L, N_KV_HEADS, HEAD_DIM)
    b_out = attend(q, k, v) * jax.nn.silu(u[..., OFF_B_GATE:OFF_C_IN])
    c_out = pool_branch(u[..., OFF_C_IN:OFF_C_GATE], pool_w, pool_scale) * jax.nn.silu(u[..., OFF_C_GATE:IN_W])
    y = jnp.concatenate([a_out, b_out, c_out], axis=-1) @ w_out
    return x + gate * y, k, v


def setup_inputs(seed: int = 0) -> dict:
    key = jax.random.key(seed)
    ks = jax.random.split(key, 24)
    f32 = jnp.float32
    nrm = lambda k, shape, s: jax.random.normal(k, shape, f32) * s
    return {
        "x_prompt": nrm(ks[0], (BATCH, SEQ, D_MODEL), 1.0),
        "x_sample": nrm(ks[1], (DEC_BATCH, DEC_SEQ, D_MODEL), 1.0),
        "c": nrm(ks[2], (DEC_BATCH, D_MODEL), 1.0),
        "cache_k": nrm(ks[3], (DEC_BATCH, DEPTH, PAST_LEN, N_KV_HEADS, HEAD_DIM), 1.0),
        "cache_v": nrm(ks[4], (DEC_BATCH, DEPTH, PAST_LEN, N_KV_HEADS, HEAD_DIM), 1.0),
        "c_ctx": nrm(ks[5], (D_MODEL,), 1.0),
        "w_ada": nrm(ks[6], (DEPTH, D_MODEL, 3 * D_MODEL), 0.5 * D_MODEL ** -0.5),
        "b_ada": nrm(ks[7], (DEPTH, 3 * D_MODEL), 0.01),
        "norm_w": 1.0 + nrm(ks[8], (DEPTH, D_MODEL), 0.02),
        "w_in": nrm(ks[9], (DEPTH, D_MODEL, IN_W), D_MODEL ** -0.5),
        "conv_dw": nrm(ks[10], (DEPTH, CONV_K, CONV_W), CONV_K ** -0.5),
        "conv_b": nrm(ks[11], (DEPTH, CONV_W), 0.01),
        "conv_ln_g": 1.0 + nrm(ks[12], (DEPTH, CONV_W), 0.02),
        "conv_ln_b": nrm(ks[13], (DEPTH, CONV_W), 0.01),
        "conv_pw": nrm(ks[14], (DEPTH, CONV_W, CONV_W), CONV_W ** -0.5),
        "attn_sink": nrm(ks[15], (DEPTH, N_HEADS), 0.5),
        "pool_w": nrm(ks[16], (DEPTH, N_POOL_GROUPS, POOL_GROUP_W, POOL_GROUP_W), POOL_GROUP_W ** -0.5),
        "pool_scale": 1.0 + nrm(ks[17], (DEPTH, POOL_W), 0.02),
        "w_out": nrm(ks[18], (DEPTH, MIX_W, D_MODEL), MIX_W ** -0.5),
        "final_norm_w": 1.0 + nrm(ks[19], (D_MODEL,), 0.02),
    }


def reference(x_prompt, x_sample, c, cache_k, cache_v, c_ctx, w_ada, b_ada, norm_w, w_in,
              conv_dw, conv_b, conv_ln_g, conv_ln_b, conv_pw, attn_sink, pool_w, pool_scale,
              w_out, final_norm_w):
    xp = x_prompt
    new_k, new_v = [], []
    for l in range(DEPTH):
        mod = (jax.nn.silu(c_ctx) @ w_ada[l] + b_ada[l])[None, None, :]
        sink_l = attn_sink[l]
        attend = lambda q, k, v, s=sink_l: context_attention(q, k, v, s)
        xp, k_l, v_l = mixer_layer(xp, mod, attend, norm_w[l], w_in[l], conv_dw[l], conv_b[l],
                                   conv_ln_g[l], conv_ln_b[l], conv_pw[l], pool_w[l], pool_scale[l], w_out[l])
        new_k.append(k_l)
        new_v.append(v_l)
    y_prompt = rms_norm(xp, final_norm_w)
    new_cache_k = jnp.stack(new_k, axis=1)
    new_cache_v = jnp.stack(new_v, axis=1)

    L = x_sample.shape[1]
    n_rows = L // GRID_W
    rows = jnp.broadcast_to(jnp.arange(n_rows)[:, None], (n_rows, GRID_W)).reshape(L)
    cols = jnp.broadcast_to(jnp.arange(GRID_W)[None, :], (n_rows, GRID_W)).reshape(L)
    xs = x_sample
    for l in range(DEPTH):
        mod = (jax.nn.silu(c) @ w_ada[l] + b_ada[l])[:, None, :]
        sink_l, ck_l, cv_l = attn_sink[l], cache_k[:, l], cache_v[:, l]
        attend = lambda q, k, v, s=sink_l, ck=ck_l, cv=cv_l: windowed_attention(
            rope_2d(q, rows, cols), rope_2d(k, rows, cols), v, ck, cv, s)
        xs, _, _ = mixer_layer(xs, mod, attend, norm_w[l], w_in[l], conv_dw[l], conv_b[l],
                               conv_ln_g[l], conv_ln_b[l], conv_pw[l], pool_w[l], pool_scale[l], w_out[l])
    y_sample = rms_norm(xs, final_norm_w)
    return (y_prompt, y_sample, new_cache_k, new_cache_v)
```

```python
import os
import numpy as np
import concourse.bass as bass
import concourse.mybir as mybir
from concourse.bass_utils import run_bass_kernel_spmd
from contextlib import ExitStack

F32 = mybir.dt.float32
BF16 = mybir.dt.bfloat16
AF = mybir.ActivationFunctionType
ALU = mybir.AluOpType

NCORES = 8
D = 1024
NL = 4
T = 1536
TW = 512
EPS = 1e-6
PA = 15
UA_BASE = [15, 301, 587]
WA = 587 + 1024 + 15
PC = 8
UC_BASE = [8, 280, 552]
WC = 552 + 1024 + 8
POOL_WINDOWS = (2, 4, 8, 16)

LV = 102
V_NORMW, V_BADA, V_CONVB, V_LNG, V_LNB, V_PSCALE, V_DW = 0, 8, 32, 34, 36, 38, 40
V_FNW = NL * LV
V_SINK = V_FNW + 8
V_PTAB = V_SINK + 32
NV = V_PTAB + 34
C_ID, C_RT, C_O1024, C_O256, C_MNEXT, C_MPREV = 0, 128, 256, 384, 512, 640
NCST = 768

INPROJ = ([("glu", 0), ("val", 0), ("glu", 1), ("val", 1), ("agate", 0), ("agate", 1)]
          + [("q", c) for c in range(4)] + [("kd", 0), ("kd", 1), ("ktok", 0), ("vtok", 0)]
          + [("bgate", c) for c in range(4)] + [("cin", 0), ("cin", 1), ("cgate", 0), ("cgate", 1)])
SLABS_PER_LAYER = 24 + len(INPROJ) + 8
NSL = 6
NDUMMY = int(os.environ.get('NDUMMY', '0'))
NBURST = int(os.environ.get('NBURST', '20'))
BURST_AT = [int(v) for v in os.environ.get('BURST_AT', '0').split(',') if v != '']
NDG = 8


def _slab_cols(kind, idx):
    if kind == "glu":
        return np.arange(256 + idx * 128, 256 + idx * 128 + 128)
    if kind == "val":
        return np.arange(idx * 128, idx * 128 + 128)
    if kind == "agate":
        return np.arange(512 + idx * 128, 512 + idx * 128 + 128)
    if kind == "q":
        return np.arange(768 + idx * 128, 768 + idx * 128 + 128)
    if kind == "kd":
        a = np.arange(1280 + idx * 64, 1280 + idx * 64 + 64)
        return np.concatenate([a, a])
    if kind == "ktok":
        return np.arange(1280, 1408)
    if kind == "vtok":
        return np.arange(1408, 1536)
    if kind == "bgate":
        return np.arange(1536 + idx * 128, 1536 + idx * 128 + 128)
    if kind == "cin":
        return np.arange(2048 + idx * 128, 2048 + idx * 128 + 128)
    if kind == "cgate":
        return np.arange(2304 + idx * 128, 2304 + idx * 128 + 128)
    raise ValueError(kind)


class Buf:
    __slots__ = ("name", "w", "r", "dsem", "dcnt")

    def __init__(self, name):
        self.name = name
        self.w = None
        self.r = {}
        self.dsem = None
        self.dcnt = 0


class Sched:
    def __init__(self, nc, es):
        self.nc = nc
        self.es = es
        self.eng = {"pe": nc.tensor, "act": nc.scalar, "dve": nc.vector, "pool": nc.gpsimd, "sp": nc.sync}
        self.sem = {e: es.enter_context(nc.semaphore("s_" + e)) for e in self.eng}
        self.cnt = {e: 0 for e in self.eng}
        self.seen = {e: {} for e in self.eng}
        self.out_events = {}

    def _deps(self, e, reads, writes, skip_own):
        deps = {}
        own = self.sem[e]

        def add(ev):
            if ev is None:
                return
            s, v = ev
            if deps.get(s, 0) < v:
                deps[s] = v

        for b in reads:
            add(b.w)
        for b in writes:
            if b.w is not None and not (skip_own and b.w[0] is own):
                add(b.w)
            for s, v in b.r.items():
                if not (skip_own and s is own):
                    add((s, v))
        return deps

    def _wait(self, e, deps):
        for s, v in deps.items():
            if self.seen[e].get(s, 0) >= v:
                continue
            self.eng[e].wait_ge(s, v)
            self.seen[e][s] = v

    @staticmethod
    def _record(ev, reads, writes):
        for b in reads:
            if b.r.get(ev[0], 0) < ev[1]:
                b.r[ev[0]] = ev[1]
        for b in writes:
            b.w = ev
            b.r = {}

    def op(self, e, fns, reads=(), writes=()):
        if callable(fns):
            fns = [fns]
        self._wait(e, self._deps(e, reads, writes, e == "pe"))
        ins = None
        for f in fns:
            ins = f(self.eng[e])
        ins.then_inc(self.sem[e], 1)
        self.cnt[e] += 1
        ev = (self.sem[e], self.cnt[e])
        self._record(ev, reads, writes)
        return ev

    def dma(self, q, out, in_, reads=(), writes=(), is_output=False):
        self._wait(q, self._deps(q, reads, writes, False))
        b = (list(writes) or list(reads))[0]
        if b.dsem is None:
            b.dsem = self.es.enter_context(self.nc.semaphore("d_" + b.name))
        b.dcnt += 16
        self.eng[q].dma_start(out=out, in_=in_).then_inc(b.dsem, 16)
        ev = (b.dsem, b.dcnt)
        self._record(ev, reads, writes)
        if is_output:
            self.out_events[b.dsem] = max(self.out_events.get(b.dsem, 0), b.dcnt)
        return ev

    def finish(self):
        for s, v in self.out_events.items():
            self.eng["sp"].wait_ge(s, v)


def transfer(src, dst):
    evs = {}
    for b in src:
        if b.w is not None:
            evs[b.w[0]] = max(evs.get(b.w[0], 0), b.w[1])
        for s, v in b.r.items():
            evs[s] = max(evs.get(s, 0), v)
    for b in dst:
        for s, v in evs.items():
            if b.r.get(s, 0) < v:
                b.r[s] = v


def build_program(nl=NL, stop=None):
    nc = bass.Bass("TRN2", target_bir_lowering=False)

    def dram(n, s, kind="ExternalInput"):
        return nc.dram_tensor(n, s, F32, kind=kind).ap()

    xT_d = dram("xT", [D, T])
    cc_d = dram("cc", [128, 16])
    vecs_d = dram("vecs", [128, NV])
    wslab_d = dram("wslab", [NL * SLABS_PER_LAYER, 128, 1024])
    cpw_d = dram("cpw", [128, NL * 2 * 256])
    plw_d = dram("plw", [128, NL * 2 * 128])
    ck_d = dram("ck", [NL, 512, 128])
    cv_d = dram("cv", [NL, 512, 128])
    rope_d = dram("rope", [128, 2048])
    cst_d = dram("cst", [128, NCST])
    yT_d = dram("yT", [D, T], "ExternalOutput")
    nk_d = dram("nk", [2, NL, 256, 128], "ExternalOutput")
    nv_d = dram("nv", [2, NL, 256, 128], "ExternalOutput")

    with ExitStack() as es:
        S = Sched(nc, es)

        def sb(n, s, d):
            return es.enter_context(nc.sbuf_tensor("sb_" + n, s, d))

        x = sb("x", [128, 8, T], F32)
        xb = [[Buf("x%d_%d" % (k, t)) for t in range(3)] for k in range(8)]
        hraw = sb("hraw", [128, 8 * T], BF16)
        hb = [[Buf("h%d_%d" % (k, t)) for t in range(3)] for k in range(8)]
        h_all = [b for row in hb for b in row]

        def hap(k, c0, c1):
            return hraw[:, k * T + c0:k * T + c1]

        m = sb("m", [128, 8, T], BF16)
        mb = [[Buf("m%d_%d" % (k, t)) for t in range(3)] for k in range(8)]
        uA = sb("uA", [128, 2, WA], BF16)
        uAb = [Buf("uA0"), Buf("uA1")]
        QT = sb("QT", [128, 4, T], BF16)
        qb_ = [[Buf("q%d_%d" % (c, t)) for t in range(3)] for c in range(4)]
        KT = sb("KT", [128, 2, T], BF16)
        kb_ = [[Buf("k%d_%d" % (c, t)) for t in range(3)] for c in range(2)]
        Vaug = sb("Vaug", [128, 12, 2, 128], BF16)
        vb = [Buf("v%d" % g) for g in range(3)]
        uC = sb("uC", [128, 2, WC], F32)
        uCb = [Buf("uC0"), Buf("uC1")]
        slabs = sb("slabs", [128, NSL, 8, 128], BF16)
        slb = [Buf("slab%d" % i) for i in range(NSL)]
        diag = sb("diag", [128, NDG, 128], BF16)
        dgb = [Buf("dg%d" % i) for i in range(NDG)]
        rope = sb("rope", [128, 2048], F32)
        ropeb = Buf("rope")
        cst = sb("cst", [128, NCST], BF16)
        cstb = Buf("cst")
        vec = sb("vec", [128, NV], F32)
        vecb = Buf("vec")
        ccs = sb("ccs", [128, 16], F32)
        ccb = Buf("cc")
        csil = sb("csil", [128, 8, 2], BF16)
        csilb = Buf("csil")
        modT2 = sb("modT", [128, 2, 24, 2], F32)
        modb2 = [Buf("mod0"), Buf("mod1")]
        gmod2 = sb("gmod", [128, 2, 8, 2], F32)
        gmodb2 = [Buf("gmod0"), Buf("gmod1")]
        esink = sb("esink", [128, 32], F32)
        esinkb = Buf("esink")
        cpw = sb("cpw", [128, NL * 2 * 256], BF16)
        cpwb = Buf("cpw")
        plw = sb("plw", [128, NL * 2 * 128], BF16)
        plwb = Buf("plw")
        cksrc = sb("cksrc", [128, 4, 2, 2, 64], BF16)
        cksrcb = [Buf("cksrc%d" % i) for i in range(4)]
        ckT = sb("ckT", [128, 2, 512], BF16)
        ckTb = [Buf("ckT0"), Buf("ckT1")]
        cvaug = sb("cvaug", [128, 4, 2, 128], BF16)
        cvb = [Buf("cvaug0"), Buf("cvaug1")]
        PT = sb("PT", [128, 4, 512], BF16)
        PTb = [Buf("pt%d" % i) for i in range(4)]
        Rt3 = sb("Rt3", [128, 3, 512], F32)
        Rb3 = [Buf("R0"), Buf("R1"), Buf("R2")]
        g32 = sb("g32", [128, 4, 512], F32)
        g32b = [Buf("g32_%d" % i) for i in range(4)]
        g16 = sb("g16", [128, 4, 512], BF16)
        g16b = [Buf("g16_%d" % i) for i in range(4)]
        edg = sb("edg", [128, 4, 8], F32)
        edb = [Buf("ed%d" % i) for i in range(4)]
        poolA = hraw[:, 0:2 * WC].bitcast(F32)
        poolB = hraw[:, 2 * WC:4 * WC].bitcast(F32)
        pAb, pBb = Buf("poolA"), Buf("poolB")
        NA32 = 5
        a32 = [hraw[:, 4 * WC + i * 1024:4 * WC + (i + 1) * 1024].bitcast(F32) for i in range(NA32)]
        a32b = [Buf("a32_%d" % i) for i in range(NA32)]
        s16 = sb("s16", [128, 4, 512], BF16)
        s16b = [Buf("s16_%d" % i) for i in range(4)]
        alias_all = [pAb, pBb] + a32b

        pbank = [es.enter_context(nc.psum_tensor("pb%d" % i, [128, 512], F32)) for i in range(8)]
        psb = [Buf("psb%d" % i) for i in range(8)]

        st = {"g32": 0, "g16": 0, "br32": 0, "rd": 0, "ed": 0, "dg": 0, "pt": 0, "sbr": 0, "s16": 0, "ms32": 0, "sq8": 0}

        def t32():
            i = st["g32"] % 4
            st["g32"] += 1
            return g32[:, i, :], g32b[i]

        def b32():
            i = st["br32"] % (NA32 + 1)
            st["br32"] += 1
            if i < NA32:
                return a32[i], a32b[i]
            return g32[:, 3, :], g32b[3]

        def ms32():
            i = st["ms32"] % 3
            st["ms32"] += 1
            return g32[:, i, :], g32b[i]

        def t16():
            i = st["g16"] % 4
            st["g16"] += 1
            return g16[:, i, :], g16b[i]

        def ring(name, n):
            i = st[name] % n
            st[name] += 1
            return i

        def act(out, in_, func, reads, writes, bias=None, scale=None):
            kw = {}
            if bias is not None:
                kw["bias"] = bias
            if scale is not None:
                kw["scale"] = scale
            return S.op("act", lambda e: e.activation(out=out, in_=in_, func=func, **kw), reads, writes)

        def tt(eng, out, in0, in1, op, reads, writes):
            return S.op(eng, lambda e: e.tensor_tensor(out=out, in0=in0, in1=in1, op=op), reads, writes)

        def ts(eng, out, in0, s1, s2, op0, op1, reads, writes):
            if s2 is None:
                return S.op(eng, lambda e: e.tensor_scalar(out=out, in0=in0, scalar1=s1, scalar2=None, op0=op0), reads, writes)
            return S.op(eng, lambda e: e.tensor_scalar(out=out, in0=in0, scalar1=s1, scalar2=s2, op0=op0, op1=op1), reads, writes)

        def stt(eng, out, in0, scalar, in1, op0, op1, reads, writes):
            return S.op(eng, lambda e: e.scalar_tensor_tensor(out=out, in0=in0, scalar=scalar, in1=in1, op0=op0, op1=op1), reads, writes)

        def cp(eng, out, in_, reads, writes):
            return S.op(eng, lambda e: e.tensor_copy(out=out, in_=in_), reads, writes)

        def mm(specs, reads, writes):
            fns = []
            for sp_ in specs:
                o, l, r, a, z = sp_[:5]
                if len(sp_) > 5 and sp_[5]:
                    fns.append(lambda e, o=o, l=l, r=r, a=a, z=z: e.matmul(o, l, r, start=a, stop=z, skip_group_check=True))
                else:
                    fns.append(lambda e, o=o, l=l, r=r, a=a, z=z: e.matmul(o, l, r, start=a, stop=z))
            return S.op("pe", fns, reads, writes)

        def vcol(c, rows=None):
            if rows is None:
                return vec[:, c:c + 1]
            return vec[rows, c:c + 1]

        ident = cst[:, C_ID:C_ID + 128]
        epsc = sb("epsc", [128, 1], F32)
        epsb = Buf("eps")
        S.op("dve", lambda e: e.memset(epsc[:, :], EPS), writes=[epsb])

        sl = {"issued": 0, "next": 0}
        total_slabs = 24 + nl * (SLABS_PER_LAYER - 24) + (nl - 1) * 24

        def slab_issue(i):
            s = i % NSL
            S.dma("pool", slabs[:, s, :, :], wslab_d[i].rearrange("p (k c) -> p k c", k=8), reads=(), writes=[slb[s]])

        def get_slab():
            i = sl["next"]
            sl["next"] += 1
            while sl["issued"] < min(total_slabs, i + NSL):
                slab_issue(sl["issued"])
                sl["issued"] += 1
            s = i % NSL
            return slabs[:, s, :, :], slb[s]

        S.dma("sp", vec[:, :], vecs_d[:, :], writes=[vecb])
        S.dma("sp", ccs[:, :], cc_d[:, :], writes=[ccb])
        S.dma("pool", cst[:, :], cst_d[:, :], writes=[cstb])
        for k in range(8):
            S.dma("sp", x[:, k, :], xT_d[k * 128:(k + 1) * 128, :], writes=xb[k])
        S.dma("sp", rope[:, :], rope_d[:, :], writes=[ropeb])
        S.dma("pool", cpw[:, :], cpw_d[:, :], writes=[cpwb])
        S.dma("pool", plw[:, :], plw_d[:, :], writes=[plwb])
        S.op("dve", lambda e: e.memset(uA[:, :, :], 0.0), writes=uAb)
        S.op("pool", lambda e: e.memset(uC[:, :, :], 0.0), writes=uCb)
        S.op("dve", lambda e: e.memset(Vaug[:, :, :, 64:128], 1.0), writes=vb)
        S.op("dve", lambda e: e.memset(cvaug[:, :, :, 64:128], 1.0), writes=cvb)
        act(csil[:, :, :], ccs[:, :].rearrange("p (k t) -> p k t", t=2), AF.Silu, [ccb], [csilb])
        act(esink[:, :], vec[:, V_SINK:V_SINK + 32], AF.Exp, [vecb], [esinkb])

        def tile_cols(t):
            return t * TW, (t + 1) * TW

        def sq_tile():
            i = st["sq8"] % 8
            st["sq8"] += 1
            if i < 4:
                return g16[:, i, :], g16b[i]
            return s16[:, i - 4, :], s16b[i - 4]

        def stats_act(t, k):
            c0, c1 = tile_cols(t)
            sq, sqb = sq_tile()
            act(sq, x[:, k, c0:c1], AF.Square, [xb[k][t]], [sqb])
            return sq, sqb

        def stats_mm(t, sq, sqb, first, last):
            mm([(pbank[5 + t][:, :], cst[:, C_O1024:C_O1024 + 128], sq, first, last)], [cstb, sqb], [psb[5 + t]])

        def stats_sq(t, k, first, last):
            sq, sqb = stats_act(t, k)
            stats_mm(t, sq, sqb, first, last)

        def stats_fin(t):
            act(Rt3[:, t, :], pbank[5 + t][:, :], AF.Sqrt, [psb[5 + t]], [Rb3[t]], bias=EPS)
            act(Rt3[:, t, :], Rt3[:, t, :], AF.Ln, [Rb3[t]], [Rb3[t]])
            act(Rt3[:, t, :], Rt3[:, t, :], AF.Exp, [Rb3[t]], [Rb3[t]], scale=-1.0)

        def ada_slab(l, c):
            sap, sbuf_ = get_slab()
            mm([(pbank[7][:, 2 * c:2 * c + 2], sap[:, k, :], csil[:, k, :], k == 0, k == 7) for k in range(8)],
               [sbuf_, csilb], [psb[7]])

        def ada_finish(l):
            vbase = l * LV
            modT = modT2[:, l % 2]
            modb = modb2[l % 2]
            gmod = gmod2[:, l % 2]
            gmodb = gmodb2[l % 2]
            tt("dve", modT, pbank[7][:, 0:48].rearrange("p (c t) -> p c t", t=2),
               vec[:, vbase + V_BADA:vbase + V_BADA + 24].unsqueeze(2).to_broadcast([128, 24, 2]), ALU.add,
               [psb[7], vecb], [modb])
            ts("dve", gmod, modT[:, 8:16, :], 1.0, None, ALU.add, None, [modb], [gmodb])
            tt("dve", gmod, gmod,
               vec[:, vbase + V_NORMW:vbase + V_NORMW + 8].unsqueeze(2).to_broadcast([128, 8, 2]), ALU.mult,
               [gmodb, vecb], [gmodb])

        def layer(l):
            vbase = l * LV
            modT = modT2[:, l % 2]
            modb = modb2[l % 2]
            gmod = gmod2[:, l % 2]
            gmodb = gmodb2[l % 2]
            if stop == 'ada':
                return
            transfer(alias_all, h_all)
            if l == 0:
                for t in range(3):
                    for k in range(8):
                        stats_sq(t, k, k == 0, k == 7)
            for t in range(3):
                stats_fin(t)
            cnt_n = 0
            for t in range(3):
                c0, c1 = tile_cols(t)
                col = 0 if t == 0 else 1
                for k in range(8):
                    xr, xrb = t32()
                    eng_ = "pool" if cnt_n % 3 == 2 else "dve"
                    cnt_n += 1
                    tt(eng_, xr, x[:, k, c0:c1], Rt3[:, t, :], ALU.mult, [xb[k][t], Rb3[t]], [xrb])
                    if k % 2 == 0:
                        act(hap(k, c0, c1), xr, AF.Identity, [xrb, gmodb, modb], [hb[k][t]],
                            bias=modT[:, k, col:col + 1], scale=gmod[:, k, col:col + 1])
                    else:
                        ts("dve", hap(k, c0, c1), xr, gmod[:, k, col:col + 1], modT[:, k, col:col + 1], ALU.mult, ALU.add,
                           [xrb, gmodb, modb], [hb[k][t]])

            if stop == 'norm':
                return
            for si, (kind, idx) in enumerate(INPROJ):
                if stop is not None and stop.startswith('inproj:') and si >= int(stop.split(':')[1]):
                    return
                sap, sbuf_ = get_slab()
                base = 0 if si % 2 == 0 else 3
                if kind == "ktok":
                    bank, bb = pbank[base], psb[base]
                    mm([(bank[:, tb * 128:(tb + 1) * 128], hap(k, tb * 128, (tb + 1) * 128), sap[:, k, :], k == 0, k == 7)
                        for tb in range(4) for k in range(8)],
                       [sbuf_] + [hb[k][0] for k in range(8)], [bb])
                    ko, kob = t32()
                    cp("dve", ko, bank[:, :], [bb], [kob])
                    for s_ in range(2):
                        S.dma("sp", nk_d[s_, l].rearrange("(r p) c -> p r c", p=128),
                              ko[:, s_ * 256:(s_ + 1) * 256].rearrange("p (r c) -> p r c", r=2), reads=[kob], is_output=True)
                    continue
                if kind == "vtok":
                    for g in range(3):
                        bank, bb = pbank[base + g], psb[base + g]
                        mm([(bank[:, tb * 128:(tb + 1) * 128], hap(k, (g * 4 + tb) * 128, (g * 4 + tb + 1) * 128), sap[:, k, :], k == 0, k == 7)
                            for tb in range(4) for k in range(8)],
                           [sbuf_] + [hb[k][g] for k in range(8)], [bb])
                        if g == 0:
                            vo, vob = t32()
                            cp("dve", vo, bank[:, :], [bb], [vob])
                            for s_ in range(2):
                                S.dma("sp", nv_d[s_, l].rearrange("(r p) c -> p r c", p=128),
                                      vo[:, s_ * 256:(s_ + 1) * 256].rearrange("p (r c) -> p r c", r=2), reads=[vob], is_output=True)
                        for kv in range(2):
                            if os.environ.get("VT_SKIP") == "act":
                                continue
                            cp("dve", Vaug[:, g * 4:(g + 1) * 4, kv, 0:64],
                               bank[:, :].rearrange("p (b v d) -> p b v d", b=4, v=2)[:, :, kv, :], [bb], [vb[g]])
                    continue
                banks = [pbank[base + t] for t in range(3)]
                bbs = [psb[base + t] for t in range(3)]
                mm([(banks[t][:, :], sap[:, k, :], hap(k, t * TW, (t + 1) * TW), k == 0, k == 7)
                    for k in range(8) for t in range(3)],
                   [sbuf_] + h_all, bbs)
                for t in range(3):
                    c0, c1 = tile_cols(t)
                    ps = banks[t][:, :]
                    pb_ = bbs[t]
                    if kind in ("glu", "val"):
                        if t == 0:
                            dst = uA[:, idx, PA:PA + 2 * 286].rearrange("p (s w) -> p s w", s=2)[:, :, 0:256]
                            src = ps.rearrange("p (s w) -> p s w", s=2)
                        else:
                            dst = uA[:, idx, UA_BASE[2] + (t - 1) * TW:UA_BASE[2] + t * TW]
                            src = ps
                        if kind == "glu":
                            act(dst, src, AF.Sigmoid, [pb_], [uAb[idx]])
                        else:
                            tt("dve", dst, src, dst, ALU.mult, [pb_, uAb[idx]], [uAb[idx]])
                    elif kind == "agate":
                        act(m[:, idx, c0:c1], ps, AF.Silu, [pb_], [mb[idx][t]])
                    elif kind == "bgate":
                        act(m[:, 2 + idx, c0:c1], ps, AF.Silu, [pb_], [mb[2 + idx][t]])
                    elif kind == "cgate":
                        act(m[:, 6 + idx, c0:c1], ps, AF.Silu, [pb_], [mb[6 + idx][t]])
                    elif kind == "cin":
                        if t == 0:
                            dst = uC[:, idx, PC:PC + 2 * 272].rearrange("p (s w) -> p s w", s=2)[:, :, 0:256]
                            src = ps.rearrange("p (s w) -> p s w", s=2)
                        else:
                            dst = uC[:, idx, UC_BASE[2] + (t - 1) * TW:UC_BASE[2] + t * TW]
                            src = ps
                        cp("dve", dst, src, [pb_], [uCb[idx]])
                    elif kind in ("q", "kd"):
                        if kind == "q":
                            dst, dstb = QT[:, idx, c0:c1], qb_[idx][t]
                        else:
                            dst, dstb = KT[:, idx, c0:c1], kb_[idx][t]
                        if t == 0:
                            cp("dve", dst, ps, [pb_], [dstb])
                        else:
                            r0 = (t - 1) * TW
                            qraw, qrb = t16()
                            act(qraw, ps, AF.Copy, [pb_], [qrb])
                            t1, t1b = t32()
                            tt("dve", t1, ps, rope[:, r0:r0 + TW], ALU.mult, [pb_, ropeb], [t1b])
                            rb_i = 6 + (t % 2)
                            mm([(pbank[rb_i][:, :], cst[:, C_RT:C_RT + 128], qraw, True, True)], [cstb, qrb], [psb[rb_i]])
                            t2, t2b = t32()
                            tt("dve", t2, pbank[rb_i][:, :], rope[:, 1024 + r0:1024 + r0 + TW], ALU.mult, [psb[rb_i], ropeb], [t2b])
                            tt("pool", dst, t1, t2, ALU.add, [t1b, t2b], [dstb])

            if stop == 'inproj':
                return
            transfer(h_all, alias_all)

            for kv in range(2):
                for dup in range(2):
                    S.dma("pool", cksrc[:, :, kv, dup, :], ck_d[l][:, kv * 64:(kv + 1) * 64].rearrange("(ct p) d -> p ct d", p=128),
                          writes=[cksrcb[kv * 2 + dup]])
                S.dma("pool", cvaug[:, :, kv, 0:64], cv_d[l][:, kv * 64:(kv + 1) * 64].rearrange("(ct p) d -> p ct d", p=128),
                      writes=[cvb[kv]])

            if stop == 'ctxdma':
                return
            thunks = []

            def lvl(eng, dst, src, sh_lo, sh_hi, lo, hi, rows, reads, writes):
                tt(eng, dst[rows, lo:hi], src[rows, lo + sh_lo:hi + sh_lo], src[rows, lo + sh_hi:hi + sh_hi], ALU.add, reads, writes)

            lo_r, hi_r = slice(0, 64), slice(64, 128)

            def pool_tile(j, t):
                u = uC[:, j, :]
                ub = uCb[j]
                tabc = V_PTAB + j * 17
                c0, c1 = tile_cols(t)
                d16, d16b = t16()
                if t == 0:
                    dst = d16.rearrange("p (s w) -> p s w", s=2)
                    sS = poolB[:, PC:PC + 2 * 272].rearrange("p (s w) -> p s w", s=2)[:, :, 0:256]
                    sU = uC[:, j, PC:PC + 2 * 272].rearrange("p (s w) -> p s w", s=2)[:, :, 0:256]
                else:
                    o = UC_BASE[2] + (t - 1) * TW
                    dst, sS, sU = d16, poolB[:, o:o + TW], uC[:, j, o:o + TW]
                stt("dve", dst, sS, vcol(tabc), sU, ALU.mult, ALU.subtract, [pBb, ub, vecb], [d16b])
                fixes = []
                if t == 0:
                    for s_ in range(2):
                        fixes.append((s_ * 256, UC_BASE[s_], tabc + 1))
                        fixes.append((s_ * 256 + 248, UC_BASE[s_] + 248, tabc + 9))
                elif t == 1:
                    fixes.append((0, UC_BASE[2], tabc + 1))
                else:
                    fixes.append((504, UC_BASE[2] + 1016, tabc + 9))
                for (dc, pc_, tc) in fixes:
                    ei = ring("ed", 4)
                    tt("dve", edg[:, ei, :], poolB[:, pc_:pc_ + 8], vec[:, tc:tc + 8], ALU.mult, [pBb, vecb], [edb[ei]])
                    tt("dve", d16[:, dc:dc + 8], edg[:, ei, :], uC[:, j, pc_:pc_ + 8], ALU.subtract, [edb[ei], ub, d16b], [d16b])
                ob = ring("sbr", 3)
                mm([(pbank[ob][:, :], plw[:, (l * 2 + j) * 128:(l * 2 + j + 1) * 128], d16, True, True)], [plwb, d16b], [psb[ob]])
                stt("dve", m[:, 6 + j, c0:c1], pbank[ob][:, :], vcol(vbase + V_PSCALE + j), m[:, 6 + j, c0:c1],
                    ALU.mult, ALU.mult, [psb[ob], vecb, mb[6 + j][t]], [mb[6 + j][t]])

            for j in range(2):
                u = uC[:, j, :]
                ub = uCb[j]
                L_ = lambda *a: thunks.append(lambda a=a: lvl("pool", *a))
                if j == 0:
                    L_(poolB, u, -1, 0, 1, WC, lo_r, [ub], [pBb])
                    L_(poolA, u, -1, 0, 1, WC, hi_r, [ub], [pAb])
                    L_(poolB, poolA, -1, 1, 2, WC - 1, hi_r, [pAb], [pBb])
                else:
                    L_(poolB, u, -1, 0, 1, WC, lo_r, [ub], [pBb])
                    L_(poolA, u, -1, 0, 1, WC, hi_r, [ub], [pAb])
                    L_(poolA, poolB, -1, 1, 2, WC - 1, lo_r, [pBb], [pAb])
                    L_(poolB, poolA, -1, 1, 2, WC - 1, hi_r, [pAb], [pBb])
                    L_(poolB, poolA, -2, 2, 4, WC - 3, lo_r, [pAb], [pBb])
                    L_(poolA, poolB, -2, 2, 4, WC - 3, hi_r, [pBb], [pAb])
                    L_(poolB, poolA, -4, 4, 8, WC - 7, hi_r, [pAb], [pBb])
                for t in range(3):
                    thunks.append(lambda j=j, t=t: pool_tile(j, t))

            work = []
            lev0, til0, lev1, til1 = thunks[0:3], thunks[3:6], thunks[6:13], thunks[13:16]
            thunks = []
            for f_ in lev0:
                f_()
            CB = (5, 6)

            conv_list = [(t, j, k) for t in range(3) for j in range(2) for k in range(31)]
            conv_dg = {}
            DG_AHEAD = 6

            def build_diag(n):
                if n >= len(conv_list):
                    return
                t, j, k = conv_list[n]
                di = ring("dg", NDG)
                ts("dve", diag[:, di, :], ident, vcol(vbase + V_DW + k * 2 + j), None, ALU.mult, None,
                   [cstb, vecb], [dgb[di]])
                conv_dg[n] = di

            def conv_item(n):
                t, j, k = conv_list[n]
                di = conv_dg.pop(n)
                bank, bb = pbank[CB[j]], psb[CB[j]]
                if t == 0:
                    spec = (bank[:, :].rearrange("p (s w) -> p s w", s=2), diag[:, di, :],
                            uA[:, j, k:k + 2 * 286].rearrange("p (s w) -> p s w", s=2)[:, :, 0:256], k == 0, k == 30)
                else:
                    o = UA_BASE[2] - PA + (t - 1) * TW + k
                    spec = (bank[:, :], diag[:, di, :], uA[:, j, o:o + TW], k == 0, k == 30)
                mm([spec], [dgb[di], uAb[j]], [bb])
                build_diag(n + DG_AHEAD)

            for n_ in range(DG_AHEAD):
                build_diag(n_)

            LN = {}

            def ln_A1(t):
                y32 = []
                for j in range(2):
                    bank, bb = pbank[CB[j]], psb[CB[j]]
                    y, yb = b32()
                    act(y, bank[:, :], AF.Identity, [bb, vecb], [yb], bias=vcol(vbase + V_CONVB + j))
                    y32.append((y, yb))
                LN[t] = {"y": y32}

            def ln_A2(t):
                aux = []
                for j in range(2):
                    y, yb = LN[t]["y"][j]
                    ysq, ysqb = t16()
                    act(ysq, y, AF.Square, [yb], [ysqb])
                    y16, y16b = t16()
                    act(y16, y, AF.Copy, [yb], [y16b])
                    aux.append((ysq, ysqb, y16, y16b))
                LN[t]["aux"] = aux

            def ln_A3(t):
                r6, r7 = ring("sbr", 3), ring("sbr", 3)
                for j in range(2):
                    ysq, ysqb, y16, y16b = LN[t]["aux"][j]
                    mm([(pbank[r6][:, :], cst[:, C_O256:C_O256 + 128], y16, j == 0, j == 1)], [cstb, y16b], [psb[r6]])
                    mm([(pbank[r7][:, :], cst[:, C_O256:C_O256 + 128], ysq, j == 0, j == 1)], [cstb, ysqb], [psb[r7]])
                mean, meanb = ms32()
                act(mean, pbank[r6][:, :], AF.Copy, [psb[r6]], [meanb])
                msq, msqb = ms32()
                act(msq, pbank[r6][:, :], AF.Square, [psb[r6]], [msqb])
                tt("dve", msq, pbank[r7][:, :], msq, ALU.subtract, [psb[r7], msqb], [msqb])
                act(msq, msq, AF.Ln, [msqb, epsb], [msqb], bias=epsc[:, 0:1])
                act(msq, msq, AF.Exp, [msqb], [msqb], scale=-0.5)
                for j in range(2):
                    y, yb = LN[t]["y"][j]
                    tt("dve", y, y, mean, ALU.subtract, [yb, meanb], [yb])
                    tt("dve", y, y, msq, ALU.mult, [yb, msqb], [yb])

            def ln_C(t):
                sj = []
                for j in range(2):
                    y, yb = LN[t]["y"][j]
                    si_ = ring("s16", 4)
                    act(s16[:, si_, :], y, AF.Silu, [yb, vecb], [s16b[si_]], bias=vcol(vbase + V_LNB + j), scale=vcol(vbase + V_LNG + j))
                    sj.append((s16[:, si_, :], s16b[si_]))
                LN[t]["s"] = sj

            def ln_D(t):
                c0, c1 = tile_cols(t)
                sj = LN[t]["s"]
                for jo in range(2):
                    ob = ring("sbr", 3)
                    mm([(pbank[ob][:, :], cpw[:, (l * 2 + ji) * 256 + jo * 128:(l * 2 + ji) * 256 + jo * 128 + 128], sj[ji][0], ji == 0, ji == 1)
                        for ji in range(2)], [cpwb, sj[0][1], sj[1][1]], [psb[ob]])
                    tt("dve", m[:, jo, c0:c1], pbank[ob][:, :], m[:, jo, c0:c1], ALU.mult, [psb[ob], mb[jo][t]], [mb[jo][t]])

            lnq = []
            for t in range(3):
                for j in range(2):
                    for k in range(31):
                        work.append(lambda n=(t * 2 + j) * 31 + k: conv_item(n))
                        if lnq and (j * 31 + k) % 3 == 2:
                            work.append(lnq.pop(0))
                lnq = [lambda t=t: ln_A1(t), lambda t=t: ln_A2(t), lambda t=t: ln_A3(t), lambda t=t: ln_C(t), lambda t=t: ln_D(t)]
                work.append(lnq.pop(0))
            work.extend(lnq)
            work[40:40] = til0 + lev1
            work.extend(til1)
            ln_thunks = []

            if stop == 'pool':
                return
            for kv in range(2):
                bi = 3 + kv
                mm([(pbank[bi][:, ct * 128:(ct + 1) * 128], cksrc[:, ct, kv, :, :].rearrange("p a d -> p (a d)"), ident, True, True) for ct in range(4)],
                   [cksrcb[kv * 2], cksrcb[kv * 2 + 1], cstb], [psb[bi]])
                cp("dve", ckT[:, kv, :], pbank[bi][:, :], [psb[bi]], [ckTb[kv]])

            rws = [slice(0, 64), slice(64, 128)]

            def unit_tiles(unit):
                tiles = []
                if unit[0] == "p":
                    _, s_, hd = unit
                    c, half, kvh = hd // 2, hd % 2, hd // 4
                    rows = rws[half]
                    tiles.append(dict(
                        smm=[(KT[rows, kvh, (s_ * 2 + kt) * 128:(s_ * 2 + kt + 1) * 128], QT[rows, c, s_ * 256:(s_ + 1) * 256], kt * 256, 256)
                             for kt in range(2)],
                        ncols=512, masks=[], srd=[kb_[kvh][0], qb_[c][0]],
                        pv=[(Vaug[:, s_ * 2 + kt, kvh, :], kt * 256, 256, 0, 0) for kt in range(2)], prd=[vb[0]]))
                    return tiles
                _, hd = unit
                c, half, kvh = hd // 2, hd % 2, hd // 4
                rows = rws[half]
                loc, ctx = [], []
                for j in range(8):
                    lo, hi = max(0, j - 1), min(8, j + 2)
                    n = (hi - lo) * 128
                    masks = []
                    for qbk in range(lo, hi):
                        if qbk == j - 1:
                            masks.append(((qbk - lo) * 128, C_MNEXT))
                        elif qbk == j + 1:
                            masks.append(((qbk - lo) * 128, C_MPREV))
                    pv = []
                    for b_ in range(2):
                        a0, a1 = max(lo * 128, b_ * 512), min(hi * 128, (b_ + 1) * 512)
                        if a1 > a0:
                            pv.append((Vaug[:, 4 + j, kvh, :], a0 - lo * 128, a1 - a0, b_, a0 - b_ * 512))
                    loc.append(dict(
                        smm=[(KT[rows, kvh, 512 + j * 128:512 + (j + 1) * 128], QT[rows, c, 512 + lo * 128:512 + hi * 128], 0, n)],
                        ncols=n, masks=masks, srd=[kb_[kvh][1 + j // 4], qb_[c][1], qb_[c][2]],
                        pv=pv, prd=[vb[1 + j // 4]]))
                for qh in range(2):
                    for ct in range(4):
                        ctx.append(dict(
                            smm=[(ckT[rows, kvh, ct * 128:(ct + 1) * 128], QT[rows, c, 512 + qh * 512:512 + (qh + 1) * 512], 0, 512)],
                            ncols=512, masks=[], srd=[ckTb[kvh], qb_[c][1 + qh]],
                            pv=[(cvaug[:, ct, kvh, :], 0, 512, qh, 0)], prd=[cvb[kvh]]))
                for i in range(8):
                    tiles.append(ctx[i])
                    tiles.append(loc[i])
                return tiles

            units = [("p", s_, hd) for s_ in range(2) for hd in range(8)] + [("s", hd) for hd in range(8)]
            flat = []
            for ui, unit in enumerate(units):
                tl = unit_tiles(unit)
                for ti, tile in enumerate(tl):
                    flat.append((ui, unit, tile, ti == len(tl) - 1))
            ostart = {}
            unit_first = {}
            for (ui_, unit_, tile_, last_) in flat:
                unit_first.setdefault(ui_, tile_)

            def emit_S(idx, tile):
                sbi = ring("sbr", 3)
                fill = []
                mm(fill + [(pbank[sbi][:, oc:oc + n], kt_, q_, True, True) for (kt_, q_, oc, n) in tile["smm"]],
                   tile["srd"] + [cstb, mb[0][0]], [psb[sbi]])
                pti = ring("pt", 4)
                tile["pti"] = pti
                nco = tile["ncols"]
                act(PT[:, pti, 0:nco], pbank[sbi][:, 0:nco], AF.Exp, [psb[sbi]], [PTb[pti]], scale=0.125)
                for (c0_, mk) in tile["masks"]:
                    tt("dve", PT[:, pti, c0_:c0_ + 128], PT[:, pti, c0_:c0_ + 128], cst[:, mk:mk + 128], ALU.mult,
                       [PTb[pti], cstb], [PTb[pti]])

            def norm_piece(Ob, Obb, n, hd, mcol0, qt):
                c, half = hd // 2, hd % 2
                ro = rws[half]
                rd, rdb_ = ms32()
                tmp, tmpb = ms32()
                act(rd[64:128, 0:n], Ob[64:128, 0:n], AF.Ln, [Obb, esinkb], [rdb_], bias=esink[64:128, l * 8 + hd:l * 8 + hd + 1])
                act(rd[64:128, 0:n], rd[64:128, 0:n], AF.Exp, [rdb_], [rdb_], scale=-1.0)
                tt("dve", tmp[ro, 0:n], Ob[0:64, 0:n], rd[64:128, 0:n], ALU.mult, [Obb, rdb_], [tmpb])
                tt("dve", m[ro, 2 + c, mcol0:mcol0 + n], tmp[ro, 0:n], m[ro, 2 + c, mcol0:mcol0 + n], ALU.mult,
                   [tmpb, mb[2 + c][qt]], [mb[2 + c][qt]])

            def emit_PV(idx, ui, unit, tile, last):
                pti = tile["pti"]
                oset = (3, 4) if (unit[0] == 's' or ui % 2 == 0) else (4, 3)
                for (va, pc0, n, ob_, oc0) in tile["pv"]:
                    bi = oset[ob_]
                    first = (ui, ob_) not in ostart
                    ostart[(ui, ob_)] = True
                    mm([(pbank[bi][:, oc0:oc0 + n], va, PT[:, pti, pc0:pc0 + n], first, True, True)],
                       [PTb[pti]] + tile["prd"], [psb[bi]])
                if last:
                    if unit[0] == "p":
                        _, s_, hd = unit
                        norm_piece(pbank[oset[0]], psb[oset[0]], 256, hd, s_ * 256, 0)
                    else:
                        _, hd = unit
                        for b_ in range(2):
                            norm_piece(pbank[oset[b_]], psb[oset[b_]], 512, hd, 512 + b_ * 512, 1 + b_)
                    if l + 1 < nl:
                        ada_slab(l + 1, ui)
                    if ln_thunks:
                        ln_thunks.pop(0)()

            pend = []
            wacc = [0.0]
            wrate = (len(work) + 4.0) / max(1, len(flat) - 8)
            for idx, (ui, unit, tile, last) in enumerate(flat):
                emit_S(idx, tile)
                wacc[0] += wrate
                while wacc[0] >= 1.0 and work:
                    work.pop(0)()
                    wacc[0] -= 1.0
                pend.append((idx, ui, unit, tile, last))
                if len(pend) > 2:
                    emit_PV(*pend.pop(0))
            while pend:
                emit_PV(*pend.pop(0))
            while work:
                work.pop(0)()

            if l + 1 < nl:
                ada_finish(l + 1)

            if stop == 'attn':
                return
            m_all = [b for row in mb for b in row]
            pend_stats = []
            for mo in range(8):
                sap, sbuf_ = get_slab()
                bidx = [(3 * mo + t) % 5 for t in range(3)]
                banks = [pbank[i] for i in bidx]
                bbs = [psb[i] for i in bidx]
                mm([(banks[t][:, :], sap[:, k, :], m[:, k, t * TW:(t + 1) * TW], k == 0, k == 7)
                    for k in range(8) for t in range(3)], [sbuf_] + m_all, bbs)
                for (t_, sq_, sqb_, mo_) in pend_stats:
                    stats_mm(t_, sq_, sqb_, mo_ == 0, mo_ == 7)
                pend_stats = []
                for t in range(3):
                    c0, c1 = tile_cols(t)
                    col = 0 if t == 0 else 1
                    stt("dve", x[:, mo, c0:c1], banks[t][:, :], modT[:, 16 + mo, col:col + 1], x[:, mo, c0:c1],
                        ALU.mult, ALU.add, [bbs[t], modb, xb[mo][t]], [xb[mo][t]])
                    sq_, sqb_ = stats_act(t, mo)
                    pend_stats.append((t, sq_, sqb_, mo))
            for (t_, sq_, sqb_, mo_) in pend_stats:
                stats_mm(t_, sq_, sqb_, mo_ == 0, mo_ == 7)

        for c in range(24):
            ada_slab(0, c)
        ada_finish(0)
        for l in range(nl):
            layer(l)

        for t in range(3):
            stats_fin(t)
        for t in range(3):
            c0, c1 = tile_cols(t)
            for k in range(8):
                yo, yob = t32()
                stt("dve", yo, x[:, k, c0:c1], vcol(V_FNW + k), Rt3[:, t, :], ALU.mult, ALU.mult, [xb[k][t], Rb3[t], vecb], [yob])
                S.dma("sp", yT_d[k * 128:(k + 1) * 128, c0:c1], yo, reads=[yob], is_output=True)
        S.finish()
    return nc


def _const_tables():
    cst = np.zeros((128, NCST), np.float32)
    cst[:, C_ID:C_ID + 128] = np.eye(128, dtype=np.float32)
    R = np.zeros((64, 64), np.float32)
    for d in range(16):
        R[d, d + 16] = -1.0
        R[d + 16, d] = 1.0
        R[32 + d, 48 + d] = -1.0
        R[48 + d, 32 + d] = 1.0
    Rt = np.zeros((128, 128), np.float32)
    Rt[0:64, 0:64] = R.T
    Rt[64:128, 64:128] = R.T
    cst[:, C_RT:C_RT + 128] = Rt
    cst[:, C_O1024:C_O1024 + 128] = 1.0 / 1024.0
    cst[:, C_O256:C_O256 + 128] = 1.0 / 256.0
    jk = np.arange(128)[:, None]
    r = np.arange(128)[None, :]
    cst[:, C_MNEXT:C_MNEXT + 128] = (jk <= r).astype(np.float32)
    cst[:, C_MPREV:C_MPREV + 128] = (jk >= r).astype(np.float32)
    quarter = 16
    freqs = 10000.0 ** (-np.arange(quarter, dtype=np.float64) / quarter)
    tpos = np.arange(1024)
    rows = (tpos // 64).astype(np.float64)
    cols = (tpos % 64).astype(np.float64)
    rope = np.zeros((128, 2048), np.float32)
    for p in range(128):
        d = p % 64
        pos = rows if d < 32 else cols
        ang = pos * freqs[d % 16]
        rope[p, 0:1024] = np.cos(ang).astype(np.float32)
        rope[p, 1024:2048] = np.sin(ang).astype(np.float32)
    ptab = np.zeros((128, 34), np.float32)
    for j in range(2):
        for p in range(128):
            w = POOL_WINDOWS[2 * j + p // 64]
            ptab[p, j * 17] = 1.0 / w
            for t in range(8):
                cnt = t + w // 2 if t < w // 2 else w
                ptab[p, j * 17 + 1 + t] = 1.0 / cnt
            for c in range(8):
                rr = 7 - c
                cnt = rr + 1 + w // 2 if w // 2 >= rr + 1 else w
                ptab[p, j * 17 + 9 + c] = 1.0 / cnt
    return cst, rope, ptab


def _colvec(v, n):
    return np.ascontiguousarray(np.asarray(v, np.float32).reshape(n, 128).T)


_PROG = {}


def kernel(x_prompt, x_sample, c, cache_k, cache_v, c_ctx, w_ada, b_ada, norm_w, w_in,
           conv_dw, conv_b, conv_ln_g, conv_ln_b, conv_pw, attn_sink, pool_w, pool_scale,
           w_out, final_norm_w, _nl=NL, _stop=None):
    f = lambda a: np.asarray(a, np.float32)
    x_prompt, x_sample, c, cache_k, cache_v, c_ctx = map(f, (x_prompt, x_sample, c, cache_k, cache_v, c_ctx))
    w_ada, b_ada, norm_w, w_in, conv_dw, conv_b = map(f, (w_ada, b_ada, norm_w, w_in, conv_dw, conv_b))
    conv_ln_g, conv_ln_b, conv_pw, attn_sink, pool_w, pool_scale = map(f, (conv_ln_g, conv_ln_b, conv_pw, attn_sink, pool_w, pool_scale))
    w_out, final_norm_w = f(w_out), f(final_norm_w)
    nl = _nl
    cst, rope, ptab = _const_tables()

    wslab = np.empty((NL * SLABS_PER_LAYER, 128, 1024), np.float32)

    def put(i, W, cols):
        wslab[i] = W[:, cols].reshape(8, 128, 128).transpose(1, 0, 2).reshape(128, 1024)

    i = 0
    for cc_ in range(24):
        put(i, w_ada[0], np.arange(cc_ * 128, cc_ * 128 + 128))
        i += 1
    for l in range(NL):
        for kind, idx in INPROJ:
            put(i, w_in[l], _slab_cols(kind, idx))
            i += 1
        if l + 1 < NL:
            for cc_ in range(24):
                put(i, w_ada[l + 1], np.arange(cc_ * 128, cc_ * 128 + 128))
                i += 1
        for mo in range(8):
            put(i, w_out[l], np.arange(mo * 128, mo * 128 + 128))
            i += 1

    vecs_base = np.zeros((128, NV), np.float32)
    for l in range(NL):
        b = l * LV
        vecs_base[:, b + V_NORMW:b + V_NORMW + 8] = _colvec(norm_w[l], 8)
        vecs_base[:, b + V_BADA:b + V_BADA + 24] = _colvec(b_ada[l], 24)
        vecs_base[:, b + V_CONVB:b + V_CONVB + 2] = _colvec(conv_b[l], 2)
        vecs_base[:, b + V_LNG:b + V_LNG + 2] = _colvec(conv_ln_g[l], 2)
        vecs_base[:, b + V_LNB:b + V_LNB + 2] = _colvec(conv_ln_b[l], 2)
        vecs_base[:, b + V_PSCALE:b + V_PSCALE + 2] = _colvec(pool_scale[l], 2)
        for k in range(31):
            vecs_base[:, b + V_DW + k * 2:b + V_DW + k * 2 + 2] = _colvec(conv_dw[l, k], 2)
    vecs_base[:, V_FNW:V_FNW + 8] = _colvec(final_norm_w, 8)
    vecs_base[:, V_SINK:V_SINK + 32] = np.broadcast_to(attn_sink.reshape(1, 32), (128, 32))
    vecs_base[:, V_PTAB:V_PTAB + 34] = ptab

    cpw = np.zeros((128, NL, 2, 256), np.float32)
    plw = np.zeros((128, NL, 2, 128), np.float32)
    for l in range(NL):
        for ji in range(2):
            cpw[:, l, ji, :] = conv_pw[l, ji * 128:(ji + 1) * 128, :]
        for g in range(4):
            j, hf = g // 2, g % 2
            plw[hf * 64:(hf + 1) * 64, l, j, hf * 64:(hf + 1) * 64] = pool_w[l, g]
    cpw = cpw.reshape(128, -1)
    plw = plw.reshape(128, -1)

    in_maps = []
    for i in range(NCORES):
        xt = np.concatenate([x_prompt[2 * i], x_prompt[2 * i + 1], x_sample[i]], axis=0)
        cc = np.stack([_colvec(c_ctx, 8), _colvec(c[i], 8)], axis=2).reshape(128, 16)
        in_maps.append({
            "xT": np.ascontiguousarray(xt.T),
            "cc": np.ascontiguousarray(cc),
            "vecs": vecs_base,
            "wslab": wslab,
            "cpw": cpw,
            "plw": plw,
            "ck": np.ascontiguousarray(cache_k[i].reshape(NL, 512, 128)),
            "cv": np.ascontiguousarray(cache_v[i].reshape(NL, 512, 128)),
            "rope": rope,
            "cst": cst,
        })
    if (nl, _stop) not in _PROG:
        _PROG[(nl, _stop)] = build_program(nl, _stop)
    res = run_bass_kernel_spmd(_PROG[(nl, _stop)], in_maps, core_ids=list(range(NCORES)))
    y_prompt = np.empty((16, 256, D), np.float32)
    y_sample = np.empty((8, 1024, D), np.float32)
    nk = np.empty((16, NL, 256, 2, 64), np.float32)
    nv = np.empty((16, NL, 256, 2, 64), np.float32)
    for i in range(NCORES):
        r = res.results[i]
        yt = np.asarray(r["yT"]).T
        y_prompt[2 * i] = yt[0:256]
        y_prompt[2 * i + 1] = yt[256:512]
        y_sample[i] = yt[512:1536]
        nk[2 * i:2 * i + 2] = np.asarray(r["nk"]).reshape(2, NL, 256, 2, 64)
        nv[2 * i:2 * i + 2] = np.asarray(r["nv"]).reshape(2, NL, 256, 2, 64)
    return (y_prompt, y_sample, nk, nv)
```

```python
import os
import numpy as np
import concourse.bass as bass
import concourse.mybir as mybir
from concourse.bass_utils import run_bass_kernel_spmd
from contextlib import ExitStack

F32 = mybir.dt.float32
BF16 = mybir.dt.bfloat16
AF = mybir.ActivationFunctionType
ALU = mybir.AluOpType

NCORES = 8
D = 1024
NL = 4
T = 1536
TW = 512
EPS = 1e-6
PA = 15
UA_BASE = [15, 301, 587]
WA = 587 + 1024 + 15
PC = 8
UC_BASE = [8, 280, 552]
WC = 552 + 1024 + 8
POOL_WINDOWS = (2, 4, 8, 16)

LV = 102
V_NORMW, V_BADA, V_CONVB, V_LNG, V_LNB, V_PSCALE, V_DW = 0, 8, 32, 34, 36, 38, 40
V_FNW = NL * LV
V_SINK = V_FNW + 8
V_PTAB = V_SINK + 32
NV = V_PTAB + 34
C_ID, C_RT, C_O1024, C_O256, C_MNEXT, C_MPREV = 0, 128, 256, 384, 512, 640
NCST = 768

INPROJ = ([("glu", 0), ("val", 0), ("glu", 1), ("val", 1), ("agate", 0), ("agate", 1)]
          + [("q", c) for c in range(4)] + [("kd", 0), ("kd", 1), ("ktok", 0), ("vtok", 0)]
          + [("bgate", c) for c in range(4)] + [("cin", 0), ("cin", 1), ("cgate", 0), ("cgate", 1)])
SLABS_PER_LAYER = 24 + len(INPROJ) + 8
NSL = 6
NDUMMY = int(os.environ.get('NDUMMY', '0'))
NBURST = int(os.environ.get('NBURST', '20'))
BURST_AT = [int(v) for v in os.environ.get('BURST_AT', '0').split(',') if v != '']
NDG = 8


def _slab_cols(kind, idx):
    if kind == "glu":
        return np.arange(256 + idx * 128, 256 + idx * 128 + 128)
    if kind == "val":
        return np.arange(idx * 128, idx * 128 + 128)
    if kind == "agate":
        return np.arange(512 + idx * 128, 512 + idx * 128 + 128)
    if kind == "q":
        return np.arange(768 + idx * 128, 768 + idx * 128 + 128)
    if kind == "kd":
        a = np.arange(1280 + idx * 64, 1280 + idx * 64 + 64)
        return np.concatenate([a, a])
    if kind == "ktok":
        return np.arange(1280, 1408)
    if kind == "vtok":
        return np.arange(1408, 1536)
    if kind == "bgate":
        return np.arange(1536 + idx * 128, 1536 + idx * 128 + 128)
    if kind == "cin":
        return np.arange(2048 + idx * 128, 2048 + idx * 128 + 128)
    if kind == "cgate":
        return np.arange(2304 + idx * 128, 2304 + idx * 128 + 128)
    raise ValueError(kind)


class Buf:
    __slots__ = ("name", "w", "r", "dsem", "dcnt")

    def __init__(self, name):
        self.name = name
        self.w = None
        self.r = {}
        self.dsem = None
        self.dcnt = 0


class Sched:
    def __init__(self, nc, es):
        self.nc = nc
        self.es = es
        self.eng = {"pe": nc.tensor, "act": nc.scalar, "dve": nc.vector, "pool": nc.gpsimd, "sp": nc.sync}
        self.sem = {e: es.enter_context(nc.semaphore("s_" + e)) for e in self.eng}
        self.cnt = {e: 0 for e in self.eng}
        self.seen = {e: {} for e in self.eng}
        self.out_events = {}

    def _deps(self, e, reads, writes, skip_own):
        deps = {}
        own = self.sem[e]

        def add(ev):
            if ev is None:
                return
            s, v = ev
            if deps.get(s, 0) < v:
                deps[s] = v

        for b in reads:
            add(b.w)
        for b in writes:
            if b.w is not None and not (skip_own and b.w[0] is own):
                add(b.w)
            for s, v in b.r.items():
                if not (skip_own and s is own):
                    add((s, v))
        return deps

    def _wait(self, e, deps):
        for s, v in deps.items():
            if self.seen[e].get(s, 0) >= v:
                continue
            self.eng[e].wait_ge(s, v)
            self.seen[e][s] = v

    @staticmethod
    def _record(ev, reads, writes):
        for b in reads:
            if b.r.get(ev[0], 0) < ev[1]:
                b.r[ev[0]] = ev[1]
        for b in writes:
            b.w = ev
            b.r = {}

    def op(self, e, fns, reads=(), writes=()):
        if callable(fns):
            fns = [fns]
        self._wait(e, self._deps(e, reads, writes, e == "pe"))
        ins = None
        for f in fns:
            ins = f(self.eng[e])
        ins.then_inc(self.sem[e], 1)
        self.cnt[e] += 1
        ev = (self.sem[e], self.cnt[e])
        self._record(ev, reads, writes)
        return ev

    def dma(self, q, out, in_, reads=(), writes=(), is_output=False):
        self._wait(q, self._deps(q, reads, writes, False))
        b = (list(writes) or list(reads))[0]
        if b.dsem is None:
            b.dsem = self.es.enter_context(self.nc.semaphore("d_" + b.name))
        b.dcnt += 16
        self.eng[q].dma_start(out=out, in_=in_).then_inc(b.dsem, 16)
        ev = (b.dsem, b.dcnt)
        self._record(ev, reads, writes)
        if is_output:
            self.out_events[b.dsem] = max(self.out_events.get(b.dsem, 0), b.dcnt)
        return ev

    def finish(self):
        for s, v in self.out_events.items():
            self.eng["sp"].wait_ge(s, v)


def transfer(src, dst):
    evs = {}
    for b in src:
        if b.w is not None:
            evs[b.w[0]] = max(evs.get(b.w[0], 0), b.w[1])
        for s, v in b.r.items():
            evs[s] = max(evs.get(s, 0), v)
    for b in dst:
        for s, v in evs.items():
            if b.r.get(s, 0) < v:
                b.r[s] = v


def build_program(nl=NL, stop=None):
    nc = bass.Bass("TRN2", target_bir_lowering=False)

    def dram(n, s, kind="ExternalInput"):
        return nc.dram_tensor(n, s, F32, kind=kind).ap()

    xT_d = dram("xT", [D, T])
    cc_d = dram("cc", [128, 16])
    vecs_d = dram("vecs", [128, NV])
    wslab_d = dram("wslab", [NL * SLABS_PER_LAYER, 128, 1024])
    cpw_d = dram("cpw", [128, NL * 2 * 256])
    plw_d = dram("plw", [128, NL * 2 * 128])
    ck_d = dram("ck", [NL, 512, 128])
    cv_d = dram("cv", [NL, 512, 128])
    rope_d = dram("rope", [128, 2048])
    cst_d = dram("cst", [128, NCST])
    yT_d = dram("yT", [D, T], "ExternalOutput")
    nk_d = dram("nk", [2, NL, 256, 128], "ExternalOutput")
    nv_d = dram("nv", [2, NL, 256, 128], "ExternalOutput")

    with ExitStack() as es:
        S = Sched(nc, es)

        def sb(n, s, d):
            return es.enter_context(nc.sbuf_tensor("sb_" + n, s, d))

        x = sb("x", [128, 8, T], F32)
        xb = [[Buf("x%d_%d" % (k, t)) for t in range(3)] for k in range(8)]
        hraw = sb("hraw", [128, 8 * T], BF16)
        hb = [[Buf("h%d_%d" % (k, t)) for t in range(3)] for k in range(8)]
        h_all = [b for row in hb for b in row]

        def hap(k, c0, c1):
            return hraw[:, k * T + c0:k * T + c1]

        m = sb("m", [128, 8, T], BF16)
        mb = [[Buf("m%d_%d" % (k, t)) for t in range(3)] for k in range(8)]
        uA = sb("uA", [128, 2, WA], BF16)
        uAb = [Buf("uA0"), Buf("uA1")]
        QT = sb("QT", [128, 4, T], BF16)
        qb_ = [[Buf("q%d_%d" % (c, t)) for t in range(3)] for c in range(4)]
        KT = sb("KT", [128, 2, T], BF16)
        kb_ = [[Buf("k%d_%d" % (c, t)) for t in range(3)] for c in range(2)]
        Vaug = sb("Vaug", [128, 12, 2, 128], BF16)
        vb = [Buf("v%d" % g) for g in range(3)]
        uC = sb("uC", [128, 2, WC], F32)
        uCb = [Buf("uC0"), Buf("uC1")]
        slabs = sb("slabs", [128, NSL, 8, 128], BF16)
        slb = [Buf("slab%d" % i) for i in range(NSL)]
        diag = sb("diag", [128, NDG, 128], BF16)
        dgb = [Buf("dg%d" % i) for i in range(NDG)]
        rope = sb("rope", [128, 2048], F32)
        ropeb = Buf("rope")
        cst = sb("cst", [128, NCST], BF16)
        cstb = Buf("cst")
        vec = sb("vec", [128, NV], F32)
        vecb = Buf("vec")
        ccs = sb("ccs", [128, 16], F32)
        ccb = Buf("cc")
        csil = sb("csil", [128, 8, 2], BF16)
        csilb = Buf("csil")
        modT2 = sb("modT", [128, 2, 24, 2], F32)
        modb2 = [Buf("mod0"), Buf("mod1")]
        gmod2 = sb("gmod", [128, 2, 8, 2], F32)
        gmodb2 = [Buf("gmod0"), Buf("gmod1")]
        esink = sb("esink", [128, 32], F32)
        esinkb = Buf("esink")
        cpw = sb("cpw", [128, NL * 2 * 256], BF16)
        cpwb = Buf("cpw")
        plw = sb("plw", [128, NL * 2 * 128], BF16)
        plwb = Buf("plw")
        cksrc = sb("cksrc", [128, 4, 2, 2, 64], BF16)
        cksrcb = [Buf("cksrc%d" % i) for i in range(4)]
        ckT = sb("ckT", [128, 2, 512], BF16)
        ckTb = [Buf("ckT0"), Buf("ckT1")]
        cvaug = sb("cvaug", [128, 4, 2, 128], BF16)
        cvb = [Buf("cvaug0"), Buf("cvaug1")]
        PT = sb("PT", [128, 4, 512], BF16)
        PTb = [Buf("pt%d" % i) for i in range(4)]
        Rt3 = sb("Rt3", [128, 3, 512], F32)
        Rb3 = [Buf("R0"), Buf("R1"), Buf("R2")]
        g32 = sb("g32", [128, 4, 512], F32)
        g32b = [Buf("g32_%d" % i) for i in range(4)]
        g16 = sb("g16", [128, 4, 512], BF16)
        g16b = [Buf("g16_%d" % i) for i in range(4)]
        edg = sb("edg", [128, 4, 8], F32)
        edb = [Buf("ed%d" % i) for i in range(4)]
        poolA = hraw[:, 0:2 * WC].bitcast(F32)
        poolB = hraw[:, 2 * WC:4 * WC].bitcast(F32)
        pAb, pBb = Buf("poolA"), Buf("poolB")
        NA32 = 5
        a32 = [hraw[:, 4 * WC + i * 1024:4 * WC + (i + 1) * 1024].bitcast(F32) for i in range(NA32)]
        a32b = [Buf("a32_%d" % i) for i in range(NA32)]
        s16 = sb("s16", [128, 4, 512], BF16)
        s16b = [Buf("s16_%d" % i) for i in range(4)]
        alias_all = [pAb, pBb] + a32b

        pbank = [es.enter_context(nc.psum_tensor("pb%d" % i, [128, 512], F32)) for i in range(8)]
        psb = [Buf("psb%d" % i) for i in range(8)]

        st = {"g32": 0, "g16": 0, "br32": 0, "rd": 0, "ed": 0, "dg": 0, "pt": 0, "sbr": 0, "s16": 0, "ms32": 0, "sq8": 0}

        def t32():
            i = st["g32"] % 4
            st["g32"] += 1
            return g32[:, i, :], g32b[i]

        def b32():
            i = st["br32"] % (NA32 + 1)
            st["br32"] += 1
            if i < NA32:
                return a32[i], a32b[i]
            return g32[:, 3, :], g32b[3]

        def ms32():
            i = st["ms32"] % 3
            st["ms32"] += 1
            return g32[:, i, :], g32b[i]

        def t16():
            i = st["g16"] % 4
            st["g16"] += 1
            return g16[:, i, :], g16b[i]

        def ring(name, n):
            i = st[name] % n
            st[name] += 1
            return i

        def act(out, in_, func, reads, writes, bias=None, scale=None):
            kw = {}
            if bias is not None:
                kw["bias"] = bias
            if scale is not None:
                kw["scale"] = scale
            return S.op("act", lambda e: e.activation(out=out, in_=in_, func=func, **kw), reads, writes)

        def tt(eng, out, in0, in1, op, reads, writes):
            return S.op(eng, lambda e: e.tensor_tensor(out=out, in0=in0, in1=in1, op=op), reads, writes)

        def ts(eng, out, in0, s1, s2, op0, op1, reads, writes):
            if s2 is None:
                return S.op(eng, lambda e: e.tensor_scalar(out=out, in0=in0, scalar1=s1, scalar2=None, op0=op0), reads, writes)
            return S.op(eng, lambda e: e.tensor_scalar(out=out, in0=in0, scalar1=s1, scalar2=s2, op0=op0, op1=op1), reads, writes)

        def stt(eng, out, in0, scalar, in1, op0, op1, reads, writes):
            return S.op(eng, lambda e: e.scalar_tensor_tensor(out=out, in0=in0, scalar=scalar, in1=in1, op0=op0, op1=op1), reads, writes)

        def cp(eng, out, in_, reads, writes):
            return S.op(eng, lambda e: e.tensor_copy(out=out, in_=in_), reads, writes)

        def mm(specs, reads, writes):
            fns = []
            for sp_ in specs:
                o, l, r, a, z = sp_[:5]
                if len(sp_) > 5 and sp_[5]:
                    fns.append(lambda e, o=o, l=l, r=r, a=a, z=z: e.matmul(o, l, r, start=a, stop=z, skip_group_check=True))
                else:
                    fns.append(lambda e, o=o, l=l, r=r, a=a, z=z: e.matmul(o, l, r, start=a, stop=z))
            return S.op("pe", fns, reads, writes)

        def vcol(c, rows=None):
            if rows is None:
                return vec[:, c:c + 1]
            return vec[rows, c:c + 1]

        ident = cst[:, C_ID:C_ID + 128]
        epsc = sb("epsc", [128, 1], F32)
        epsb = Buf("eps")
        S.op("dve", lambda e: e.memset(epsc[:, :], EPS), writes=[epsb])

        sl = {"issued": 0, "next": 0}
        total_slabs = 24 + nl * (SLABS_PER_LAYER - 24) + (nl - 1) * 24

        def slab_issue(i):
            s = i % NSL
            S.dma("pool", slabs[:, s, :, :], wslab_d[i].rearrange("p (k c) -> p k c", k=8), reads=(), writes=[slb[s]])

        def get_slab():
            i = sl["next"]
            sl["next"] += 1
            while sl["issued"] < min(total_slabs, i + NSL):
                slab_issue(sl["issued"])
                sl["issued"] += 1
            s = i % NSL
            return slabs[:, s, :, :], slb[s]

        S.dma("sp", vec[:, :], vecs_d[:, :], writes=[vecb])
        S.dma("sp", ccs[:, :], cc_d[:, :], writes=[ccb])
        S.dma("pool", cst[:, :], cst_d[:, :], writes=[cstb])
        for k in range(8):
            S.dma("sp", x[:, k, :], xT_d[k * 128:(k + 1) * 128, :], writes=xb[k])
        S.dma("sp", rope[:, :], rope_d[:, :], writes=[ropeb])
        S.dma("pool", cpw[:, :], cpw_d[:, :], writes=[cpwb])
        S.dma("pool", plw[:, :], plw_d[:, :], writes=[plwb])
        S.op("dve", lambda e: e.memset(uA[:, :, :], 0.0), writes=uAb)
        S.op("pool", lambda e: e.memset(uC[:, :, :], 0.0), writes=uCb)
        S.op("dve", lambda e: e.memset(Vaug[:, :, :, 64:128], 1.0), writes=vb)
        S.op("dve", lambda e: e.memset(cvaug[:, :, :, 64:128], 1.0), writes=cvb)
        act(csil[:, :, :], ccs[:, :].rearrange("p (k t) -> p k t", t=2), AF.Silu, [ccb], [csilb])
        act(esink[:, :], vec[:, V_SINK:V_SINK + 32], AF.Exp, [vecb], [esinkb])

        def tile_cols(t):
            return t * TW, (t + 1) * TW

        def sq_tile():
            i = st["sq8"] % 8
            st["sq8"] += 1
            if i < 4:
                return g16[:, i, :], g16b[i]
            return s16[:, i - 4, :], s16b[i - 4]

        def stats_act(t, k):
            c0, c1 = tile_cols(t)
            sq, sqb = sq_tile()
            act(sq, x[:, k, c0:c1], AF.Square, [xb[k][t]], [sqb])
            return sq, sqb

        def stats_mm(t, sq, sqb, first, last):
            mm([(pbank[5 + t][:, :], cst[:, C_O1024:C_O1024 + 128], sq, first, last)], [cstb, sqb], [psb[5 + t]])

        def stats_sq(t, k, first, last):
            sq, sqb = stats_act(t, k)
            stats_mm(t, sq, sqb, first, last)

        def stats_fin(t):
            act(Rt3[:, t, :], pbank[5 + t][:, :], AF.Sqrt, [psb[5 + t]], [Rb3[t]], bias=EPS)
            act(Rt3[:, t, :], Rt3[:, t, :], AF.Ln, [Rb3[t]], [Rb3[t]])
            act(Rt3[:, t, :], Rt3[:, t, :], AF.Exp, [Rb3[t]], [Rb3[t]], scale=-1.0)

        def ada_slab(l, c, ab=7):
            sap, sbuf_ = get_slab()
            mm([(pbank[ab][:, 2 * c:2 * c + 2], sap[:, k, :], csil[:, k, :], k == 0, k == 7) for k in range(8)],
               [sbuf_, csilb], [psb[ab]])

        def ada_finish(l, ab=7):
            vbase = l * LV
            modT = modT2[:, l % 2]
            modb = modb2[l % 2]
            gmod = gmod2[:, l % 2]
            gmodb = gmodb2[l % 2]
            tt("dve", modT, pbank[ab][:, 0:48].rearrange("p (c t) -> p c t", t=2),
               vec[:, vbase + V_BADA:vbase + V_BADA + 24].unsqueeze(2).to_broadcast([128, 24, 2]), ALU.add,
               [psb[ab], vecb], [modb])
            ts("dve", gmod, modT[:, 8:16, :], 1.0, None, ALU.add, None, [modb], [gmodb])
            tt("dve", gmod, gmod,
               vec[:, vbase + V_NORMW:vbase + V_NORMW + 8].unsqueeze(2).to_broadcast([128, 8, 2]), ALU.mult,
               [gmodb, vecb], [gmodb])

        def layer(l):
            vbase = l * LV
            modT = modT2[:, l % 2]
            modb = modb2[l % 2]
            gmod = gmod2[:, l % 2]
            gmodb = gmodb2[l % 2]
            if stop == 'ada':
                return
            transfer(alias_all, h_all)
            for t in range(3):
                stats_fin(t)
            cnt_n = 0
            for k in range(8):
                for t in range(3):
                    c0, c1 = tile_cols(t)
                    col = 0 if t == 0 else 1
                    xr, xrb = t32()
                    eng_ = "pool" if (k + t) % 3 == 2 else "dve"
                    cnt_n += 1
                    tt(eng_, xr, x[:, k, c0:c1], Rt3[:, t, :], ALU.mult, [xb[k][t], Rb3[t]], [xrb])
                    if k % 2 == 0:
                        act(hap(k, c0, c1), xr, AF.Identity, [xrb, gmodb, modb], [hb[k][t]],
                            bias=modT[:, k, col:col + 1], scale=gmod[:, k, col:col + 1])
                    else:
                        ts("dve", hap(k, c0, c1), xr, gmod[:, k, col:col + 1], modT[:, k, col:col + 1], ALU.mult, ALU.add,
                           [xrb, gmodb, modb], [hb[k][t]])

            if stop == 'norm':
                return
            for si, (kind, idx) in enumerate(INPROJ):
                if stop is not None and stop.startswith('inproj:') and si >= int(stop.split(':')[1]):
                    return
                sap, sbuf_ = get_slab()
                base = 0 if si % 2 == 0 else 3
                if kind == "ktok":
                    bank, bb = pbank[base], psb[base]
                    mm([(bank[:, tb * 128:(tb + 1) * 128], hap(k, tb * 128, (tb + 1) * 128), sap[:, k, :], k == 0, k == 7)
                        for tb in range(4) for k in range(8)],
                       [sbuf_] + [hb[k][0] for k in range(8)], [bb])
                    ko, kob = t32()
                    cp("dve", ko, bank[:, :], [bb], [kob])
                    for s_ in range(2):
                        S.dma("sp", nk_d[s_, l].rearrange("(r p) c -> p r c", p=128),
                              ko[:, s_ * 256:(s_ + 1) * 256].rearrange("p (r c) -> p r c", r=2), reads=[kob], is_output=True)
                    continue
                if kind == "vtok":
                    for g in range(3):
                        bank, bb = pbank[base + g], psb[base + g]
                        mm([(bank[:, tb * 128:(tb + 1) * 128], hap(k, (g * 4 + tb) * 128, (g * 4 + tb + 1) * 128), sap[:, k, :], k == 0, k == 7)
                            for tb in range(4) for k in range(8)],
                           [sbuf_] + [hb[k][g] for k in range(8)], [bb])
                        if g == 0:
                            vo, vob = t32()
                            cp("dve", vo, bank[:, :], [bb], [vob])
                            for s_ in range(2):
                                S.dma("sp", nv_d[s_, l].rearrange("(r p) c -> p r c", p=128),
                                      vo[:, s_ * 256:(s_ + 1) * 256].rearrange("p (r c) -> p r c", r=2), reads=[vob], is_output=True)
                        for kv in range(2):
                            if os.environ.get("VT_SKIP") == "act":
                                continue
                            cp("dve", Vaug[:, g * 4:(g + 1) * 4, kv, 0:64],
                               bank[:, :].rearrange("p (b v d) -> p b v d", b=4, v=2)[:, :, kv, :], [bb], [vb[g]])
                    continue
                banks = [pbank[base + t] for t in range(3)]
                bbs = [psb[base + t] for t in range(3)]
                if si == 0:
                    for k in range(8):
                        mm([(banks[t][:, :], sap[:, k, :], hap(k, t * TW, (t + 1) * TW), k == 0, k == 7) for t in range(3)],
                           [sbuf_] + hb[k], bbs)
                else:
                    mm([(banks[t][:, :], sap[:, k, :], hap(k, t * TW, (t + 1) * TW), k == 0, k == 7)
                        for k in range(8) for t in range(3)],
                       [sbuf_] + h_all, bbs)
                for t in range(3):
                    c0, c1 = tile_cols(t)
                    ps = banks[t][:, :]
                    pb_ = bbs[t]
                    if kind in ("glu", "val"):
                        if t == 0:
                            dst = uA[:, idx, PA:PA + 2 * 286].rearrange("p (s w) -> p s w", s=2)[:, :, 0:256]
                            src = ps.rearrange("p (s w) -> p s w", s=2)
                        else:
                            dst = uA[:, idx, UA_BASE[2] + (t - 1) * TW:UA_BASE[2] + t * TW]
                            src = ps
                        if kind == "glu":
                            act(dst, src, AF.Sigmoid, [pb_], [uAb[idx]])
                        else:
                            tt("dve", dst, src, dst, ALU.mult, [pb_, uAb[idx]], [uAb[idx]])
                    elif kind == "agate":
                        act(m[:, idx, c0:c1], ps, AF.Silu, [pb_], [mb[idx][t]])
                    elif kind == "bgate":
                        act(m[:, 2 + idx, c0:c1], ps, AF.Silu, [pb_], [mb[2 + idx][t]])
                    elif kind == "cgate":
                        act(m[:, 6 + idx, c0:c1], ps, AF.Silu, [pb_], [mb[6 + idx][t]])
                    elif kind == "cin":
                        if t == 0:
                            dst = uC[:, idx, PC:PC + 2 * 272].rearrange("p (s w) -> p s w", s=2)[:, :, 0:256]
                            src = ps.rearrange("p (s w) -> p s w", s=2)
                        else:
                            dst = uC[:, idx, UC_BASE[2] + (t - 1) * TW:UC_BASE[2] + t * TW]
                            src = ps
                        cp("dve", dst, src, [pb_], [uCb[idx]])
                    elif kind in ("q", "kd"):
                        if kind == "q":
                            dst, dstb = QT[:, idx, c0:c1], qb_[idx][t]
                        else:
                            dst, dstb = KT[:, idx, c0:c1], kb_[idx][t]
                        if t == 0:
                            cp("dve", dst, ps, [pb_], [dstb])
                        else:
                            r0 = (t - 1) * TW
                            qraw, qrb = t16()
                            act(qraw, ps, AF.Copy, [pb_], [qrb])
                            t1, t1b = t32()
                            tt("dve", t1, ps, rope[:, r0:r0 + TW], ALU.mult, [pb_, ropeb], [t1b])
                            rb_i = 6 + (t % 2)
                            mm([(pbank[rb_i][:, :], cst[:, C_RT:C_RT + 128], qraw, True, True)], [cstb, qrb], [psb[rb_i]])
                            t2, t2b = t32()
                            tt("dve", t2, pbank[rb_i][:, :], rope[:, 1024 + r0:1024 + r0 + TW], ALU.mult, [psb[rb_i], ropeb], [t2b])
                            tt("pool", dst, t1, t2, ALU.add, [t1b, t2b], [dstb])

            if stop == 'inproj':
                return
            transfer(h_all, alias_all)

            for kv in range(2):
                for dup in range(2):
                    S.dma("pool", cksrc[:, :, kv, dup, :], ck_d[l][:, kv * 64:(kv + 1) * 64].rearrange("(ct p) d -> p ct d", p=128),
                          writes=[cksrcb[kv * 2 + dup]])
                S.dma("pool", cvaug[:, :, kv, 0:64], cv_d[l][:, kv * 64:(kv + 1) * 64].rearrange("(ct p) d -> p ct d", p=128),
                      writes=[cvb[kv]])

            if stop == 'ctxdma':
                return
            thunks = []

            def lvl(eng, dst, src, sh_lo, sh_hi, lo, hi, rows, reads, writes):
                tt(eng, dst[rows, lo:hi], src[rows, lo + sh_lo:hi + sh_lo], src[rows, lo + sh_hi:hi + sh_hi], ALU.add, reads, writes)

            lo_r, hi_r = slice(0, 64), slice(64, 128)

            def pool_tile(j, t):
                u = uC[:, j, :]
                ub = uCb[j]
                tabc = V_PTAB + j * 17
                c0, c1 = tile_cols(t)
                d16, d16b = t16()
                if t == 0:
                    dst = d16.rearrange("p (s w) -> p s w", s=2)
                    sS = poolB[:, PC:PC + 2 * 272].rearrange("p (s w) -> p s w", s=2)[:, :, 0:256]
                    sU = uC[:, j, PC:PC + 2 * 272].rearrange("p (s w) -> p s w", s=2)[:, :, 0:256]
                else:
                    o = UC_BASE[2] + (t - 1) * TW
                    dst, sS, sU = d16, poolB[:, o:o + TW], uC[:, j, o:o + TW]
                stt("dve", dst, sS, vcol(tabc), sU, ALU.mult, ALU.subtract, [pBb, ub, vecb], [d16b])
                fixes = []
                if t == 0:
                    for s_ in range(2):
                        fixes.append((s_ * 256, UC_BASE[s_], tabc + 1))
                        fixes.append((s_ * 256 + 248, UC_BASE[s_] + 248, tabc + 9))
                elif t == 1:
                    fixes.append((0, UC_BASE[2], tabc + 1))
                else:
                    fixes.append((504, UC_BASE[2] + 1016, tabc + 9))
                for (dc, pc_, tc) in fixes:
                    ei = ring("ed", 4)
                    tt("dve", edg[:, ei, :], poolB[:, pc_:pc_ + 8], vec[:, tc:tc + 8], ALU.mult, [pBb, vecb], [edb[ei]])
                    tt("dve", d16[:, dc:dc + 8], edg[:, ei, :], uC[:, j, pc_:pc_ + 8], ALU.subtract, [edb[ei], ub, d16b], [d16b])
                ob = ring("sbr", 3)
                mm([(pbank[ob][:, :], plw[:, (l * 2 + j) * 128:(l * 2 + j + 1) * 128], d16, True, True)], [plwb, d16b], [psb[ob]])
                stt("dve", m[:, 6 + j, c0:c1], pbank[ob][:, :], vcol(vbase + V_PSCALE + j), m[:, 6 + j, c0:c1],
                    ALU.mult, ALU.mult, [psb[ob], vecb, mb[6 + j][t]], [mb[6 + j][t]])

            for j in range(2):
                u = uC[:, j, :]
                ub = uCb[j]
                L_ = lambda *a: thunks.append(lambda a=a: lvl("pool", *a))
                if j == 0:
                    L_(poolB, u, -1, 0, 1, WC, lo_r, [ub], [pBb])
                    L_(poolA, u, -1, 0, 1, WC, hi_r, [ub], [pAb])
                    L_(poolB, poolA, -1, 1, 2, WC - 1, hi_r, [pAb], [pBb])
                else:
                    L_(poolB, u, -1, 0, 1, WC, lo_r, [ub], [pBb])
                    L_(poolA, u, -1, 0, 1, WC, hi_r, [ub], [pAb])
                    L_(poolA, poolB, -1, 1, 2, WC - 1, lo_r, [pBb], [pAb])
                    L_(poolB, poolA, -1, 1, 2, WC - 1, hi_r, [pAb], [pBb])
                    L_(poolB, poolA, -2, 2, 4, WC - 3, lo_r, [pAb], [pBb])
                    L_(poolA, poolB, -2, 2, 4, WC - 3, hi_r, [pBb], [pAb])
                    L_(poolB, poolA, -4, 4, 8, WC - 7, hi_r, [pAb], [pBb])
                for t in range(3):
                    thunks.append(lambda j=j, t=t: pool_tile(j, t))

            work = []
            lev0, til0, lev1, til1 = thunks[0:3], thunks[3:6], thunks[6:13], thunks[13:16]
            thunks = []
            for f_ in lev0:
                f_()
            CB = (5, 6)

            conv_list = [(t, j, k) for t in range(3) for j in range(2) for k in range(31)]
            conv_dg = {}
            DG_AHEAD = 6

            def build_diag(n):
                if n >= len(conv_list):
                    return
                t, j, k = conv_list[n]
                di = ring("dg", NDG)
                ts("dve", diag[:, di, :], ident, vcol(vbase + V_DW + k * 2 + j), None, ALU.mult, None,
                   [cstb, vecb], [dgb[di]])
                conv_dg[n] = di

            def conv_item(n):
                t, j, k = conv_list[n]
                di = conv_dg.pop(n)
                bank, bb = pbank[CB[j]], psb[CB[j]]
                if t == 0:
                    spec = (bank[:, :].rearrange("p (s w) -> p s w", s=2), diag[:, di, :],
                            uA[:, j, k:k + 2 * 286].rearrange("p (s w) -> p s w", s=2)[:, :, 0:256], k == 0, k == 30)
                else:
                    o = UA_BASE[2] - PA + (t - 1) * TW + k
                    spec = (bank[:, :], diag[:, di, :], uA[:, j, o:o + TW], k == 0, k == 30)
                mm([spec], [dgb[di], uAb[j]], [bb])
                build_diag(n + DG_AHEAD)

            for n_ in range(DG_AHEAD):
                build_diag(n_)

            LN = {}

            def ln_A1(t):
                y32 = []
                for j in range(2):
                    bank, bb = pbank[CB[j]], psb[CB[j]]
                    y, yb = b32()
                    act(y, bank[:, :], AF.Identity, [bb, vecb], [yb], bias=vcol(vbase + V_CONVB + j))
                    y32.append((y, yb))
                LN[t] = {"y": y32}

            def ln_A2(t):
                aux = []
                for j in range(2):
                    y, yb = LN[t]["y"][j]
                    ysq, ysqb = t16()
                    act(ysq, y, AF.Square, [yb], [ysqb])
                    y16, y16b = t16()
                    act(y16, y, AF.Copy, [yb], [y16b])
                    aux.append((ysq, ysqb, y16, y16b))
                LN[t]["aux"] = aux

            def ln_A3(t):
                r6, r7 = ring("sbr", 3), ring("sbr", 3)
                for j in range(2):
                    ysq, ysqb, y16, y16b = LN[t]["aux"][j]
                    mm([(pbank[r6][:, :], cst[:, C_O256:C_O256 + 128], y16, j == 0, j == 1)], [cstb, y16b], [psb[r6]])
                    mm([(pbank[r7][:, :], cst[:, C_O256:C_O256 + 128], ysq, j == 0, j == 1)], [cstb, ysqb], [psb[r7]])
                mean, meanb = ms32()
                act(mean, pbank[r6][:, :], AF.Copy, [psb[r6]], [meanb])
                msq, msqb = ms32()
                act(msq, pbank[r6][:, :], AF.Square, [psb[r6]], [msqb])
                tt("dve", msq, pbank[r7][:, :], msq, ALU.subtract, [psb[r7], msqb], [msqb])
                act(msq, msq, AF.Ln, [msqb, epsb], [msqb], bias=epsc[:, 0:1])
                act(msq, msq, AF.Exp, [msqb], [msqb], scale=-0.5)
                for j in range(2):
                    y, yb = LN[t]["y"][j]
                    tt("dve", y, y, mean, ALU.subtract, [yb, meanb], [yb])
                    tt("dve", y, y, msq, ALU.mult, [yb, msqb], [yb])

            def ln_C(t):
                sj = []
                for j in range(2):
                    y, yb = LN[t]["y"][j]
                    si_ = ring("s16", 4)
                    act(s16[:, si_, :], y, AF.Silu, [yb, vecb], [s16b[si_]], bias=vcol(vbase + V_LNB + j), scale=vcol(vbase + V_LNG + j))
                    sj.append((s16[:, si_, :], s16b[si_]))
                LN[t]["s"] = sj

            def ln_D(t):
                c0, c1 = tile_cols(t)
                sj = LN[t]["s"]
                for jo in range(2):
                    ob = ring("sbr", 3)
                    mm([(pbank[ob][:, :], cpw[:, (l * 2 + ji) * 256 + jo * 128:(l * 2 + ji) * 256 + jo * 128 + 128], sj[ji][0], ji == 0, ji == 1)
                        for ji in range(2)], [cpwb, sj[0][1], sj[1][1]], [psb[ob]])
                    tt("dve", m[:, jo, c0:c1], pbank[ob][:, :], m[:, jo, c0:c1], ALU.mult, [psb[ob], mb[jo][t]], [mb[jo][t]])

            lnq = []
            for t in range(3):
                for j in range(2):
                    for k in range(31):
                        work.append(lambda n=(t * 2 + j) * 31 + k: conv_item(n))
                        if lnq and (j * 31 + k) % 3 == 2:
                            work.append(lnq.pop(0))
                lnq = [lambda t=t: ln_A1(t), lambda t=t: ln_A2(t), lambda t=t: ln_A3(t), lambda t=t: ln_C(t), lambda t=t: ln_D(t)]
                work.append(lnq.pop(0))
            work.extend(lnq)
            work[40:40] = til0 + lev1
            work.extend(til1)
            ln_thunks = []

            if stop == 'pool':
                return
            for kv in range(2):
                bi = 3 + kv
                mm([(pbank[bi][:, ct * 128:(ct + 1) * 128], cksrc[:, ct, kv, :, :].rearrange("p a d -> p (a d)"), ident, True, True) for ct in range(4)],
                   [cksrcb[kv * 2], cksrcb[kv * 2 + 1], cstb], [psb[bi]])
                cp("dve", ckT[:, kv, :], pbank[bi][:, :], [psb[bi]], [ckTb[kv]])

            rws = [slice(0, 64), slice(64, 128)]

            def unit_tiles(unit):
                tiles = []
                if unit[0] == "p":
                    _, s_, hd = unit
                    c, half, kvh = hd // 2, hd % 2, hd // 4
                    rows = rws[half]
                    tiles.append(dict(
                        smm=[(KT[rows, kvh, (s_ * 2 + kt) * 128:(s_ * 2 + kt + 1) * 128], QT[rows, c, s_ * 256:(s_ + 1) * 256], kt * 256, 256)
                             for kt in range(2)],
                        ncols=512, masks=[], srd=[kb_[kvh][0], qb_[c][0]],
                        pv=[(Vaug[:, s_ * 2 + kt, kvh, :], kt * 256, 256, 0, 0) for kt in range(2)], prd=[vb[0]]))
                    return tiles
                _, hd = unit
                c, half, kvh = hd // 2, hd % 2, hd // 4
                rows = rws[half]
                loc, ctx = [], []
                for j in range(8):
                    lo, hi = max(0, j - 1), min(8, j + 2)
                    n = (hi - lo) * 128
                    masks = []
                    for qbk in range(lo, hi):
                        if qbk == j - 1:
                            masks.append(((qbk - lo) * 128, C_MNEXT))
                        elif qbk == j + 1:
                            masks.append(((qbk - lo) * 128, C_MPREV))
                    pv = []
                    for b_ in range(2):
                        a0, a1 = max(lo * 128, b_ * 512), min(hi * 128, (b_ + 1) * 512)
                        if a1 > a0:
                            pv.append((Vaug[:, 4 + j, kvh, :], a0 - lo * 128, a1 - a0, b_, a0 - b_ * 512))
                    loc.append(dict(
                        smm=[(KT[rows, kvh, 512 + j * 128:512 + (j + 1) * 128], QT[rows, c, 512 + lo * 128:512 + hi * 128], 0, n)],
                        ncols=n, masks=masks, srd=[kb_[kvh][1 + j // 4], qb_[c][1], qb_[c][2]],
                        pv=pv, prd=[vb[1 + j // 4]]))
                for qh in range(2):
                    for ct in range(4):
                        ctx.append(dict(
                            smm=[(ckT[rows, kvh, ct * 128:(ct + 1) * 128], QT[rows, c, 512 + qh * 512:512 + (qh + 1) * 512], 0, 512)],
                            ncols=512, masks=[], srd=[ckTb[kvh], qb_[c][1 + qh]],
                            pv=[(cvaug[:, ct, kvh, :], 0, 512, qh, 0)], prd=[cvb[kvh]]))
                for i in range(8):
                    tiles.append(ctx[i])
                    tiles.append(loc[i])
                return tiles

            units = [("p", s_, hd) for s_ in range(2) for hd in range(8)] + [("s", hd) for hd in range(8)]
            flat = []
            for ui, unit in enumerate(units):
                tl = unit_tiles(unit)
                for ti, tile in enumerate(tl):
                    flat.append((ui, unit, tile, ti == len(tl) - 1))
            ostart = {}
            unit_first = {}
            for (ui_, unit_, tile_, last_) in flat:
                unit_first.setdefault(ui_, tile_)

            def emit_S(idx, tile):
                sbi = ring("sbr", 3)
                fill = []
                mm(fill + [(pbank[sbi][:, oc:oc + n], kt_, q_, True, True) for (kt_, q_, oc, n) in tile["smm"]],
                   tile["srd"] + [cstb, mb[0][0]], [psb[sbi]])
                pti = ring("pt", 4)
                tile["pti"] = pti
                nco = tile["ncols"]
                act(PT[:, pti, 0:nco], pbank[sbi][:, 0:nco], AF.Exp, [psb[sbi]], [PTb[pti]], scale=0.125)
                for (c0_, mk) in tile["masks"]:
                    tt("dve", PT[:, pti, c0_:c0_ + 128], PT[:, pti, c0_:c0_ + 128], cst[:, mk:mk + 128], ALU.mult,
                       [PTb[pti], cstb], [PTb[pti]])

            def norm_piece(Ob, Obb, n, hd, mcol0, qt):
                c, half = hd // 2, hd % 2
                ro = rws[half]
                rd, rdb_ = ms32()
                tmp, tmpb = ms32()
                act(rd[64:128, 0:n], Ob[64:128, 0:n], AF.Ln, [Obb, esinkb], [rdb_], bias=esink[64:128, l * 8 + hd:l * 8 + hd + 1])
                act(rd[64:128, 0:n], rd[64:128, 0:n], AF.Exp, [rdb_], [rdb_], scale=-1.0)
                tt("dve", tmp[ro, 0:n], Ob[0:64, 0:n], rd[64:128, 0:n], ALU.mult, [Obb, rdb_], [tmpb])
                tt("dve", m[ro, 2 + c, mcol0:mcol0 + n], tmp[ro, 0:n], m[ro, 2 + c, mcol0:mcol0 + n], ALU.mult,
                   [tmpb, mb[2 + c][qt]], [mb[2 + c][qt]])

            def emit_PV(idx, ui, unit, tile, last):
                pti = tile["pti"]
                oset = (3, 4) if (unit[0] == 's' or ui % 2 == 0) else (4, 3)
                for (va, pc0, n, ob_, oc0) in tile["pv"]:
                    bi = oset[ob_]
                    first = (ui, ob_) not in ostart
                    ostart[(ui, ob_)] = True
                    mm([(pbank[bi][:, oc0:oc0 + n], va, PT[:, pti, pc0:pc0 + n], first, True, True)],
                       [PTb[pti]] + tile["prd"], [psb[bi]])
                if last:
                    if unit[0] == "p":
                        _, s_, hd = unit
                        norm_piece(pbank[oset[0]], psb[oset[0]], 256, hd, s_ * 256, 0)
                    else:
                        _, hd = unit
                        for b_ in range(2):
                            norm_piece(pbank[oset[b_]], psb[oset[b_]], 512, hd, 512 + b_ * 512, 1 + b_)
                    if l + 1 < nl:
                        ada_slab(l + 1, ui)
                    if ln_thunks:
                        ln_thunks.pop(0)()

            pend = []
            wacc = [0.0]
            wrate = (len(work) + 4.0) / max(1, len(flat) - 8)
            for idx, (ui, unit, tile, last) in enumerate(flat):
                emit_S(idx, tile)
                wacc[0] += wrate
                while wacc[0] >= 1.0 and work:
                    work.pop(0)()
                    wacc[0] -= 1.0
                pend.append((idx, ui, unit, tile, last))
                if len(pend) > 2:
                    emit_PV(*pend.pop(0))
            while pend:
                emit_PV(*pend.pop(0))
            while work:
                work.pop(0)()

            if l + 1 < nl:
                ada_finish(l + 1)

            if stop == 'attn':
                return
            m_all = [b for row in mb for b in row]
            pend_stats = []
            for mo in range(8):
                sap, sbuf_ = get_slab()
                bidx = [(3 * mo + t) % 5 for t in range(3)]
                banks = [pbank[i] for i in bidx]
                bbs = [psb[i] for i in bidx]
                mm([(banks[t][:, :], sap[:, k, :], m[:, k, t * TW:(t + 1) * TW], k == 0, k == 7)
                    for k in range(8) for t in range(3)], [sbuf_] + m_all, bbs)
                for (t_, sq_, sqb_, mo_) in pend_stats:
                    stats_mm(t_, sq_, sqb_, mo_ == 0, mo_ == 7)
                pend_stats = []
                for t in range(3):
                    c0, c1 = tile_cols(t)
                    col = 0 if t == 0 else 1
                    stt("dve", x[:, mo, c0:c1], banks[t][:, :], modT[:, 16 + mo, col:col + 1], x[:, mo, c0:c1],
                        ALU.mult, ALU.add, [bbs[t], modb, xb[mo][t]], [xb[mo][t]])
                    sq_, sqb_ = stats_act(t, mo)
                    pend_stats.append((t, sq_, sqb_, mo))
            for (t_, sq_, sqb_, mo_) in pend_stats:
                stats_mm(t_, sq_, sqb_, mo_ == 0, mo_ == 7)

        for t in range(3):
            for k in range(8):
                stats_sq(t, k, k == 0, k == 7)
        for c in range(24):
            ada_slab(0, c, 0)
        ada_finish(0, 0)
        for l in range(nl):
            layer(l)

        for t in range(3):
            stats_fin(t)
        for t in range(3):
            c0, c1 = tile_cols(t)
            for k in range(8):
                yo, yob = t32()
                stt("dve", yo, x[:, k, c0:c1], vcol(V_FNW + k), Rt3[:, t, :], ALU.mult, ALU.mult, [xb[k][t], Rb3[t], vecb], [yob])
                S.dma("sp", yT_d[k * 128:(k + 1) * 128, c0:c1], yo, reads=[yob], is_output=True)
        S.finish()
    return nc


def _const_tables():
    cst = np.zeros((128, NCST), np.float32)
    cst[:, C_ID:C_ID + 128] = np.eye(128, dtype=np.float32)
    R = np.zeros((64, 64), np.float32)
    for d in range(16):
        R[d, d + 16] = -1.0
        R[d + 16, d] = 1.0
        R[32 + d, 48 + d] = -1.0
        R[48 + d, 32 + d] = 1.0
    Rt = np.zeros((128, 128), np.float32)
    Rt[0:64, 0:64] = R.T
    Rt[64:128, 64:128] = R.T
    cst[:, C_RT:C_RT + 128] = Rt
    cst[:, C_O1024:C_O1024 + 128] = 1.0 / 1024.0
    cst[:, C_O256:C_O256 + 128] = 1.0 / 256.0
    jk = np.arange(128)[:, None]
    r = np.arange(128)[None, :]
    cst[:, C_MNEXT:C_MNEXT + 128] = (jk <= r).astype(np.float32)
    cst[:, C_MPREV:C_MPREV + 128] = (jk >= r).astype(np.float32)
    quarter = 16
    freqs = 10000.0 ** (-np.arange(quarter, dtype=np.float64) / quarter)
    tpos = np.arange(1024)
    rows = (tpos // 64).astype(np.float64)
    cols = (tpos % 64).astype(np.float64)
    rope = np.zeros((128, 2048), np.float32)
    for p in range(128):
        d = p % 64
        pos = rows if d < 32 else cols
        ang = pos * freqs[d % 16]
        rope[p, 0:1024] = np.cos(ang).astype(np.float32)
        rope[p, 1024:2048] = np.sin(ang).astype(np.float32)
    ptab = np.zeros((128, 34), np.float32)
    for j in range(2):
        for p in range(128):
            w = POOL_WINDOWS[2 * j + p // 64]
            ptab[p, j * 17] = 1.0 / w
            for t in range(8):
                cnt = t + w // 2 if t < w // 2 else w
                ptab[p, j * 17 + 1 + t] = 1.0 / cnt
            for c in range(8):
                rr = 7 - c
                cnt = rr + 1 + w // 2 if w // 2 >= rr + 1 else w
                ptab[p, j * 17 + 9 + c] = 1.0 / cnt
    return cst, rope, ptab


def _colvec(v, n):
    return np.ascontiguousarray(np.asarray(v, np.float32).reshape(n, 128).T)


_PROG = {}


def kernel(x_prompt, x_sample, c, cache_k, cache_v, c_ctx, w_ada, b_ada, norm_w, w_in,
           conv_dw, conv_b, conv_ln_g, conv_ln_b, conv_pw, attn_sink, pool_w, pool_scale,
           w_out, final_norm_w, _nl=NL, _stop=None):
    f = lambda a: np.asarray(a, np.float32)
    x_prompt, x_sample, c, cache_k, cache_v, c_ctx = map(f, (x_prompt, x_sample, c, cache_k, cache_v, c_ctx))
    w_ada, b_ada, norm_w, w_in, conv_dw, conv_b = map(f, (w_ada, b_ada, norm_w, w_in, conv_dw, conv_b))
    conv_ln_g, conv_ln_b, conv_pw, attn_sink, pool_w, pool_scale = map(f, (conv_ln_g, conv_ln_b, conv_pw, attn_sink, pool_w, pool_scale))
    w_out, final_norm_w = f(w_out), f(final_norm_w)
    nl = _nl
    cst, rope, ptab = _const_tables()

    wslab = np.empty((NL * SLABS_PER_LAYER, 128, 1024), np.float32)

    def put(i, W, cols):
        wslab[i] = W[:, cols].reshape(8, 128, 128).transpose(1, 0, 2).reshape(128, 1024)

    i = 0
    for cc_ in range(24):
        put(i, w_ada[0], np.arange(cc_ * 128, cc_ * 128 + 128))
        i += 1
    for l in range(NL):
        for kind, idx in INPROJ:
            put(i, w_in[l], _slab_cols(kind, idx))
            i += 1
        if l + 1 < NL:
            for cc_ in range(24):
                put(i, w_ada[l + 1], np.arange(cc_ * 128, cc_ * 128 + 128))
                i += 1
        for mo in range(8):
            put(i, w_out[l], np.arange(mo * 128, mo * 128 + 128))
            i += 1

    vecs_base = np.zeros((128, NV), np.float32)
    for l in range(NL):
        b = l * LV
        vecs_base[:, b + V_NORMW:b + V_NORMW + 8] = _colvec(norm_w[l], 8)
        vecs_base[:, b + V_BADA:b + V_BADA + 24] = _colvec(b_ada[l], 24)
        vecs_base[:, b + V_CONVB:b + V_CONVB + 2] = _colvec(conv_b[l], 2)
        vecs_base[:, b + V_LNG:b + V_LNG + 2] = _colvec(conv_ln_g[l], 2)
        vecs_base[:, b + V_LNB:b + V_LNB + 2] = _colvec(conv_ln_b[l], 2)
        vecs_base[:, b + V_PSCALE:b + V_PSCALE + 2] = _colvec(pool_scale[l], 2)
        for k in range(31):
            vecs_base[:, b + V_DW + k * 2:b + V_DW + k * 2 + 2] = _colvec(conv_dw[l, k], 2)
    vecs_base[:, V_FNW:V_FNW + 8] = _colvec(final_norm_w, 8)
    vecs_base[:, V_SINK:V_SINK + 32] = np.broadcast_to(attn_sink.reshape(1, 32), (128, 32))
    vecs_base[:, V_PTAB:V_PTAB + 34] = ptab

    cpw = np.zeros((128, NL, 2, 256), np.float32)
    plw = np.zeros((128, NL, 2, 128), np.float32)
    for l in range(NL):
        for ji in range(2):
            cpw[:, l, ji, :] = conv_pw[l, ji * 128:(ji + 1) * 128, :]
        for g in range(4):
            j, hf = g // 2, g % 2
            plw[hf * 64:(hf + 1) * 64, l, j, hf * 64:(hf + 1) * 64] = pool_w[l, g]
    cpw = cpw.reshape(128, -1)
    plw = plw.reshape(128, -1)

    in_maps = []
    for i in range(NCORES):
        xt = np.concatenate([x_prompt[2 * i], x_prompt[2 * i + 1], x_sample[i]], axis=0)
        cc = np.stack([_colvec(c_ctx, 8), _colvec(c[i], 8)], axis=2).reshape(128, 16)
        in_maps.append({
            "xT": np.ascontiguousarray(xt.T),
            "cc": np.ascontiguousarray(cc),
            "vecs": vecs_base,
            "wslab": wslab,
            "cpw": cpw,
            "plw": plw,
            "ck": np.ascontiguousarray(cache_k[i].reshape(NL, 512, 128)),
            "cv": np.ascontiguousarray(cache_v[i].reshape(NL, 512, 128)),
            "rope": rope,
            "cst": cst,
        })
    if (nl, _stop) not in _PROG:
        _PROG[(nl, _stop)] = build_program(nl, _stop)
    res = run_bass_kernel_spmd(_PROG[(nl, _stop)], in_maps, core_ids=list(range(NCORES)))
    y_prompt = np.empty((16, 256, D), np.float32)
    y_sample = np.empty((8, 1024, D), np.float32)
    nk = np.empty((16, NL, 256, 2, 64), np.float32)
    nv = np.empty((16, NL, 256, 2, 64), np.float32)
    for i in range(NCORES):
        r = res.results[i]
        yt = np.asarray(r["yT"]).T
        y_prompt[2 * i] = yt[0:256]
        y_prompt[2 * i + 1] = yt[256:512]
        y_sample[i] = yt[512:1536]
        nk[2 * i:2 * i + 2] = np.asarray(r["nk"]).reshape(2, NL, 256, 2, 64)
        nv[2 * i:2 * i + 2] = np.asarray(r["nv"]).reshape(2, NL, 256, 2, 64)
    return (y_prompt, y_sample, nk, nv)
```

```python
import os
import numpy as np
import concourse.bass as bass
import concourse.mybir as mybir
from concourse.bass_utils import run_bass_kernel_spmd
from contextlib import ExitStack

F32 = mybir.dt.float32
BF16 = mybir.dt.bfloat16
AF = mybir.ActivationFunctionType
ALU = mybir.AluOpType

NCORES = 8
D = 1024
NL = 4
T = 1536
TW = 512
EPS = 1e-6
PA = 15
UA_BASE = [15, 301, 587]
WA = 587 + 1024 + 15
PC = 8
UC_BASE = [8, 280, 552]
WC = 552 + 1024 + 8
POOL_WINDOWS = (2, 4, 8, 16)

LV = 102
V_NORMW, V_BADA, V_CONVB, V_LNG, V_LNB, V_PSCALE, V_DW = 0, 8, 32, 34, 36, 38, 40
V_FNW = NL * LV
V_SINK = V_FNW + 8
V_PTAB = V_SINK + 32
NV = V_PTAB + 34
C_ID, C_RT, C_O1024, C_O256, C_MNEXT, C_MPREV = 0, 128, 256, 384, 512, 640
NCST = 768

INPROJ = ([("glu", 0), ("val", 0), ("glu", 1), ("val", 1), ("agate", 0), ("agate", 1)]
          + [("q", c) for c in range(4)] + [("kd", 0), ("kd", 1), ("ktok", 0), ("vtok", 0)]
          + [("bgate", c) for c in range(4)] + [("cin", 0), ("cin", 1), ("cgate", 0), ("cgate", 1)])
SLABS_PER_LAYER = 24 + len(INPROJ) + 8
NSL = 6
NDUMMY = int(os.environ.get('NDUMMY', '0'))
NBURST = int(os.environ.get('NBURST', '20'))
BURST_AT = [int(v) for v in os.environ.get('BURST_AT', '0').split(',') if v != '']
NDG = 8


def _slab_cols(kind, idx):
    if kind == "glu":
        return np.arange(256 + idx * 128, 256 + idx * 128 + 128)
    if kind == "val":
        return np.arange(idx * 128, idx * 128 + 128)
    if kind == "agate":
        return np.arange(512 + idx * 128, 512 + idx * 128 + 128)
    if kind == "q":
        return np.arange(768 + idx * 128, 768 + idx * 128 + 128)
    if kind == "kd":
        a = np.arange(1280 + idx * 64, 1280 + idx * 64 + 64)
        return np.concatenate([a, a])
    if kind == "ktok":
        return np.arange(1280, 1408)
    if kind == "vtok":
        return np.arange(1408, 1536)
    if kind == "bgate":
        return np.arange(1536 + idx * 128, 1536 + idx * 128 + 128)
    if kind == "cin":
        return np.arange(2048 + idx * 128, 2048 + idx * 128 + 128)
    if kind == "cgate":
        return np.arange(2304 + idx * 128, 2304 + idx * 128 + 128)
    raise ValueError(kind)


class Buf:
    __slots__ = ("name", "w", "r", "dsem", "dcnt")

    def __init__(self, name):
        self.name = name
        self.w = None
        self.r = {}
        self.dsem = None
        self.dcnt = 0


class Sched:
    def __init__(self, nc, es):
        self.nc = nc
        self.es = es
        self.eng = {"pe": nc.tensor, "act": nc.scalar, "dve": nc.vector, "pool": nc.gpsimd, "sp": nc.sync}
        self.sem = {e: es.enter_context(nc.semaphore("s_" + e)) for e in self.eng}
        self.cnt = {e: 0 for e in self.eng}
        self.seen = {e: {} for e in self.eng}
        self.out_events = {}

    def _deps(self, e, reads, writes, skip_own):
        deps = {}
        own = self.sem[e]

        def add(ev):
            if ev is None:
                return
            s, v = ev
            if deps.get(s, 0) < v:
                deps[s] = v

        for b in reads:
            add(b.w)
        for b in writes:
            if b.w is not None and not (skip_own and b.w[0] is own):
                add(b.w)
            for s, v in b.r.items():
                if not (skip_own and s is own):
                    add((s, v))
        return deps

    def _wait(self, e, deps):
        for s, v in deps.items():
            if self.seen[e].get(s, 0) >= v:
                continue
            self.eng[e].wait_ge(s, v)
            self.seen[e][s] = v

    @staticmethod
    def _record(ev, reads, writes):
        for b in reads:
            if b.r.get(ev[0], 0) < ev[1]:
                b.r[ev[0]] = ev[1]
        for b in writes:
            b.w = ev
            b.r = {}

    def op(self, e, fns, reads=(), writes=()):
        if callable(fns):
            fns = [fns]
        self._wait(e, self._deps(e, reads, writes, e == "pe"))
        ins = None
        for f in fns:
            ins = f(self.eng[e])
        ins.then_inc(self.sem[e], 1)
        self.cnt[e] += 1
        ev = (self.sem[e], self.cnt[e])
        self._record(ev, reads, writes)
        return ev

    def dma(self, q, out, in_, reads=(), writes=(), is_output=False):
        self._wait(q, self._deps(q, reads, writes, False))
        b = (list(writes) or list(reads))[0]
        if b.dsem is None:
            b.dsem = self.es.enter_context(self.nc.semaphore("d_" + b.name))
        b.dcnt += 16
        self.eng[q].dma_start(out=out, in_=in_).then_inc(b.dsem, 16)
        ev = (b.dsem, b.dcnt)
        self._record(ev, reads, writes)
        if is_output:
            self.out_events[b.dsem] = max(self.out_events.get(b.dsem, 0), b.dcnt)
        return ev

    def finish(self):
        for s, v in self.out_events.items():
            self.eng["sp"].wait_ge(s, v)


def transfer(src, dst):
    evs = {}
    for b in src:
        if b.w is not None:
            evs[b.w[0]] = max(evs.get(b.w[0], 0), b.w[1])
        for s, v in b.r.items():
            evs[s] = max(evs.get(s, 0), v)
    for b in dst:
        for s, v in evs.items():
            if b.r.get(s, 0) < v:
                b.r[s] = v


def build_program(nl=NL, stop=None):
    nc = bass.Bass("TRN2", target_bir_lowering=False)

    def dram(n, s, kind="ExternalInput"):
        return nc.dram_tensor(n, s, F32, kind=kind).ap()

    xT_d = dram("xT", [D, T])
    cc_d = dram("cc", [128, 16])
    vecs_d = dram("vecs", [128, NV])
    wslab_d = dram("wslab", [NL * SLABS_PER_LAYER, 128, 1024])
    cpw_d = dram("cpw", [128, NL * 2 * 256])
    plw_d = dram("plw", [128, NL * 2 * 128])
    ck_d = dram("ck", [NL, 512, 128])
    cv_d = dram("cv", [NL, 512, 128])
    rope_d = dram("rope", [128, 2048])
    cst_d = dram("cst", [128, NCST])
    yT_d = dram("yT", [D, T], "ExternalOutput")
    nk_d = dram("nk", [2, NL, 256, 128], "ExternalOutput")
    nv_d = dram("nv", [2, NL, 256, 128], "ExternalOutput")

    with ExitStack() as es:
        S = Sched(nc, es)

        def sb(n, s, d):
            return es.enter_context(nc.sbuf_tensor("sb_" + n, s, d))

        x = sb("x", [128, 8, T], F32)
        xb = [[Buf("x%d_%d" % (k, t)) for t in range(3)] for k in range(8)]
        hraw = sb("hraw", [128, 8 * T], BF16)
        hb = [[Buf("h%d_%d" % (k, t)) for t in range(3)] for k in range(8)]
        h_all = [b for row in hb for b in row]

        def hap(k, c0, c1):
            return hraw[:, k * T + c0:k * T + c1]

        m = sb("m", [128, 8, T], BF16)
        mb = [[Buf("m%d_%d" % (k, t)) for t in range(3)] for k in range(8)]
        uA = sb("uA", [128, 2, WA], BF16)
        uAb = [Buf("uA0"), Buf("uA1")]
        QT = sb("QT", [128, 4, T], BF16)
        qb_ = [[Buf("q%d_%d" % (c, t)) for t in range(3)] for c in range(4)]
        KT = sb("KT", [128, 2, T], BF16)
        kb_ = [[Buf("k%d_%d" % (c, t)) for t in range(3)] for c in range(2)]
        Vaug = sb("Vaug", [128, 12, 2, 128], BF16)
        vb = [Buf("v%d" % g) for g in range(3)]
        uC = sb("uC", [128, 2, WC], F32)
        uCb = [Buf("uC0"), Buf("uC1")]
        slabs = sb("slabs", [128, NSL, 8, 128], BF16)
        slb = [Buf("slab%d" % i) for i in range(NSL)]
        diag = sb("diag", [128, NDG, 128], BF16)
        dgb = [Buf("dg%d" % i) for i in range(NDG)]
        rope = sb("rope", [128, 2048], F32)
        ropeb = Buf("rope")
        cst = sb("cst", [128, NCST], BF16)
        cstb = Buf("cst")
        vec = sb("vec", [128, NV], F32)
        vecb = Buf("vec")
        ccs = sb("ccs", [128, 16], F32)
        ccb = Buf("cc")
        csil = sb("csil", [128, 8, 2], BF16)
        csilb = Buf("csil")
        modT2 = sb("modT", [128, 2, 24, 2], F32)
        modb2 = [Buf("mod0"), Buf("mod1")]
        gmod2 = sb("gmod", [128, 2, 8, 2], F32)
        gmodb2 = [Buf("gmod0"), Buf("gmod1")]
        esink = sb("esink", [128, 32], F32)
        esinkb = Buf("esink")
        cpw = sb("cpw", [128, NL * 2 * 256], BF16)
        cpwb = Buf("cpw")
        plw = sb("plw", [128, NL * 2 * 128], BF16)
        plwb = Buf("plw")
        cksrc = sb("cksrc", [128, 4, 2, 2, 64], BF16)
        cksrcb = [Buf("cksrc%d" % i) for i in range(4)]
        ckT = sb("ckT", [128, 2, 512], BF16)
        ckTb = [Buf("ckT0"), Buf("ckT1")]
        cvaug = sb("cvaug", [128, 4, 2, 128], BF16)
        cvb = [Buf("cvaug0"), Buf("cvaug1")]
        PT = sb("PT", [128, 4, 512], BF16)
        PTb = [Buf("pt%d" % i) for i in range(4)]
        Rt3 = sb("Rt3", [128, 3, 512], F32)
        Rb3 = [Buf("R0"), Buf("R1"), Buf("R2")]
        g32 = sb("g32", [128, 4, 512], F32)
        g32b = [Buf("g32_%d" % i) for i in range(4)]
        g16 = sb("g16", [128, 4, 512], BF16)
        g16b = [Buf("g16_%d" % i) for i in range(4)]
        edg = sb("edg", [128, 4, 8], F32)
        edb = [Buf("ed%d" % i) for i in range(4)]
        poolA = hraw[:, 0:2 * WC].bitcast(F32)
        poolB = hraw[:, 2 * WC:4 * WC].bitcast(F32)
        pAb, pBb = Buf("poolA"), Buf("poolB")
        NA32 = 5
        a32 = [hraw[:, 4 * WC + i * 1024:4 * WC + (i + 1) * 1024].bitcast(F32) for i in range(NA32)]
        a32b = [Buf("a32_%d" % i) for i in range(NA32)]
        s16 = sb("s16", [128, 4, 512], BF16)
        s16b = [Buf("s16_%d" % i) for i in range(4)]
        alias_all = [pAb, pBb] + a32b

        pbank = [es.enter_context(nc.psum_tensor("pb%d" % i, [128, 512], F32)) for i in range(8)]
        psb = [Buf("psb%d" % i) for i in range(8)]

        st = {"g32": 0, "g16": 0, "br32": 0, "rd": 0, "ed": 0, "dg": 0, "pt": 0, "sbr": 0, "s16": 0, "ms32": 0, "sq8": 0}

        def t32():
            i = st["g32"] % 4
            st["g32"] += 1
            return g32[:, i, :], g32b[i]

        def b32():
            i = st["br32"] % (NA32 + 1)
            st["br32"] += 1
            if i < NA32:
                return a32[i], a32b[i]
            return g32[:, 3, :], g32b[3]

        def ms32():
            i = st["ms32"] % 3
            st["ms32"] += 1
            return g32[:, i, :], g32b[i]

        def t16():
            i = st["g16"] % 4
            st["g16"] += 1
            return g16[:, i, :], g16b[i]

        def ring(name, n):
            i = st[name] % n
            st[name] += 1
            return i

        def act(out, in_, func, reads, writes, bias=None, scale=None):
            kw = {}
            if bias is not None:
                kw["bias"] = bias
            if scale is not None:
                kw["scale"] = scale
            return S.op("act", lambda e: e.activation(out=out, in_=in_, func=func, **kw), reads, writes)

        def tt(eng, out, in0, in1, op, reads, writes):
            return S.op(eng, lambda e: e.tensor_tensor(out=out, in0=in0, in1=in1, op=op), reads, writes)

        def ts(eng, out, in0, s1, s2, op0, op1, reads, writes):
            if s2 is None:
                return S.op(eng, lambda e: e.tensor_scalar(out=out, in0=in0, scalar1=s1, scalar2=None, op0=op0), reads, writes)
            return S.op(eng, lambda e: e.tensor_scalar(out=out, in0=in0, scalar1=s1, scalar2=s2, op0=op0, op1=op1), reads, writes)

        def stt(eng, out, in0, scalar, in1, op0, op1, reads, writes):
            return S.op(eng, lambda e: e.scalar_tensor_tensor(out=out, in0=in0, scalar=scalar, in1=in1, op0=op0, op1=op1), reads, writes)

        def cp(eng, out, in_, reads, writes):
            return S.op(eng, lambda e: e.tensor_copy(out=out, in_=in_), reads, writes)

        def mm(specs, reads, writes):
            fns = []
            for sp_ in specs:
                o, l, r, a, z = sp_[:5]
                if len(sp_) > 5 and sp_[5]:
                    fns.append(lambda e, o=o, l=l, r=r, a=a, z=z: e.matmul(o, l, r, start=a, stop=z, skip_group_check=True))
                else:
                    fns.append(lambda e, o=o, l=l, r=r, a=a, z=z: e.matmul(o, l, r, start=a, stop=z))
            return S.op("pe", fns, reads, writes)

        def vcol(c, rows=None):
            if rows is None:
                return vec[:, c:c + 1]
            return vec[rows, c:c + 1]

        ident = cst[:, C_ID:C_ID + 128]
        epsc = sb("epsc", [128, 1], F32)
        epsb = Buf("eps")
        S.op("dve", lambda e: e.memset(epsc[:, :], EPS), writes=[epsb])

        sl = {"issued": 0, "next": 0}
        total_slabs = 24 + nl * (SLABS_PER_LAYER - 24) + (nl - 1) * 24

        def slab_issue(i):
            s = i % NSL
            S.dma("pool", slabs[:, s, :, :], wslab_d[i].rearrange("p (k c) -> p k c", k=8), reads=(), writes=[slb[s]])

        def get_slab():
            i = sl["next"]
            sl["next"] += 1
            while sl["issued"] < min(total_slabs, i + NSL):
                slab_issue(sl["issued"])
                sl["issued"] += 1
            s = i % NSL
            return slabs[:, s, :, :], slb[s]

        S.dma("sp", vec[:, :], vecs_d[:, :], writes=[vecb])
        S.dma("sp", ccs[:, :], cc_d[:, :], writes=[ccb])
        S.dma("pool", cst[:, :], cst_d[:, :], writes=[cstb])
        for k in range(8):
            S.dma("sp", x[:, k, :], xT_d[k * 128:(k + 1) * 128, :], writes=xb[k])
        S.dma("sp", rope[:, :], rope_d[:, :], writes=[ropeb])
        S.dma("pool", cpw[:, :], cpw_d[:, :], writes=[cpwb])
        S.dma("pool", plw[:, :], plw_d[:, :], writes=[plwb])
        S.op("dve", lambda e: e.memset(uA[:, :, :], 0.0), writes=uAb)
        S.op("pool", lambda e: e.memset(uC[:, :, :], 0.0), writes=uCb)
        S.op("dve", lambda e: e.memset(Vaug[:, :, :, 64:128], 1.0), writes=vb)
        S.op("dve", lambda e: e.memset(cvaug[:, :, :, 64:128], 1.0), writes=cvb)
        act(csil[:, :, :], ccs[:, :].rearrange("p (k t) -> p k t", t=2), AF.Silu, [ccb], [csilb])
        act(esink[:, :], vec[:, V_SINK:V_SINK + 32], AF.Exp, [vecb], [esinkb])

        def tile_cols(t):
            return t * TW, (t + 1) * TW

        def sq_tile():
            i = st["sq8"] % 8
            st["sq8"] += 1
            if i < 4:
                return g16[:, i, :], g16b[i]
            return s16[:, i - 4, :], s16b[i - 4]

        def stats_act(t, k):
            c0, c1 = tile_cols(t)
            sq, sqb = sq_tile()
            act(sq, x[:, k, c0:c1], AF.Square, [xb[k][t]], [sqb])
            return sq, sqb

        def stats_mm(t, sq, sqb, first, last):
            mm([(pbank[5 + t][:, :], cst[:, C_O1024:C_O1024 + 128], sq, first, last)], [cstb, sqb], [psb[5 + t]])

        def stats_sq(t, k, first, last):
            sq, sqb = stats_act(t, k)
            stats_mm(t, sq, sqb, first, last)

        def stats_fin(t):
            act(Rt3[:, t, :], pbank[5 + t][:, :], AF.Sqrt, [psb[5 + t]], [Rb3[t]], bias=EPS)
            act(Rt3[:, t, :], Rt3[:, t, :], AF.Ln, [Rb3[t]], [Rb3[t]])
            act(Rt3[:, t, :], Rt3[:, t, :], AF.Exp, [Rb3[t]], [Rb3[t]], scale=-1.0)

        def ada_slab(l, c, ab=7):
            sap, sbuf_ = get_slab()
            mm([(pbank[ab][:, 2 * c:2 * c + 2], sap[:, k, :], csil[:, k, :], k == 0, k == 7) for k in range(8)],
               [sbuf_, csilb], [psb[ab]])

        def ada_finish(l, ab=7):
            vbase = l * LV
            modT = modT2[:, l % 2]
            modb = modb2[l % 2]
            gmod = gmod2[:, l % 2]
            gmodb = gmodb2[l % 2]
            tt("dve", modT, pbank[ab][:, 0:48].rearrange("p (c t) -> p c t", t=2),
               vec[:, vbase + V_BADA:vbase + V_BADA + 24].unsqueeze(2).to_broadcast([128, 24, 2]), ALU.add,
               [psb[ab], vecb], [modb])
            ts("dve", gmod, modT[:, 8:16, :], 1.0, None, ALU.add, None, [modb], [gmodb])
            tt("dve", gmod, gmod,
               vec[:, vbase + V_NORMW:vbase + V_NORMW + 8].unsqueeze(2).to_broadcast([128, 8, 2]), ALU.mult,
               [gmodb, vecb], [gmodb])

        def layer(l):
            vbase = l * LV
            modT = modT2[:, l % 2]
            modb = modb2[l % 2]
            gmod = gmod2[:, l % 2]
            gmodb = gmodb2[l % 2]
            if stop == 'ada':
                return
            transfer(alias_all, h_all)
            for t in range(3):
                stats_fin(t)
            cnt_n = 0
            for k in range(8):
                for t in range(3):
                    c0, c1 = tile_cols(t)
                    col = 0 if t == 0 else 1
                    xr, xrb = t32()
                    eng_ = "pool" if (k + t) % 3 == 2 else "dve"
                    cnt_n += 1
                    tt(eng_, xr, x[:, k, c0:c1], Rt3[:, t, :], ALU.mult, [xb[k][t], Rb3[t]], [xrb])
                    if k % 2 == 0:
                        act(hap(k, c0, c1), xr, AF.Identity, [xrb, gmodb, modb], [hb[k][t]],
                            bias=modT[:, k, col:col + 1], scale=gmod[:, k, col:col + 1])
                    else:
                        ts("dve", hap(k, c0, c1), xr, gmod[:, k, col:col + 1], modT[:, k, col:col + 1], ALU.mult, ALU.add,
                           [xrb, gmodb, modb], [hb[k][t]])

            if stop == 'norm':
                return
            for si, (kind, idx) in enumerate(INPROJ):
                if stop is not None and stop.startswith('inproj:') and si >= int(stop.split(':')[1]):
                    return
                sap, sbuf_ = get_slab()
                base = 0 if si % 2 == 0 else 3
                if kind == "ktok":
                    bank, bb = pbank[base], psb[base]
                    mm([(bank[:, tb * 128:(tb + 1) * 128], hap(k, tb * 128, (tb + 1) * 128), sap[:, k, :], k == 0, k == 7)
                        for tb in range(4) for k in range(8)],
                       [sbuf_] + [hb[k][0] for k in range(8)], [bb])
                    ko, kob = t32()
                    cp("dve", ko, bank[:, :], [bb], [kob])
                    for s_ in range(2):
                        S.dma("sp", nk_d[s_, l].rearrange("(r p) c -> p r c", p=128),
                              ko[:, s_ * 256:(s_ + 1) * 256].rearrange("p (r c) -> p r c", r=2), reads=[kob], is_output=True)
                    continue
                if kind == "vtok":
                    for g in range(3):
                        bank, bb = pbank[base + g], psb[base + g]
                        mm([(bank[:, tb * 128:(tb + 1) * 128], hap(k, (g * 4 + tb) * 128, (g * 4 + tb + 1) * 128), sap[:, k, :], k == 0, k == 7)
                            for tb in range(4) for k in range(8)],
                           [sbuf_] + [hb[k][g] for k in range(8)], [bb])
                        if g == 0:
                            vo, vob = t32()
                            cp("dve", vo, bank[:, :], [bb], [vob])
                            for s_ in range(2):
                                S.dma("sp", nv_d[s_, l].rearrange("(r p) c -> p r c", p=128),
                                      vo[:, s_ * 256:(s_ + 1) * 256].rearrange("p (r c) -> p r c", r=2), reads=[vob], is_output=True)
                        for kv in range(2):
                            if os.environ.get("VT_SKIP") == "act":
                                continue
                            cp("dve", Vaug[:, g * 4:(g + 1) * 4, kv, 0:64],
                               bank[:, :].rearrange("p (b v d) -> p b v d", b=4, v=2)[:, :, kv, :], [bb], [vb[g]])
                    continue
                banks = [pbank[base + t] for t in range(3)]
                bbs = [psb[base + t] for t in range(3)]
                if si == 0:
                    for k in range(8):
                        mm([(banks[t][:, :], sap[:, k, :], hap(k, t * TW, (t + 1) * TW), k == 0, k == 7) for t in range(3)],
                           [sbuf_] + hb[k], bbs)
                else:
                    mm([(banks[t][:, :], sap[:, k, :], hap(k, t * TW, (t + 1) * TW), k == 0, k == 7)
                        for k in range(8) for t in range(3)],
                       [sbuf_] + h_all, bbs)
                for t in range(3):
                    c0, c1 = tile_cols(t)
                    ps = banks[t][:, :]
                    pb_ = bbs[t]
                    if kind in ("glu", "val"):
                        if t == 0:
                            dst = uA[:, idx, PA:PA + 2 * 286].rearrange("p (s w) -> p s w", s=2)[:, :, 0:256]
                            src = ps.rearrange("p (s w) -> p s w", s=2)
                        else:
                            dst = uA[:, idx, UA_BASE[2] + (t - 1) * TW:UA_BASE[2] + t * TW]
                            src = ps
                        if kind == "glu":
                            act(dst, src, AF.Sigmoid, [pb_], [uAb[idx]])
                        else:
                            tt("dve", dst, src, dst, ALU.mult, [pb_, uAb[idx]], [uAb[idx]])
                    elif kind == "agate":
                        act(m[:, idx, c0:c1], ps, AF.Silu, [pb_], [mb[idx][t]])
                    elif kind == "bgate":
                        act(m[:, 2 + idx, c0:c1], ps, AF.Silu, [pb_], [mb[2 + idx][t]])
                    elif kind == "cgate":
                        act(m[:, 6 + idx, c0:c1], ps, AF.Silu, [pb_], [mb[6 + idx][t]])
                    elif kind == "cin":
                        if t == 0:
                            dst = uC[:, idx, PC:PC + 2 * 272].rearrange("p (s w) -> p s w", s=2)[:, :, 0:256]
                            src = ps.rearrange("p (s w) -> p s w", s=2)
                        else:
                            dst = uC[:, idx, UC_BASE[2] + (t - 1) * TW:UC_BASE[2] + t * TW]
                            src = ps
                        cp("dve", dst, src, [pb_], [uCb[idx]])
                    elif kind in ("q", "kd"):
                        if kind == "q":
                            dst, dstb = QT[:, idx, c0:c1], qb_[idx][t]
                        else:
                            dst, dstb = KT[:, idx, c0:c1], kb_[idx][t]
                        if t == 0:
                            cp("dve", dst, ps, [pb_], [dstb])
                        else:
                            r0 = (t - 1) * TW
                            qraw, qrb = t16()
                            act(qraw, ps, AF.Copy, [pb_], [qrb])
                            t1, t1b = t32()
                            tt("dve", t1, ps, rope[:, r0:r0 + TW], ALU.mult, [pb_, ropeb], [t1b])
                            rb_i = 6 + (t % 2)
                            mm([(pbank[rb_i][:, :], cst[:, C_RT:C_RT + 128], qraw, True, True)], [cstb, qrb], [psb[rb_i]])
                            t2, t2b = t32()
                            tt("dve", t2, pbank[rb_i][:, :], rope[:, 1024 + r0:1024 + r0 + TW], ALU.mult, [psb[rb_i], ropeb], [t2b])
                            tt("pool", dst, t1, t2, ALU.add, [t1b, t2b], [dstb])

            if stop == 'inproj':
                return
            transfer(h_all, alias_all)

            for kv in range(2):
                for dup in range(2):
                    S.dma("pool", cksrc[:, :, kv, dup, :], ck_d[l][:, kv * 64:(kv + 1) * 64].rearrange("(ct p) d -> p ct d", p=128),
                          writes=[cksrcb[kv * 2 + dup]])
                S.dma("pool", cvaug[:, :, kv, 0:64], cv_d[l][:, kv * 64:(kv + 1) * 64].rearrange("(ct p) d -> p ct d", p=128),
                      writes=[cvb[kv]])

            if stop == 'ctxdma':
                return
            thunks = []

            def lvl(eng, dst, src, sh_lo, sh_hi, lo, hi, rows, reads, writes):
                tt(eng, dst[rows, lo:hi], src[rows, lo + sh_lo:hi + sh_lo], src[rows, lo + sh_hi:hi + sh_hi], ALU.add, reads, writes)

            lo_r, hi_r = slice(0, 64), slice(64, 128)

            def pool_tile(j, t):
                u = uC[:, j, :]
                ub = uCb[j]
                tabc = V_PTAB + j * 17
                c0, c1 = tile_cols(t)
                d16, d16b = t16()
                if t == 0:
                    dst = d16.rearrange("p (s w) -> p s w", s=2)
                    sS = poolB[:, PC:PC + 2 * 272].rearrange("p (s w) -> p s w", s=2)[:, :, 0:256]
                    sU = uC[:, j, PC:PC + 2 * 272].rearrange("p (s w) -> p s w", s=2)[:, :, 0:256]
                else:
                    o = UC_BASE[2] + (t - 1) * TW
                    dst, sS, sU = d16, poolB[:, o:o + TW], uC[:, j, o:o + TW]
                stt("dve", dst, sS, vcol(tabc), sU, ALU.mult, ALU.subtract, [pBb, ub, vecb], [d16b])
                fixes = []
                if t == 0:
                    for s_ in range(2):
                        fixes.append((s_ * 256, UC_BASE[s_], tabc + 1))
                        fixes.append((s_ * 256 + 248, UC_BASE[s_] + 248, tabc + 9))
                elif t == 1:
                    fixes.append((0, UC_BASE[2], tabc + 1))
                else:
                    fixes.append((504, UC_BASE[2] + 1016, tabc + 9))
                for (dc, pc_, tc) in fixes:
                    ei = ring("ed", 4)
                    tt("dve", edg[:, ei, :], poolB[:, pc_:pc_ + 8], vec[:, tc:tc + 8], ALU.mult, [pBb, vecb], [edb[ei]])
                    tt("dve", d16[:, dc:dc + 8], edg[:, ei, :], uC[:, j, pc_:pc_ + 8], ALU.subtract, [edb[ei], ub, d16b], [d16b])
                ob = ring("sbr", 3)
                mm([(pbank[ob][:, :], plw[:, (l * 2 + j) * 128:(l * 2 + j + 1) * 128], d16, True, True)], [plwb, d16b], [psb[ob]])
                stt("dve", m[:, 6 + j, c0:c1], pbank[ob][:, :], vcol(vbase + V_PSCALE + j), m[:, 6 + j, c0:c1],
                    ALU.mult, ALU.mult, [psb[ob], vecb, mb[6 + j][t]], [mb[6 + j][t]])

            for j in range(2):
                u = uC[:, j, :]
                ub = uCb[j]
                L_ = lambda *a: thunks.append(lambda a=a: lvl("pool", *a))
                if j == 0:
                    L_(poolB, u, -1, 0, 1, WC, lo_r, [ub], [pBb])
                    L_(poolA, u, -1, 0, 1, WC, hi_r, [ub], [pAb])
                    L_(poolB, poolA, -1, 1, 2, WC - 1, hi_r, [pAb], [pBb])
                else:
                    L_(poolB, u, -1, 0, 1, WC, lo_r, [ub], [pBb])
                    L_(poolA, u, -1, 0, 1, WC, hi_r, [ub], [pAb])
                    L_(poolA, poolB, -1, 1, 2, WC - 1, lo_r, [pBb], [pAb])
                    L_(poolB, poolA, -1, 1, 2, WC - 1, hi_r, [pAb], [pBb])
                    L_(poolB, poolA, -2, 2, 4, WC - 3, lo_r, [pAb], [pBb])
                    L_(poolA, poolB, -2, 2, 4, WC - 3, hi_r, [pBb], [pAb])
                    L_(poolB, poolA, -4, 4, 8, WC - 7, hi_r, [pAb], [pBb])
                for t in range(3):
                    thunks.append(lambda j=j, t=t: pool_tile(j, t))

            work = []
            lev0, til0, lev1, til1 = thunks[0:3], thunks[3:6], thunks[6:13], thunks[13:16]
            thunks = []
            for f_ in lev0:
                f_()
            CB = (5, 6)

            conv_list = [(t, j, k) for t in range(3) for j in range(2) for k in range(31)]
            conv_dg = {}
            DG_AHEAD = 6

            def build_diag(n):
                if n >= len(conv_list):
                    return
                t, j, k = conv_list[n]
                di = ring("dg", NDG)
                ts("dve", diag[:, di, :], ident, vcol(vbase + V_DW + k * 2 + j), None, ALU.mult, None,
                   [cstb, vecb], [dgb[di]])
                conv_dg[n] = di

            def conv_item(n):
                t, j, k = conv_list[n]
                di = conv_dg.pop(n)
                bank, bb = pbank[CB[j]], psb[CB[j]]
                if t == 0:
                    spec = (bank[:, :].rearrange("p (s w) -> p s w", s=2), diag[:, di, :],
                            uA[:, j, k:k + 2 * 286].rearrange("p (s w) -> p s w", s=2)[:, :, 0:256], k == 0, k == 30)
                else:
                    o = UA_BASE[2] - PA + (t - 1) * TW + k
                    spec = (bank[:, :], diag[:, di, :], uA[:, j, o:o + TW], k == 0, k == 30)
                mm([spec], [dgb[di], uAb[j]], [bb])
                build_diag(n + DG_AHEAD)

            for n_ in range(DG_AHEAD):
                build_diag(n_)

            LN = {}

            def ln_A1(t):
                y32 = []
                for j in range(2):
                    bank, bb = pbank[CB[j]], psb[CB[j]]
                    y, yb = b32()
                    act(y, bank[:, :], AF.Identity, [bb, vecb], [yb], bias=vcol(vbase + V_CONVB + j))
                    y32.append((y, yb))
                LN[t] = {"y": y32}

            def ln_A2(t):
                aux = []
                for j in range(2):
                    y, yb = LN[t]["y"][j]
                    ysq, ysqb = t16()
                    act(ysq, y, AF.Square, [yb], [ysqb])
                    y16, y16b = t16()
                    act(y16, y, AF.Copy, [yb], [y16b])
                    aux.append((ysq, ysqb, y16, y16b))
                LN[t]["aux"] = aux

            def ln_A3(t):
                r6, r7 = ring("sbr", 3), ring("sbr", 3)
                for j in range(2):
                    ysq, ysqb, y16, y16b = LN[t]["aux"][j]
                    mm([(pbank[r6][:, :], cst[:, C_O256:C_O256 + 128], y16, j == 0, j == 1)], [cstb, y16b], [psb[r6]])
                    mm([(pbank[r7][:, :], cst[:, C_O256:C_O256 + 128], ysq, j == 0, j == 1)], [cstb, ysqb], [psb[r7]])
                mean, meanb = ms32()
                act(mean, pbank[r6][:, :], AF.Copy, [psb[r6]], [meanb])
                msq, msqb = ms32()
                act(msq, pbank[r6][:, :], AF.Square, [psb[r6]], [msqb])
                tt("dve", msq, pbank[r7][:, :], msq, ALU.subtract, [psb[r7], msqb], [msqb])
                act(msq, msq, AF.Ln, [msqb, epsb], [msqb], bias=epsc[:, 0:1])
                act(msq, msq, AF.Exp, [msqb], [msqb], scale=-0.5)
                for j in range(2):
                    y, yb = LN[t]["y"][j]
                    tt("dve", y, y, mean, ALU.subtract, [yb, meanb], [yb])
                    tt("dve", y, y, msq, ALU.mult, [yb, msqb], [yb])

            def ln_C(t):
                sj = []
                for j in range(2):
                    y, yb = LN[t]["y"][j]
                    si_ = ring("s16", 4)
                    act(s16[:, si_, :], y, AF.Silu, [yb, vecb], [s16b[si_]], bias=vcol(vbase + V_LNB + j), scale=vcol(vbase + V_LNG + j))
                    sj.append((s16[:, si_, :], s16b[si_]))
                LN[t]["s"] = sj

            def ln_D(t):
                c0, c1 = tile_cols(t)
                sj = LN[t]["s"]
                for jo in range(2):
                    ob = ring("sbr", 3)
                    mm([(pbank[ob][:, :], cpw[:, (l * 2 + ji) * 256 + jo * 128:(l * 2 + ji) * 256 + jo * 128 + 128], sj[ji][0], ji == 0, ji == 1)
                        for ji in range(2)], [cpwb, sj[0][1], sj[1][1]], [psb[ob]])
                    tt("dve", m[:, jo, c0:c1], pbank[ob][:, :], m[:, jo, c0:c1], ALU.mult, [psb[ob], mb[jo][t]], [mb[jo][t]])

            lnq = []
            for t in range(3):
                for j in range(2):
                    for k in range(31):
                        work.append(lambda n=(t * 2 + j) * 31 + k: conv_item(n))
                        if lnq and (j * 31 + k) % 3 == 2:
                            work.append(lnq.pop(0))
                lnq = [lambda t=t: ln_A1(t), lambda t=t: ln_A2(t), lambda t=t: ln_A3(t), lambda t=t: ln_C(t), lambda t=t: ln_D(t)]
                work.append(lnq.pop(0))
            work.extend(lnq)
            work[40:40] = til0 + lev1
            work.extend(til1)
            ln_thunks = []

            if stop == 'pool':
                return
            for kv in range(2):
                bi = 3 + kv
                mm([(pbank[bi][:, ct * 128:(ct + 1) * 128], cksrc[:, ct, kv, :, :].rearrange("p a d -> p (a d)"), ident, True, True) for ct in range(4)],
                   [cksrcb[kv * 2], cksrcb[kv * 2 + 1], cstb], [psb[bi]])
                cp("dve", ckT[:, kv, :], pbank[bi][:, :], [psb[bi]], [ckTb[kv]])

            rws = [slice(0, 64), slice(64, 128)]

            def unit_tiles(unit):
                tiles = []
                if unit[0] == "p":
                    _, s_, hd = unit
                    c, half, kvh = hd // 2, hd % 2, hd // 4
                    rows = rws[half]
                    tiles.append(dict(
                        smm=[(KT[rows, kvh, (s_ * 2 + kt) * 128:(s_ * 2 + kt + 1) * 128], QT[rows, c, s_ * 256:(s_ + 1) * 256], kt * 256, 256)
                             for kt in range(2)],
                        ncols=512, masks=[], srd=[kb_[kvh][0], qb_[c][0]],
                        pv=[(Vaug[:, s_ * 2 + kt, kvh, :], kt * 256, 256, 0, 0) for kt in range(2)], prd=[vb[0]]))
                    return tiles
                _, hd = unit
                c, half, kvh = hd // 2, hd % 2, hd // 4
                rows = rws[half]
                loc, ctx = [], []
                for j in range(8):
                    lo, hi = max(0, j - 1), min(8, j + 2)
                    n = (hi - lo) * 128
                    masks = []
                    for qbk in range(lo, hi):
                        if qbk == j - 1:
                            masks.append(((qbk - lo) * 128, C_MNEXT))
                        elif qbk == j + 1:
                            masks.append(((qbk - lo) * 128, C_MPREV))
                    pv = []
                    for b_ in range(2):
                        a0, a1 = max(lo * 128, b_ * 512), min(hi * 128, (b_ + 1) * 512)
                        if a1 > a0:
                            pv.append((Vaug[:, 4 + j, kvh, :], a0 - lo * 128, a1 - a0, b_, a0 - b_ * 512))
                    loc.append(dict(
                        smm=[(KT[rows, kvh, 512 + j * 128:512 + (j + 1) * 128], QT[rows, c, 512 + lo * 128:512 + hi * 128], 0, n)],
                        ncols=n, masks=masks, srd=[kb_[kvh][1 + j // 4], qb_[c][1], qb_[c][2]],
                        pv=pv, prd=[vb[1 + j // 4]]))
                for qh in range(2):
                    for ct in range(4):
                        ctx.append(dict(
                            smm=[(ckT[rows, kvh, ct * 128:(ct + 1) * 128], QT[rows, c, 512 + qh * 512:512 + (qh + 1) * 512], 0, 512)],
                            ncols=512, masks=[], srd=[ckTb[kvh], qb_[c][1 + qh]],
                            pv=[(cvaug[:, ct, kvh, :], 0, 512, qh, 0)], prd=[cvb[kvh]]))
                for i in range(8):
                    tiles.append(ctx[i])
                    tiles.append(loc[i])
                return tiles

            units = [("p", s_, hd) for s_ in range(2) for hd in range(8)] + [("s", hd) for hd in range(8)]
            flat = []
            for ui, unit in enumerate(units):
                tl = unit_tiles(unit)
                for ti, tile in enumerate(tl):
                    flat.append((ui, unit, tile, ti == len(tl) - 1))
            ostart = {}
            unit_first = {}
            for (ui_, unit_, tile_, last_) in flat:
                unit_first.setdefault(ui_, tile_)

            def emit_S(idx, tile):
                sbi = ring("sbr", 3)
                fill = []
                mm(fill + [(pbank[sbi][:, oc:oc + n], kt_, q_, True, True) for (kt_, q_, oc, n) in tile["smm"]],
                   tile["srd"] + [cstb, mb[0][0]], [psb[sbi]])
                pti = ring("pt", 4)
                tile["pti"] = pti
                nco = tile["ncols"]
                act(PT[:, pti, 0:nco], pbank[sbi][:, 0:nco], AF.Exp, [psb[sbi]], [PTb[pti]], scale=0.125)
                for (c0_, mk) in tile["masks"]:
                    tt("dve", PT[:, pti, c0_:c0_ + 128], PT[:, pti, c0_:c0_ + 128], cst[:, mk:mk + 128], ALU.mult,
                       [PTb[pti], cstb], [PTb[pti]])

            def norm_piece(Ob, Obb, n, hd, mcol0, qt):
                c, half = hd // 2, hd % 2
                ro = rws[half]
                rd, rdb_ = ms32()
                tmp, tmpb = ms32()
                act(rd[64:128, 0:n], Ob[64:128, 0:n], AF.Ln, [Obb, esinkb], [rdb_], bias=esink[64:128, l * 8 + hd:l * 8 + hd + 1])
                act(rd[64:128, 0:n], rd[64:128, 0:n], AF.Exp, [rdb_], [rdb_], scale=-1.0)
                tt("dve", tmp[ro, 0:n], Ob[0:64, 0:n], rd[64:128, 0:n], ALU.mult, [Obb, rdb_], [tmpb])
                tt("dve", m[ro, 2 + c, mcol0:mcol0 + n], tmp[ro, 0:n], m[ro, 2 + c, mcol0:mcol0 + n], ALU.mult,
                   [tmpb, mb[2 + c][qt]], [mb[2 + c][qt]])

            def emit_PV(idx, ui, unit, tile, last):
                pti = tile["pti"]
                oset = (3, 4) if (unit[0] == 's' or ui % 2 == 0) else (4, 3)
                for (va, pc0, n, ob_, oc0) in tile["pv"]:
                    bi = oset[ob_]
                    first = (ui, ob_) not in ostart
                    ostart[(ui, ob_)] = True
                    mm([(pbank[bi][:, oc0:oc0 + n], va, PT[:, pti, pc0:pc0 + n], first, True, True)],
                       [PTb[pti]] + tile["prd"], [psb[bi]])
                if last:
                    if unit[0] == "p":
                        _, s_, hd = unit
                        norm_piece(pbank[oset[0]], psb[oset[0]], 256, hd, s_ * 256, 0)
                    else:
                        _, hd = unit
                        for b_ in range(2):
                            norm_piece(pbank[oset[b_]], psb[oset[b_]], 512, hd, 512 + b_ * 512, 1 + b_)
                    if l + 1 < nl:
                        ada_slab(l + 1, ui)
                    if ln_thunks:
                        ln_thunks.pop(0)()

            pend = []
            wacc = [0.0]
            wrate = (len(work) + 4.0) / max(1, len(flat) - 8)
            for idx, (ui, unit, tile, last) in enumerate(flat):
                emit_S(idx, tile)
                wacc[0] += wrate
                while wacc[0] >= 1.0 and work:
                    work.pop(0)()
                    wacc[0] -= 1.0
                pend.append((idx, ui, unit, tile, last))
                if len(pend) > 2:
                    emit_PV(*pend.pop(0))
            while pend:
                emit_PV(*pend.pop(0))
            while work:
                work.pop(0)()

            if l + 1 < nl:
                ada_finish(l + 1)

            if stop == 'attn':
                return
            m_all = [b for row in mb for b in row]
            pend_stats = []
            for mo in range(8):
                sap, sbuf_ = get_slab()
                bidx = [(3 * mo + t) % 5 for t in range(3)]
                banks = [pbank[i] for i in bidx]
                bbs = [psb[i] for i in bidx]
                if mo == 0:
                    korder = [2, 3, 4, 5, 0, 1, 6, 7]
                    for ki, k in enumerate(korder):
                        mm([(banks[t][:, :], sap[:, k, :], m[:, k, t * TW:(t + 1) * TW], ki == 0, ki == 7) for t in range(3)],
                           [sbuf_] + mb[k], bbs)
                else:
                    mm([(banks[t][:, :], sap[:, k, :], m[:, k, t * TW:(t + 1) * TW], k == 0, k == 7)
                        for k in range(8) for t in range(3)], [sbuf_] + m_all, bbs)
                for (t_, sq_, sqb_, mo_) in pend_stats:
                    stats_mm(t_, sq_, sqb_, mo_ == 0, mo_ == 7)
                pend_stats = []
                for t in range(3):
                    c0, c1 = tile_cols(t)
                    col = 0 if t == 0 else 1
                    stt("dve", x[:, mo, c0:c1], banks[t][:, :], modT[:, 16 + mo, col:col + 1], x[:, mo, c0:c1],
                        ALU.mult, ALU.add, [bbs[t], modb, xb[mo][t]], [xb[mo][t]])
                    sq_, sqb_ = stats_act(t, mo)
                    pend_stats.append((t, sq_, sqb_, mo))
            for (t_, sq_, sqb_, mo_) in pend_stats:
                stats_mm(t_, sq_, sqb_, mo_ == 0, mo_ == 7)

        for t in range(3):
            for k in range(8):
                stats_sq(t, k, k == 0, k == 7)
        for c in range(24):
            ada_slab(0, c, 0)
        ada_finish(0, 0)
        for l in range(nl):
            layer(l)

        for t in range(3):
            stats_fin(t)
        for t in range(3):
            c0, c1 = tile_cols(t)
            for k in range(8):
                yo, yob = t32()
                stt("dve", yo, x[:, k, c0:c1], vcol(V_FNW + k), Rt3[:, t, :], ALU.mult, ALU.mult, [xb[k][t], Rb3[t], vecb], [yob])
                S.dma("sp", yT_d[k * 128:(k + 1) * 128, c0:c1], yo, reads=[yob], is_output=True)
        S.finish()
    return nc


def _const_tables():
    cst = np.zeros((128, NCST), np.float32)
    cst[:, C_ID:C_ID + 128] = np.eye(128, dtype=np.float32)
    R = np.zeros((64, 64), np.float32)
    for d in range(16):
        R[d, d + 16] = -1.0
        R[d + 16, d] = 1.0
        R[32 + d, 48 + d] = -1.0
        R[48 + d, 32 + d] = 1.0
    Rt = np.zeros((128, 128), np.float32)
    Rt[0:64, 0:64] = R.T
    Rt[64:128, 64:128] = R.T
    cst[:, C_RT:C_RT + 128] = Rt
    cst[:, C_O1024:C_O1024 + 128] = 1.0 / 1024.0
    cst[:, C_O256:C_O256 + 128] = 1.0 / 256.0
    jk = np.arange(128)[:, None]
    r = np.arange(128)[None, :]
    cst[:, C_MNEXT:C_MNEXT + 128] = (jk <= r).astype(np.float32)
    cst[:, C_MPREV:C_MPREV + 128] = (jk >= r).astype(np.float32)
    quarter = 16
    freqs = 10000.0 ** (-np.arange(quarter, dtype=np.float64) / quarter)
    tpos = np.arange(1024)
    rows = (tpos // 64).astype(np.float64)
    cols = (tpos % 64).astype(np.float64)
    rope = np.zeros((128, 2048), np.float32)
    for p in range(128):
        d = p % 64
        pos = rows if d < 32 else cols
        ang = pos * freqs[d % 16]
        rope[p, 0:1024] = np.cos(ang).astype(np.float32)
        rope[p, 1024:2048] = np.sin(ang).astype(np.float32)
    ptab = np.zeros((128, 34), np.float32)
    for j in range(2):
        for p in range(128):
            w = POOL_WINDOWS[2 * j + p // 64]
            ptab[p, j * 17] = 1.0 / w
            for t in range(8):
                cnt = t + w // 2 if t < w // 2 else w
                ptab[p, j * 17 + 1 + t] = 1.0 / cnt
            for c in range(8):
                rr = 7 - c
                cnt = rr + 1 + w // 2 if w // 2 >= rr + 1 else w
                ptab[p, j * 17 + 9 + c] = 1.0 / cnt
    return cst, rope, ptab


def _colvec(v, n):
    return np.ascontiguousarray(np.asarray(v, np.float32).reshape(n, 128).T)


_PROG = {}


def kernel(x_prompt, x_sample, c, cache_k, cache_v, c_ctx, w_ada, b_ada, norm_w, w_in,
           conv_dw, conv_b, conv_ln_g, conv_ln_b, conv_pw, attn_sink, pool_w, pool_scale,
           w_out, final_norm_w, _nl=NL, _stop=None):
    f = lambda a: np.asarray(a, np.float32)
    x_prompt, x_sample, c, cache_k, cache_v, c_ctx = map(f, (x_prompt, x_sample, c, cache_k, cache_v, c_ctx))
    w_ada, b_ada, norm_w, w_in, conv_dw, conv_b = map(f, (w_ada, b_ada, norm_w, w_in, conv_dw, conv_b))
    conv_ln_g, conv_ln_b, conv_pw, attn_sink, pool_w, pool_scale = map(f, (conv_ln_g, conv_ln_b, conv_pw, attn_sink, pool_w, pool_scale))
    w_out, final_norm_w = f(w_out), f(final_norm_w)
    nl = _nl
    cst, rope, ptab = _const_tables()

    wslab = np.empty((NL * SLABS_PER_LAYER, 128, 1024), np.float32)

    def put(i, W, cols):
        wslab[i] = W[:, cols].reshape(8, 128, 128).transpose(1, 0, 2).reshape(128, 1024)

    i = 0
    for cc_ in range(24):
        put(i, w_ada[0], np.arange(cc_ * 128, cc_ * 128 + 128))
        i += 1
    for l in range(NL):
        for kind, idx in INPROJ:
            put(i, w_in[l], _slab_cols(kind, idx))
            i += 1
        if l + 1 < NL:
            for cc_ in range(24):
                put(i, w_ada[l + 1], np.arange(cc_ * 128, cc_ * 128 + 128))
                i += 1
        for mo in range(8):
            put(i, w_out[l], np.arange(mo * 128, mo * 128 + 128))
            i += 1

    vecs_base = np.zeros((128, NV), np.float32)
    for l in range(NL):
        b = l * LV
        vecs_base[:, b + V_NORMW:b + V_NORMW + 8] = _colvec(norm_w[l], 8)
        vecs_base[:, b + V_BADA:b + V_BADA + 24] = _colvec(b_ada[l], 24)
        vecs_base[:, b + V_CONVB:b + V_CONVB + 2] = _colvec(conv_b[l], 2)
        vecs_base[:, b + V_LNG:b + V_LNG + 2] = _colvec(conv_ln_g[l], 2)
        vecs_base[:, b + V_LNB:b + V_LNB + 2] = _colvec(conv_ln_b[l], 2)
        vecs_base[:, b + V_PSCALE:b + V_PSCALE + 2] = _colvec(pool_scale[l], 2)
        for k in range(31):
            vecs_base[:, b + V_DW + k * 2:b + V_DW + k * 2 + 2] = _colvec(conv_dw[l, k], 2)
    vecs_base[:, V_FNW:V_FNW + 8] = _colvec(final_norm_w, 8)
    vecs_base[:, V_SINK:V_SINK + 32] = np.broadcast_to(attn_sink.reshape(1, 32), (128, 32))
    vecs_base[:, V_PTAB:V_PTAB + 34] = ptab

    cpw = np.zeros((128, NL, 2, 256), np.float32)
    plw = np.zeros((128, NL, 2, 128), np.float32)
    for l in range(NL):
        for ji in range(2):
            cpw[:, l, ji, :] = conv_pw[l, ji * 128:(ji + 1) * 128, :]
        for g in range(4):
            j, hf = g // 2, g % 2
            plw[hf * 64:(hf + 1) * 64, l, j, hf * 64:(hf + 1) * 64] = pool_w[l, g]
    cpw = cpw.reshape(128, -1)
    plw = plw.reshape(128, -1)

    in_maps = []
    for i in range(NCORES):
        xt = np.concatenate([x_prompt[2 * i], x_prompt[2 * i + 1], x_sample[i]], axis=0)
        cc = np.stack([_colvec(c_ctx, 8), _colvec(c[i], 8)], axis=2).reshape(128, 16)
        in_maps.append({
            "xT": np.ascontiguousarray(xt.T),
            "cc": np.ascontiguousarray(cc),
            "vecs": vecs_base,
            "wslab": wslab,
            "cpw": cpw,
            "plw": plw,
            "ck": np.ascontiguousarray(cache_k[i].reshape(NL, 512, 128)),
            "cv": np.ascontiguousarray(cache_v[i].reshape(NL, 512, 128)),
            "rope": rope,
            "cst": cst,
        })
    if (nl, _stop) not in _PROG:
        _PROG[(nl, _stop)] = build_program(nl, _stop)
    res = run_bass_kernel_spmd(_PROG[(nl, _stop)], in_maps, core_ids=list(range(NCORES)))
    y_prompt = np.empty((16, 256, D), np.float32)
    y_sample = np.empty((8, 1024, D), np.float32)
    nk = np.empty((16, NL, 256, 2, 64), np.float32)
    nv = np.empty((16, NL, 256, 2, 64), np.float32)
    for i in range(NCORES):
        r = res.results[i]
        yt = np.asarray(r["yT"]).T
        y_prompt[2 * i] = yt[0:256]
        y_prompt[2 * i + 1] = yt[256:512]
        y_sample[i] = yt[512:1536]
        nk[2 * i:2 * i + 2] = np.asarray(r["nk"]).reshape(2, NL, 256, 2, 64)
        nv[2 * i:2 * i + 2] = np.asarray(r["nv"]).reshape(2, NL, 256, 2, 64)
    return (y_prompt, y_sample, nk, nv)
```

```python
import os
import numpy as np
import concourse.bass as bass
import concourse.mybir as mybir
from concourse.bass_utils import run_bass_kernel_spmd
from contextlib import ExitStack

F32 = mybir.dt.float32
BF16 = mybir.dt.bfloat16
AF = mybir.ActivationFunctionType
ALU = mybir.AluOpType

NCORES = 8
D = 1024
NL = 4
T = 1536
TW = 512
EPS = 1e-6
PA = 15
UA_BASE = [15, 301, 587]
WA = 587 + 1024 + 15
PC = 8
UC_BASE = [8, 280, 552]
WC = 552 + 1024 + 8
POOL_WINDOWS = (2, 4, 8, 16)

LV = 102
V_NORMW, V_BADA, V_CONVB, V_LNG, V_LNB, V_PSCALE, V_DW = 0, 8, 32, 34, 36, 38, 40
V_FNW = NL * LV
V_SINK = V_FNW + 8
V_PTAB = V_SINK + 32
NV = V_PTAB + 34
C_ID, C_RT, C_O1024, C_O256, C_MNEXT, C_MPREV = 0, 128, 256, 384, 512, 640
NCST = 768

INPROJ = ([("glu", 0), ("val", 0), ("glu", 1), ("val", 1), ("agate", 0), ("agate", 1)]
          + [("q", c) for c in range(4)] + [("kd", 0), ("kd", 1), ("ktok", 0), ("vtok", 0)]
          + [("bgate", c) for c in range(4)] + [("cin", 0), ("cin", 1), ("cgate", 0), ("cgate", 1)])
SLABS_PER_LAYER = 24 + len(INPROJ) + 8
NSL = 6
NDUMMY = int(os.environ.get('NDUMMY', '0'))
NBURST = int(os.environ.get('NBURST', '20'))
BURST_AT = [int(v) for v in os.environ.get('BURST_AT', '0').split(',') if v != '']
NDG = 8


def _slab_cols(kind, idx):
    if kind == "glu":
        return np.arange(256 + idx * 128, 256 + idx * 128 + 128)
    if kind == "val":
        return np.arange(idx * 128, idx * 128 + 128)
    if kind == "agate":
        return np.arange(512 + idx * 128, 512 + idx * 128 + 128)
    if kind == "q":
        return np.arange(768 + idx * 128, 768 + idx * 128 + 128)
    if kind == "kd":
        a = np.arange(1280 + idx * 64, 1280 + idx * 64 + 64)
        return np.concatenate([a, a])
    if kind == "ktok":
        return np.arange(1280, 1408)
    if kind == "vtok":
        return np.arange(1408, 1536)
    if kind == "bgate":
        return np.arange(1536 + idx * 128, 1536 + idx * 128 + 128)
    if kind == "cin":
        return np.arange(2048 + idx * 128, 2048 + idx * 128 + 128)
    if kind == "cgate":
        return np.arange(2304 + idx * 128, 2304 + idx * 128 + 128)
    raise ValueError(kind)


class Buf:
    __slots__ = ("name", "w", "r", "dsem", "dcnt")

    def __init__(self, name):
        self.name = name
        self.w = None
        self.r = {}
        self.dsem = None
        self.dcnt = 0


class Sched:
    def __init__(self, nc, es):
        self.nc = nc
        self.es = es
        self.eng = {"pe": nc.tensor, "act": nc.scalar, "dve": nc.vector, "pool": nc.gpsimd, "sp": nc.sync}
        self.sem = {e: es.enter_context(nc.semaphore("s_" + e)) for e in self.eng}
        self.cnt = {e: 0 for e in self.eng}
        self.seen = {e: {} for e in self.eng}
        self.out_events = {}

    def _deps(self, e, reads, writes, skip_own):
        deps = {}
        own = self.sem[e]

        def add(ev):
            if ev is None:
                return
            s, v = ev
            if deps.get(s, 0) < v:
                deps[s] = v

        for b in reads:
            add(b.w)
        for b in writes:
            if b.w is not None and not (skip_own and b.w[0] is own):
                add(b.w)
            for s, v in b.r.items():
                if not (skip_own and s is own):
                    add((s, v))
        return deps

    def _wait(self, e, deps):
        for s, v in deps.items():
            if self.seen[e].get(s, 0) >= v:
                continue
            self.eng[e].wait_ge(s, v)
            self.seen[e][s] = v

    @staticmethod
    def _record(ev, reads, writes):
        for b in reads:
            if b.r.get(ev[0], 0) < ev[1]:
                b.r[ev[0]] = ev[1]
        for b in writes:
            b.w = ev
            b.r = {}

    def op(self, e, fns, reads=(), writes=()):
        if callable(fns):
            fns = [fns]
        self._wait(e, self._deps(e, reads, writes, e == "pe"))
        ins = None
        for f in fns:
            ins = f(self.eng[e])
        ins.then_inc(self.sem[e], 1)
        self.cnt[e] += 1
        ev = (self.sem[e], self.cnt[e])
        self._record(ev, reads, writes)
        return ev

    def dma(self, q, out, in_, reads=(), writes=(), is_output=False):
        self._wait(q, self._deps(q, reads, writes, False))
        b = (list(writes) or list(reads))[0]
        if b.dsem is None:
            b.dsem = self.es.enter_context(self.nc.semaphore("d_" + b.name))
        b.dcnt += 16
        self.eng[q].dma_start(out=out, in_=in_).then_inc(b.dsem, 16)
        ev = (b.dsem, b.dcnt)
        self._record(ev, reads, writes)
        if is_output:
            self.out_events[b.dsem] = max(self.out_events.get(b.dsem, 0), b.dcnt)
        return ev

    def finish(self):
        for s, v in self.out_events.items():
            self.eng["sp"].wait_ge(s, v)


def transfer(src, dst):
    evs = {}
    for b in src:
        if b.w is not None:
            evs[b.w[0]] = max(evs.get(b.w[0], 0), b.w[1])
        for s, v in b.r.items():
            evs[s] = max(evs.get(s, 0), v)
    for b in dst:
        for s, v in evs.items():
            if b.r.get(s, 0) < v:
                b.r[s] = v


def build_program(nl=NL, stop=None):
    nc = bass.Bass("TRN2", target_bir_lowering=False)

    def dram(n, s, kind="ExternalInput"):
        return nc.dram_tensor(n, s, F32, kind=kind).ap()

    xT_d = dram("xT", [D, T])
    cc_d = dram("cc", [128, 16])
    vecs_d = dram("vecs", [128, NV])
    wslab_d = dram("wslab", [NL * SLABS_PER_LAYER, 128, 1024])
    cpw_d = dram("cpw", [128, NL * 2 * 256])
    plw_d = dram("plw", [128, NL * 2 * 128])
    ck_d = dram("ck", [NL, 512, 128])
    cv_d = dram("cv", [NL, 512, 128])
    rope_d = dram("rope", [128, 2048])
    cst_d = dram("cst", [128, NCST])
    yT_d = dram("yT", [D, T], "ExternalOutput")
    nk_d = dram("nk", [2, NL, 256, 128], "ExternalOutput")
    nv_d = dram("nv", [2, NL, 256, 128], "ExternalOutput")

    with ExitStack() as es:
        S = Sched(nc, es)

        def sb(n, s, d):
            return es.enter_context(nc.sbuf_tensor("sb_" + n, s, d))

        x = sb("x", [128, 8, T], F32)
        xb = [[Buf("x%d_%d" % (k, t)) for t in range(3)] for k in range(8)]
        hraw = sb("hraw", [128, 8 * T], BF16)
        hb = [[Buf("h%d_%d" % (k, t)) for t in range(3)] for k in range(8)]
        h_all = [b for row in hb for b in row]

        def hap(k, c0, c1):
            return hraw[:, k * T + c0:k * T + c1]

        m = sb("m", [128, 8, T], BF16)
        mb = [[Buf("m%d_%d" % (k, t)) for t in range(3)] for k in range(8)]
        uA = sb("uA", [128, 2, WA], BF16)
        uAb = [Buf("uA0"), Buf("uA1")]
        QT = sb("QT", [128, 4, T], BF16)
        qb_ = [[Buf("q%d_%d" % (c, t)) for t in range(3)] for c in range(4)]
        KT = sb("KT", [128, 2, T], BF16)
        kb_ = [[Buf("k%d_%d" % (c, t)) for t in range(3)] for c in range(2)]
        Vaug = sb("Vaug", [128, 12, 2, 128], BF16)
        vb = [Buf("v%d" % g) for g in range(3)]
        uC = sb("uC", [128, 2, WC], F32)
        uCb = [Buf("uC0"), Buf("uC1")]
        slabs = sb("slabs", [128, NSL, 8, 128], BF16)
        slb = [Buf("slab%d" % i) for i in range(NSL)]
        diag = sb("diag", [128, NDG, 128], BF16)
        dgb = [Buf("dg%d" % i) for i in range(NDG)]
        rope = sb("rope", [128, 2048], F32)
        ropeb = Buf("rope")
        cst = sb("cst", [128, NCST], BF16)
        cstb = Buf("cst")
        vec = sb("vec", [128, NV], F32)
        vecb = Buf("vec")
        ccs = sb("ccs", [128, 16], F32)
        ccb = Buf("cc")
        csil = sb("csil", [128, 8, 2], BF16)
        csilb = Buf("csil")
        modT2 = sb("modT", [128, 2, 24, 2], F32)
        modb2 = [Buf("mod0"), Buf("mod1")]
        gmod2 = sb("gmod", [128, 2, 8, 2], F32)
        gmodb2 = [Buf("gmod0"), Buf("gmod1")]
        esink = sb("esink", [128, 32], F32)
        esinkb = Buf("esink")
        cpw = sb("cpw", [128, NL * 2 * 256], BF16)
        cpwb = Buf("cpw")
        plw = sb("plw", [128, NL * 2 * 128], BF16)
        plwb = Buf("plw")
        cksrc = sb("cksrc", [128, 4, 2, 2, 64], BF16)
        cksrcb = [Buf("cksrc%d" % i) for i in range(4)]
        ckT = sb("ckT", [128, 2, 512], BF16)
        ckTb = [Buf("ckT0"), Buf("ckT1")]
        cvaug = sb("cvaug", [128, 4, 2, 128], BF16)
        cvb = [Buf("cvaug0"), Buf("cvaug1")]
        PT = sb("PT", [128, 4, 512], BF16)
        PTb = [Buf("pt%d" % i) for i in range(4)]
        Rt3 = sb("Rt3", [128, 3, 512], F32)
        Rb3 = [Buf("R0"), Buf("R1"), Buf("R2")]
        g32 = sb("g32", [128, 4, 512], F32)
        g32b = [Buf("g32_%d" % i) for i in range(4)]
        g16 = sb("g16", [128, 4, 512], BF16)
        g16b = [Buf("g16_%d" % i) for i in range(4)]
        edg = sb("edg", [128, 4, 8], F32)
        edb = [Buf("ed%d" % i) for i in range(4)]
        poolA = hraw[:, 0:2 * WC].bitcast(F32)
        poolB = hraw[:, 2 * WC:4 * WC].bitcast(F32)
        pAb, pBb = Buf("poolA"), Buf("poolB")
        NA32 = 5
        a32 = [hraw[:, 4 * WC + i * 1024:4 * WC + (i + 1) * 1024].bitcast(F32) for i in range(NA32)]
        a32b = [Buf("a32_%d" % i) for i in range(NA32)]
        s16 = sb("s16", [128, 4, 512], BF16)
        s16b = [Buf("s16_%d" % i) for i in range(4)]
        alias_all = [pAb, pBb] + a32b

        pbank = [es.enter_context(nc.psum_tensor("pb%d" % i, [128, 512], F32)) for i in range(8)]
        psb = [Buf("psb%d" % i) for i in range(8)]

        st = {"g32": 0, "g16": 0, "br32": 0, "rd": 0, "ed": 0, "dg": 0, "pt": 0, "sbr": 0, "s16": 0, "ms32": 0, "sq8": 0}

        def t32():
            i = st["g32"] % 4
            st["g32"] += 1
            return g32[:, i, :], g32b[i]

        def b32():
            i = st["br32"] % (NA32 + 1)
            st["br32"] += 1
            if i < NA32:
                return a32[i], a32b[i]
            return g32[:, 3, :], g32b[3]

        def ms32():
            i = st["ms32"] % 3
            st["ms32"] += 1
            return g32[:, i, :], g32b[i]

        def t16():
            i = st["g16"] % 4
            st["g16"] += 1
            return g16[:, i, :], g16b[i]

        def ring(name, n):
            i = st[name] % n
            st[name] += 1
            return i

        def act(out, in_, func, reads, writes, bias=None, scale=None):
            kw = {}
            if bias is not None:
                kw["bias"] = bias
            if scale is not None:
                kw["scale"] = scale
            return S.op("act", lambda e: e.activation(out=out, in_=in_, func=func, **kw), reads, writes)

        def tt(eng, out, in0, in1, op, reads, writes):
            return S.op(eng, lambda e: e.tensor_tensor(out=out, in0=in0, in1=in1, op=op), reads, writes)

        def ts(eng, out, in0, s1, s2, op0, op1, reads, writes):
            if s2 is None:
                return S.op(eng, lambda e: e.tensor_scalar(out=out, in0=in0, scalar1=s1, scalar2=None, op0=op0), reads, writes)
            return S.op(eng, lambda e: e.tensor_scalar(out=out, in0=in0, scalar1=s1, scalar2=s2, op0=op0, op1=op1), reads, writes)

        def stt(eng, out, in0, scalar, in1, op0, op1, reads, writes):
            return S.op(eng, lambda e: e.scalar_tensor_tensor(out=out, in0=in0, scalar=scalar, in1=in1, op0=op0, op1=op1), reads, writes)

        def cp(eng, out, in_, reads, writes):
            return S.op(eng, lambda e: e.tensor_copy(out=out, in_=in_), reads, writes)

        def mm(specs, reads, writes):
            fns = []
            for sp_ in specs:
                o, l, r, a, z = sp_[:5]
                if len(sp_) > 5 and sp_[5]:
                    fns.append(lambda e, o=o, l=l, r=r, a=a, z=z: e.matmul(o, l, r, start=a, stop=z, skip_group_check=True))
                else:
                    fns.append(lambda e, o=o, l=l, r=r, a=a, z=z: e.matmul(o, l, r, start=a, stop=z))
            return S.op("pe", fns, reads, writes)

        def vcol(c, rows=None):
            if rows is None:
                return vec[:, c:c + 1]
            return vec[rows, c:c + 1]

        ident = cst[:, C_ID:C_ID + 128]
        epsc = sb("epsc", [128, 1], F32)
        epsb = Buf("eps")
        S.op("dve", lambda e: e.memset(epsc[:, :], EPS), writes=[epsb])

        sl = {"issued": 0, "next": 0}
        total_slabs = 24 + nl * (SLABS_PER_LAYER - 24) + (nl - 1) * 24

        def slab_issue(i):
            s = i % NSL
            S.dma("pool", slabs[:, s, :, :], wslab_d[i].rearrange("p (k c) -> p k c", k=8), reads=(), writes=[slb[s]])

        def get_slab():
            i = sl["next"]
            sl["next"] += 1
            while sl["issued"] < min(total_slabs, i + NSL):
                slab_issue(sl["issued"])
                sl["issued"] += 1
            s = i % NSL
            return slabs[:, s, :, :], slb[s]

        S.dma("sp", vec[:, :], vecs_d[:, :], writes=[vecb])
        S.dma("sp", ccs[:, :], cc_d[:, :], writes=[ccb])
        S.dma("pool", cst[:, :], cst_d[:, :], writes=[cstb])
        for k in range(8):
            S.dma("sp", x[:, k, :], xT_d[k * 128:(k + 1) * 128, :], writes=xb[k])
        S.dma("sp", rope[:, :], rope_d[:, :], writes=[ropeb])
        S.dma("pool", cpw[:, :], cpw_d[:, :], writes=[cpwb])
        S.dma("pool", plw[:, :], plw_d[:, :], writes=[plwb])
        S.op("dve", lambda e: e.memset(uA[:, :, :], 0.0), writes=uAb)
        S.op("pool", lambda e: e.memset(uC[:, :, :], 0.0), writes=uCb)
        S.op("dve", lambda e: e.memset(Vaug[:, :, :, 64:128], 1.0), writes=vb)
        S.op("dve", lambda e: e.memset(cvaug[:, :, :, 64:128], 1.0), writes=cvb)
        act(csil[:, :, :], ccs[:, :].rearrange("p (k t) -> p k t", t=2), AF.Silu, [ccb], [csilb])
        act(esink[:, :], vec[:, V_SINK:V_SINK + 32], AF.Exp, [vecb], [esinkb])

        def tile_cols(t):
            return t * TW, (t + 1) * TW

        def sq_tile():
            i = st["sq8"] % 8
            st["sq8"] += 1
            if i < 4:
                return g16[:, i, :], g16b[i]
            return s16[:, i - 4, :], s16b[i - 4]

        def stats_act(t, k):
            c0, c1 = tile_cols(t)
            sq, sqb = sq_tile()
            act(sq, x[:, k, c0:c1], AF.Square, [xb[k][t]], [sqb])
            return sq, sqb

        def stats_mm(t, sq, sqb, first, last):
            mm([(pbank[5 + t][:, :], cst[:, C_O1024:C_O1024 + 128], sq, first, last)], [cstb, sqb], [psb[5 + t]])

        def stats_sq(t, k, first, last):
            sq, sqb = stats_act(t, k)
            stats_mm(t, sq, sqb, first, last)

        def stats_fin(t):
            act(Rt3[:, t, :], pbank[5 + t][:, :], AF.Ln, [psb[5 + t]], [Rb3[t]], bias=EPS)
            act(Rt3[:, t, :], Rt3[:, t, :], AF.Exp, [Rb3[t]], [Rb3[t]], scale=-0.5)

        def ada_slab(l, c, ab=7):
            sap, sbuf_ = get_slab()
            mm([(pbank[ab][:, 2 * c:2 * c + 2], sap[:, k, :], csil[:, k, :], k == 0, k == 7) for k in range(8)],
               [sbuf_, csilb], [psb[ab]])

        def ada_finish(l, ab=7):
            vbase = l * LV
            modT = modT2[:, l % 2]
            modb = modb2[l % 2]
            gmod = gmod2[:, l % 2]
            gmodb = gmodb2[l % 2]
            tt("dve", modT, pbank[ab][:, 0:48].rearrange("p (c t) -> p c t", t=2),
               vec[:, vbase + V_BADA:vbase + V_BADA + 24].unsqueeze(2).to_broadcast([128, 24, 2]), ALU.add,
               [psb[ab], vecb], [modb])
            ts("dve", gmod, modT[:, 8:16, :], 1.0, None, ALU.add, None, [modb], [gmodb])
            tt("dve", gmod, gmod,
               vec[:, vbase + V_NORMW:vbase + V_NORMW + 8].unsqueeze(2).to_broadcast([128, 8, 2]), ALU.mult,
               [gmodb, vecb], [gmodb])

        def layer(l):
            vbase = l * LV
            modT = modT2[:, l % 2]
            modb = modb2[l % 2]
            gmod = gmod2[:, l % 2]
            gmodb = gmodb2[l % 2]
            if stop == 'ada':
                return
            transfer(alias_all, h_all)
            for t in range(3):
                stats_fin(t)
            cnt_n = 0
            for k in range(8):
                for t in range(3):
                    c0, c1 = tile_cols(t)
                    col = 0 if t == 0 else 1
                    xr, xrb = t32()
                    eng_ = "pool" if (k + t) % 3 == 2 else "dve"
                    cnt_n += 1
                    tt(eng_, xr, x[:, k, c0:c1], Rt3[:, t, :], ALU.mult, [xb[k][t], Rb3[t]], [xrb])
                    if k % 2 == 0:
                        act(hap(k, c0, c1), xr, AF.Identity, [xrb, gmodb, modb], [hb[k][t]],
                            bias=modT[:, k, col:col + 1], scale=gmod[:, k, col:col + 1])
                    else:
                        ts("dve", hap(k, c0, c1), xr, gmod[:, k, col:col + 1], modT[:, k, col:col + 1], ALU.mult, ALU.add,
                           [xrb, gmodb, modb], [hb[k][t]])

            if stop == 'norm':
                return
            for si, (kind, idx) in enumerate(INPROJ):
                if stop is not None and stop.startswith('inproj:') and si >= int(stop.split(':')[1]):
                    return
                sap, sbuf_ = get_slab()
                base = 0 if si % 2 == 0 else 3
                if kind == "ktok":
                    bank, bb = pbank[base], psb[base]
                    mm([(bank[:, tb * 128:(tb + 1) * 128], hap(k, tb * 128, (tb + 1) * 128), sap[:, k, :], k == 0, k == 7)
                        for tb in range(4) for k in range(8)],
                       [sbuf_] + [hb[k][0] for k in range(8)], [bb])
                    ko, kob = t32()
                    cp("dve", ko, bank[:, :], [bb], [kob])
                    for s_ in range(2):
                        S.dma("sp", nk_d[s_, l].rearrange("(r p) c -> p r c", p=128),
                              ko[:, s_ * 256:(s_ + 1) * 256].rearrange("p (r c) -> p r c", r=2), reads=[kob], is_output=True)
                    continue
                if kind == "vtok":
                    for g in range(3):
                        bank, bb = pbank[base + g], psb[base + g]
                        mm([(bank[:, tb * 128:(tb + 1) * 128], hap(k, (g * 4 + tb) * 128, (g * 4 + tb + 1) * 128), sap[:, k, :], k == 0, k == 7)
                            for tb in range(4) for k in range(8)],
                           [sbuf_] + [hb[k][g] for k in range(8)], [bb])
                        if g == 0:
                            vo, vob = t32()
                            cp("dve", vo, bank[:, :], [bb], [vob])
                            for s_ in range(2):
                                S.dma("sp", nv_d[s_, l].rearrange("(r p) c -> p r c", p=128),
                                      vo[:, s_ * 256:(s_ + 1) * 256].rearrange("p (r c) -> p r c", r=2), reads=[vob], is_output=True)
                        for kv in range(2):
                            if os.environ.get("VT_SKIP") == "act":
                                continue
                            cp("dve", Vaug[:, g * 4:(g + 1) * 4, kv, 0:64],
                               bank[:, :].rearrange("p (b v d) -> p b v d", b=4, v=2)[:, :, kv, :], [bb], [vb[g]])
                    continue
                banks = [pbank[base + t] for t in range(3)]
                bbs = [psb[base + t] for t in range(3)]
                if si == 0:
                    for k in range(8):
                        mm([(banks[t][:, :], sap[:, k, :], hap(k, t * TW, (t + 1) * TW), k == 0, k == 7) for t in range(3)],
                           [sbuf_] + hb[k], bbs)
                else:
                    mm([(banks[t][:, :], sap[:, k, :], hap(k, t * TW, (t + 1) * TW), k == 0, k == 7)
                        for k in range(8) for t in range(3)],
                       [sbuf_] + h_all, bbs)
                for t in range(3):
                    c0, c1 = tile_cols(t)
                    ps = banks[t][:, :]
                    pb_ = bbs[t]
                    if kind in ("glu", "val"):
                        if t == 0:
                            dst = uA[:, idx, PA:PA + 2 * 286].rearrange("p (s w) -> p s w", s=2)[:, :, 0:256]
                            src = ps.rearrange("p (s w) -> p s w", s=2)
                        else:
                            dst = uA[:, idx, UA_BASE[2] + (t - 1) * TW:UA_BASE[2] + t * TW]
                            src = ps
                        if kind == "glu":
                            act(dst, src, AF.Sigmoid, [pb_], [uAb[idx]])
                        else:
                            tt("dve", dst, src, dst, ALU.mult, [pb_, uAb[idx]], [uAb[idx]])
                    elif kind == "agate":
                        act(m[:, idx, c0:c1], ps, AF.Silu, [pb_], [mb[idx][t]])
                    elif kind == "bgate":
                        act(m[:, 2 + idx, c0:c1], ps, AF.Silu, [pb_], [mb[2 + idx][t]])
                    elif kind == "cgate":
                        act(m[:, 6 + idx, c0:c1], ps, AF.Silu, [pb_], [mb[6 + idx][t]])
                    elif kind == "cin":
                        if t == 0:
                            dst = uC[:, idx, PC:PC + 2 * 272].rearrange("p (s w) -> p s w", s=2)[:, :, 0:256]
                            src = ps.rearrange("p (s w) -> p s w", s=2)
                        else:
                            dst = uC[:, idx, UC_BASE[2] + (t - 1) * TW:UC_BASE[2] + t * TW]
                            src = ps
                        cp("dve", dst, src, [pb_], [uCb[idx]])
                    elif kind in ("q", "kd"):
                        if kind == "q":
                            dst, dstb = QT[:, idx, c0:c1], qb_[idx][t]
                        else:
                            dst, dstb = KT[:, idx, c0:c1], kb_[idx][t]
                        if t == 0:
                            cp("dve", dst, ps, [pb_], [dstb])
                        else:
                            r0 = (t - 1) * TW
                            qraw, qrb = t16()
                            act(qraw, ps, AF.Copy, [pb_], [qrb])
                            t1, t1b = t32()
                            tt("dve", t1, ps, rope[:, r0:r0 + TW], ALU.mult, [pb_, ropeb], [t1b])
                            rb_i = 6 + (t % 2)
                            mm([(pbank[rb_i][:, :], cst[:, C_RT:C_RT + 128], qraw, True, True)], [cstb, qrb], [psb[rb_i]])
                            t2, t2b = t32()
                            tt("dve", t2, pbank[rb_i][:, :], rope[:, 1024 + r0:1024 + r0 + TW], ALU.mult, [psb[rb_i], ropeb], [t2b])
                            tt("pool", dst, t1, t2, ALU.add, [t1b, t2b], [dstb])

            if stop == 'inproj':
                return
            transfer(h_all, alias_all)

            for kv in range(2):
                for dup in range(2):
                    S.dma("pool", cksrc[:, :, kv, dup, :], ck_d[l][:, kv * 64:(kv + 1) * 64].rearrange("(ct p) d -> p ct d", p=128),
                          writes=[cksrcb[kv * 2 + dup]])
                S.dma("pool", cvaug[:, :, kv, 0:64], cv_d[l][:, kv * 64:(kv + 1) * 64].rearrange("(ct p) d -> p ct d", p=128),
                      writes=[cvb[kv]])

            if stop == 'ctxdma':
                return
            thunks = []

            def lvl(eng, dst, src, sh_lo, sh_hi, lo, hi, rows, reads, writes):
                tt(eng, dst[rows, lo:hi], src[rows, lo + sh_lo:hi + sh_lo], src[rows, lo + sh_hi:hi + sh_hi], ALU.add, reads, writes)

            lo_r, hi_r = slice(0, 64), slice(64, 128)

            def pool_tile(j, t):
                u = uC[:, j, :]
                ub = uCb[j]
                tabc = V_PTAB + j * 17
                c0, c1 = tile_cols(t)
                d16, d16b = t16()
                if t == 0:
                    dst = d16.rearrange("p (s w) -> p s w", s=2)
                    sS = poolB[:, PC:PC + 2 * 272].rearrange("p (s w) -> p s w", s=2)[:, :, 0:256]
                    sU = uC[:, j, PC:PC + 2 * 272].rearrange("p (s w) -> p s w", s=2)[:, :, 0:256]
                else:
                    o = UC_BASE[2] + (t - 1) * TW
                    dst, sS, sU = d16, poolB[:, o:o + TW], uC[:, j, o:o + TW]
                stt("dve", dst, sS, vcol(tabc), sU, ALU.mult, ALU.subtract, [pBb, ub, vecb], [d16b])
                fixes = []
                if t == 0:
                    for s_ in range(2):
                        fixes.append((s_ * 256, UC_BASE[s_], tabc + 1))
                        fixes.append((s_ * 256 + 248, UC_BASE[s_] + 248, tabc + 9))
                elif t == 1:
                    fixes.append((0, UC_BASE[2], tabc + 1))
                else:
                    fixes.append((504, UC_BASE[2] + 1016, tabc + 9))
                for (dc, pc_, tc) in fixes:
                    ei = ring("ed", 4)
                    tt("dve", edg[:, ei, :], poolB[:, pc_:pc_ + 8], vec[:, tc:tc + 8], ALU.mult, [pBb, vecb], [edb[ei]])
                    tt("dve", d16[:, dc:dc + 8], edg[:, ei, :], uC[:, j, pc_:pc_ + 8], ALU.subtract, [edb[ei], ub, d16b], [d16b])
                ob = ring("sbr", 3)
                mm([(pbank[ob][:, :], plw[:, (l * 2 + j) * 128:(l * 2 + j + 1) * 128], d16, True, True)], [plwb, d16b], [psb[ob]])
                stt("dve", m[:, 6 + j, c0:c1], pbank[ob][:, :], vcol(vbase + V_PSCALE + j), m[:, 6 + j, c0:c1],
                    ALU.mult, ALU.mult, [psb[ob], vecb, mb[6 + j][t]], [mb[6 + j][t]])

            for j in range(2):
                u = uC[:, j, :]
                ub = uCb[j]
                L_ = lambda *a: thunks.append(lambda a=a: lvl("pool", *a))
                if j == 0:
                    L_(poolB, u, -1, 0, 1, WC, lo_r, [ub], [pBb])
                    L_(poolA, u, -1, 0, 1, WC, hi_r, [ub], [pAb])
                    L_(poolB, poolA, -1, 1, 2, WC - 1, hi_r, [pAb], [pBb])
                else:
                    L_(poolB, u, -1, 0, 1, WC, lo_r, [ub], [pBb])
                    L_(poolA, u, -1, 0, 1, WC, hi_r, [ub], [pAb])
                    L_(poolA, poolB, -1, 1, 2, WC - 1, lo_r, [pBb], [pAb])
                    L_(poolB, poolA, -1, 1, 2, WC - 1, hi_r, [pAb], [pBb])
                    L_(poolB, poolA, -2, 2, 4, WC - 3, lo_r, [pAb], [pBb])
                    L_(poolA, poolB, -2, 2, 4, WC - 3, hi_r, [pBb], [pAb])
                    L_(poolB, poolA, -4, 4, 8, WC - 7, hi_r, [pAb], [pBb])
                for t in range(3):
                    thunks.append(lambda j=j, t=t: pool_tile(j, t))

            work = []
            lev0, til0, lev1, til1 = thunks[0:3], thunks[3:6], thunks[6:13], thunks[13:16]
            thunks = []
            for f_ in lev0:
                f_()
            CB = (5, 6)

            conv_list = [(t, j, k) for t in range(3) for j in range(2) for k in range(31)]
            conv_dg = {}
            DG_AHEAD = 6

            def build_diag(n):
                if n >= len(conv_list):
                    return
                t, j, k = conv_list[n]
                di = ring("dg", NDG)
                ts("dve", diag[:, di, :], ident, vcol(vbase + V_DW + k * 2 + j), None, ALU.mult, None,
                   [cstb, vecb], [dgb[di]])
                conv_dg[n] = di

            def conv_item(n):
                t, j, k = conv_list[n]
                di = conv_dg.pop(n)
                bank, bb = pbank[CB[j]], psb[CB[j]]
                if t == 0:
                    spec = (bank[:, :].rearrange("p (s w) -> p s w", s=2), diag[:, di, :],
                            uA[:, j, k:k + 2 * 286].rearrange("p (s w) -> p s w", s=2)[:, :, 0:256], k == 0, k == 30)
                else:
                    o = UA_BASE[2] - PA + (t - 1) * TW + k
                    spec = (bank[:, :], diag[:, di, :], uA[:, j, o:o + TW], k == 0, k == 30)
                mm([spec], [dgb[di], uAb[j]], [bb])
                build_diag(n + DG_AHEAD)

            for n_ in range(DG_AHEAD):
                build_diag(n_)

            LN = {}

            def ln_A1(t):
                y32 = []
                for j in range(2):
                    bank, bb = pbank[CB[j]], psb[CB[j]]
                    y, yb = b32()
                    act(y, bank[:, :], AF.Identity, [bb, vecb], [yb], bias=vcol(vbase + V_CONVB + j))
                    y32.append((y, yb))
                LN[t] = {"y": y32}

            def ln_A2(t):
                aux = []
                for j in range(2):
                    y, yb = LN[t]["y"][j]
                    ysq, ysqb = t16()
                    act(ysq, y, AF.Square, [yb], [ysqb])
                    y16, y16b = t16()
                    act(y16, y, AF.Copy, [yb], [y16b])
                    aux.append((ysq, ysqb, y16, y16b))
                LN[t]["aux"] = aux

            def ln_A3(t):
                r6, r7 = ring("sbr", 3), ring("sbr", 3)
                for j in range(2):
                    ysq, ysqb, y16, y16b = LN[t]["aux"][j]
                    mm([(pbank[r6][:, :], cst[:, C_O256:C_O256 + 128], y16, j == 0, j == 1)], [cstb, y16b], [psb[r6]])
                    mm([(pbank[r7][:, :], cst[:, C_O256:C_O256 + 128], ysq, j == 0, j == 1)], [cstb, ysqb], [psb[r7]])
                mean, meanb = ms32()
                act(mean, pbank[r6][:, :], AF.Copy, [psb[r6]], [meanb])
                msq, msqb = ms32()
                act(msq, pbank[r6][:, :], AF.Square, [psb[r6]], [msqb])
                tt("dve", msq, pbank[r7][:, :], msq, ALU.subtract, [psb[r7], msqb], [msqb])
                act(msq, msq, AF.Ln, [msqb, epsb], [msqb], bias=epsc[:, 0:1])
                act(msq, msq, AF.Exp, [msqb], [msqb], scale=-0.5)
                for j in range(2):
                    y, yb = LN[t]["y"][j]
                    tt("dve", y, y, mean, ALU.subtract, [yb, meanb], [yb])
                    tt("dve", y, y, msq, ALU.mult, [yb, msqb], [yb])

            def ln_C(t):
                sj = []
                for j in range(2):
                    y, yb = LN[t]["y"][j]
                    si_ = ring("s16", 4)
                    act(s16[:, si_, :], y, AF.Silu, [yb, vecb], [s16b[si_]], bias=vcol(vbase + V_LNB + j), scale=vcol(vbase + V_LNG + j))
                    sj.append((s16[:, si_, :], s16b[si_]))
                LN[t]["s"] = sj

            def ln_D(t):
                c0, c1 = tile_cols(t)
                sj = LN[t]["s"]
                for jo in range(2):
                    ob = ring("sbr", 3)
                    mm([(pbank[ob][:, :], cpw[:, (l * 2 + ji) * 256 + jo * 128:(l * 2 + ji) * 256 + jo * 128 + 128], sj[ji][0], ji == 0, ji == 1)
                        for ji in range(2)], [cpwb, sj[0][1], sj[1][1]], [psb[ob]])
                    tt("dve", m[:, jo, c0:c1], pbank[ob][:, :], m[:, jo, c0:c1], ALU.mult, [psb[ob], mb[jo][t]], [mb[jo][t]])

            lnq = []
            for t in range(3):
                for j in range(2):
                    for k in range(31):
                        work.append(lambda n=(t * 2 + j) * 31 + k: conv_item(n))
                        if lnq and (j * 31 + k) % 3 == 2:
                            work.append(lnq.pop(0))
                lnq = [lambda t=t: ln_A1(t), lambda t=t: ln_A2(t), lambda t=t: ln_A3(t), lambda t=t: ln_C(t), lambda t=t: ln_D(t)]
                work.append(lnq.pop(0))
            work.extend(lnq)
            work[40:40] = til0 + lev1
            work.extend(til1)
            ln_thunks = []

            if stop == 'pool':
                return
            for kv in range(2):
                bi = 3 + kv
                mm([(pbank[bi][:, ct * 128:(ct + 1) * 128], cksrc[:, ct, kv, :, :].rearrange("p a d -> p (a d)"), ident, True, True) for ct in range(4)],
                   [cksrcb[kv * 2], cksrcb[kv * 2 + 1], cstb], [psb[bi]])
                cp("dve", ckT[:, kv, :], pbank[bi][:, :], [psb[bi]], [ckTb[kv]])

            rws = [slice(0, 64), slice(64, 128)]

            def unit_tiles(unit):
                tiles = []
                if unit[0] == "p":
                    _, s_, hd = unit
                    c, half, kvh = hd // 2, hd % 2, hd // 4
                    rows = rws[half]
                    tiles.append(dict(
                        smm=[(KT[rows, kvh, (s_ * 2 + kt) * 128:(s_ * 2 + kt + 1) * 128], QT[rows, c, s_ * 256:(s_ + 1) * 256], kt * 256, 256)
                             for kt in range(2)],
                        ncols=512, masks=[], srd=[kb_[kvh][0], qb_[c][0]],
                        pv=[(Vaug[:, s_ * 2 + kt, kvh, :], kt * 256, 256, 0, 0) for kt in range(2)], prd=[vb[0]]))
                    return tiles
                _, hd = unit
                c, half, kvh = hd // 2, hd % 2, hd // 4
                rows = rws[half]
                loc, ctx = [], []
                for j in range(8):
                    lo, hi = max(0, j - 1), min(8, j + 2)
                    n = (hi - lo) * 128
                    masks = []
                    for qbk in range(lo, hi):
                        if qbk == j - 1:
                            masks.append(((qbk - lo) * 128, C_MNEXT))
                        elif qbk == j + 1:
                            masks.append(((qbk - lo) * 128, C_MPREV))
                    pv = []
                    for b_ in range(2):
                        a0, a1 = max(lo * 128, b_ * 512), min(hi * 128, (b_ + 1) * 512)
                        if a1 > a0:
                            pv.append((Vaug[:, 4 + j, kvh, :], a0 - lo * 128, a1 - a0, b_, a0 - b_ * 512))
                    loc.append(dict(
                        smm=[(KT[rows, kvh, 512 + j * 128:512 + (j + 1) * 128], QT[rows, c, 512 + lo * 128:512 + hi * 128], 0, n)],
                        ncols=n, masks=masks, srd=[kb_[kvh][1 + j // 4], qb_[c][1], qb_[c][2]],
                        pv=pv, prd=[vb[1 + j // 4]]))
                for qh in range(2):
                    for ct in range(4):
                        ctx.append(dict(
                            smm=[(ckT[rows, kvh, ct * 128:(ct + 1) * 128], QT[rows, c, 512 + qh * 512:512 + (qh + 1) * 512], 0, 512)],
                            ncols=512, masks=[], srd=[ckTb[kvh], qb_[c][1 + qh]],
                            pv=[(cvaug[:, ct, kvh, :], 0, 512, qh, 0)], prd=[cvb[kvh]]))
                for i in range(8):
                    tiles.append(ctx[i])
                    tiles.append(loc[i])
                return tiles

            units = [("p", s_, hd) for s_ in range(2) for hd in range(8)] + [("s", hd) for hd in range(8)]
            flat = []
            for ui, unit in enumerate(units):
                tl = unit_tiles(unit)
                for ti, tile in enumerate(tl):
                    flat.append((ui, unit, tile, ti == len(tl) - 1))
            ostart = {}
            unit_first = {}
            for (ui_, unit_, tile_, last_) in flat:
                unit_first.setdefault(ui_, tile_)

            def emit_S(idx, tile):
                sbi = ring("sbr", 3)
                fill = []
                mm(fill + [(pbank[sbi][:, oc:oc + n], kt_, q_, True, True) for (kt_, q_, oc, n) in tile["smm"]],
                   tile["srd"] + [cstb, mb[0][0]], [psb[sbi]])
                pti = ring("pt", 4)
                tile["pti"] = pti
                nco = tile["ncols"]
                act(PT[:, pti, 0:nco], pbank[sbi][:, 0:nco], AF.Exp, [psb[sbi]], [PTb[pti]], scale=0.125)
                for (c0_, mk) in tile["masks"]:
                    tt("dve", PT[:, pti, c0_:c0_ + 128], PT[:, pti, c0_:c0_ + 128], cst[:, mk:mk + 128], ALU.mult,
                       [PTb[pti], cstb], [PTb[pti]])

            def norm_piece(Ob, Obb, n, hd, mcol0, qt):
                c, half = hd // 2, hd % 2
                ro = rws[half]
                rd, rdb_ = ms32()
                tmp, tmpb = ms32()
                act(rd[64:128, 0:n], Ob[64:128, 0:n], AF.Ln, [Obb, esinkb], [rdb_], bias=esink[64:128, l * 8 + hd:l * 8 + hd + 1])
                act(rd[64:128, 0:n], rd[64:128, 0:n], AF.Exp, [rdb_], [rdb_], scale=-1.0)
                tt("dve", tmp[ro, 0:n], Ob[0:64, 0:n], rd[64:128, 0:n], ALU.mult, [Obb, rdb_], [tmpb])
                tt("dve", m[ro, 2 + c, mcol0:mcol0 + n], tmp[ro, 0:n], m[ro, 2 + c, mcol0:mcol0 + n], ALU.mult,
                   [tmpb, mb[2 + c][qt]], [mb[2 + c][qt]])

            def emit_PV(idx, ui, unit, tile, last):
                pti = tile["pti"]
                oset = (3, 4) if (unit[0] == 's' or ui % 2 == 0) else (4, 3)
                for (va, pc0, n, ob_, oc0) in tile["pv"]:
                    bi = oset[ob_]
                    first = (ui, ob_) not in ostart
                    ostart[(ui, ob_)] = True
                    mm([(pbank[bi][:, oc0:oc0 + n], va, PT[:, pti, pc0:pc0 + n], first, True, True)],
                       [PTb[pti]] + tile["prd"], [psb[bi]])
                if last:
                    if unit[0] == "p":
                        _, s_, hd = unit
                        norm_piece(pbank[oset[0]], psb[oset[0]], 256, hd, s_ * 256, 0)
                    else:
                        _, hd = unit
                        for b_ in range(2):
                            norm_piece(pbank[oset[b_]], psb[oset[b_]], 512, hd, 512 + b_ * 512, 1 + b_)
                    if l + 1 < nl:
                        ada_slab(l + 1, ui)
                    if ln_thunks:
                        ln_thunks.pop(0)()

            pend = []
            wacc = [0.0]
            wrate = (len(work) + 4.0) / max(1, len(flat) - 8)
            for idx, (ui, unit, tile, last) in enumerate(flat):
                emit_S(idx, tile)
                wacc[0] += wrate
                while wacc[0] >= 1.0 and work:
                    work.pop(0)()
                    wacc[0] -= 1.0
                pend.append((idx, ui, unit, tile, last))
                if len(pend) > 2:
                    emit_PV(*pend.pop(0))
            while pend:
                emit_PV(*pend.pop(0))
            while work:
                work.pop(0)()

            if l + 1 < nl:
                ada_finish(l + 1)

            if stop == 'attn':
                return
            m_all = [b for row in mb for b in row]
            pend_stats = []
            for mo in range(8):
                sap, sbuf_ = get_slab()
                bidx = [(3 * mo + t) % 5 for t in range(3)]
                banks = [pbank[i] for i in bidx]
                bbs = [psb[i] for i in bidx]
                mm([(banks[t][:, :], sap[:, k, :], m[:, k, t * TW:(t + 1) * TW], k == 0, k == 7)
                    for k in range(8) for t in range(3)], [sbuf_] + m_all, bbs)
                for (t_, sq_, sqb_, mo_) in pend_stats:
                    stats_mm(t_, sq_, sqb_, mo_ == 0, mo_ == 7)
                pend_stats = []
                for t in range(3):
                    c0, c1 = tile_cols(t)
                    col = 0 if t == 0 else 1
                    stt("dve", x[:, mo, c0:c1], banks[t][:, :], modT[:, 16 + mo, col:col + 1], x[:, mo, c0:c1],
                        ALU.mult, ALU.add, [bbs[t], modb, xb[mo][t]], [xb[mo][t]])
                    sq_, sqb_ = stats_act(t, mo)
                    pend_stats.append((t, sq_, sqb_, mo))
            for (t_, sq_, sqb_, mo_) in pend_stats:
                stats_mm(t_, sq_, sqb_, mo_ == 0, mo_ == 7)

        for t in range(3):
            for k in range(8):
                stats_sq(t, k, k == 0, k == 7)
        for c in range(24):
            ada_slab(0, c, 0)
        ada_finish(0, 0)
        for l in range(nl):
            layer(l)

        for t in range(3):
            stats_fin(t)
        for t in range(3):
            c0, c1 = tile_cols(t)
            for k in range(8):
                yo, yob = t32()
                stt("dve", yo, x[:, k, c0:c1], vcol(V_FNW + k), Rt3[:, t, :], ALU.mult, ALU.mult, [xb[k][t], Rb3[t], vecb], [yob])
                S.dma("sp", yT_d[k * 128:(k + 1) * 128, c0:c1], yo, reads=[yob], is_output=True)
        S.finish()
    return nc


def _const_tables():
    cst = np.zeros((128, NCST), np.float32)
    cst[:, C_ID:C_ID + 128] = np.eye(128, dtype=np.float32)
    R = np.zeros((64, 64), np.float32)
    for d in range(16):
        R[d, d + 16] = -1.0
        R[d + 16, d] = 1.0
        R[32 + d, 48 + d] = -1.0
        R[48 + d, 32 + d] = 1.0
    Rt = np.zeros((128, 128), np.float32)
    Rt[0:64, 0:64] = R.T
    Rt[64:128, 64:128] = R.T
    cst[:, C_RT:C_RT + 128] = Rt
    cst[:, C_O1024:C_O1024 + 128] = 1.0 / 1024.0
    cst[:, C_O256:C_O256 + 128] = 1.0 / 256.0
    jk = np.arange(128)[:, None]
    r = np.arange(128)[None, :]
    cst[:, C_MNEXT:C_MNEXT + 128] = (jk <= r).astype(np.float32)
    cst[:, C_MPREV:C_MPREV + 128] = (jk >= r).astype(np.float32)
    quarter = 16
    freqs = 10000.0 ** (-np.arange(quarter, dtype=np.float64) / quarter)
    tpos = np.arange(1024)
    rows = (tpos // 64).astype(np.float64)
    cols = (tpos % 64).astype(np.float64)
    rope = np.zeros((128, 2048), np.float32)
    for p in range(128):
        d = p % 64
        pos = rows if d < 32 else cols
        ang = pos * freqs[d % 16]
        rope[p, 0:1024] = np.cos(ang).astype(np.float32)
        rope[p, 1024:2048] = np.sin(ang).astype(np.float32)
    ptab = np.zeros((128, 34), np.float32)
    for j in range(2):
        for p in range(128):
            w = POOL_WINDOWS[2 * j + p // 64]
            ptab[p, j * 17] = 1.0 / w
            for t in range(8):
                cnt = t + w // 2 if t < w // 2 else w
                ptab[p, j * 17 + 1 + t] = 1.0 / cnt
            for c in range(8):
                rr = 7 - c
                cnt = rr + 1 + w // 2 if w // 2 >= rr + 1 else w
                ptab[p, j * 17 + 9 + c] = 1.0 / cnt
    return cst, rope, ptab


def _colvec(v, n):
    return np.ascontiguousarray(np.asarray(v, np.float32).reshape(n, 128).T)


_PROG = {}


def kernel(x_prompt, x_sample, c, cache_k, cache_v, c_ctx, w_ada, b_ada, norm_w, w_in,
           conv_dw, conv_b, conv_ln_g, conv_ln_b, conv_pw, attn_sink, pool_w, pool_scale,
           w_out, final_norm_w, _nl=NL, _stop=None):
    f = lambda a: np.asarray(a, np.float32)
    x_prompt, x_sample, c, cache_k, cache_v, c_ctx = map(f, (x_prompt, x_sample, c, cache_k, cache_v, c_ctx))
    w_ada, b_ada, norm_w, w_in, conv_dw, conv_b = map(f, (w_ada, b_ada, norm_w, w_in, conv_dw, conv_b))
    conv_ln_g, conv_ln_b, conv_pw, attn_sink, pool_w, pool_scale = map(f, (conv_ln_g, conv_ln_b, conv_pw, attn_sink, pool_w, pool_scale))
    w_out, final_norm_w = f(w_out), f(final_norm_w)
    nl = _nl
    cst, rope, ptab = _const_tables()

    wslab = np.empty((NL * SLABS_PER_LAYER, 128, 1024), np.float32)

    def put(i, W, cols):
        wslab[i] = W[:, cols].reshape(8, 128, 128).transpose(1, 0, 2).reshape(128, 1024)

    i = 0
    for cc_ in range(24):
        put(i, w_ada[0], np.arange(cc_ * 128, cc_ * 128 + 128))
        i += 1
    for l in range(NL):
        for kind, idx in INPROJ:
            put(i, w_in[l], _slab_cols(kind, idx))
            i += 1
        if l + 1 < NL:
            for cc_ in range(24):
                put(i, w_ada[l + 1], np.arange(cc_ * 128, cc_ * 128 + 128))
                i += 1
        for mo in range(8):
            put(i, w_out[l], np.arange(mo * 128, mo * 128 + 128))
            i += 1

    vecs_base = np.zeros((128, NV), np.float32)
    for l in range(NL):
        b = l * LV
        vecs_base[:, b + V_NORMW:b + V_NORMW + 8] = _colvec(norm_w[l], 8)
        vecs_base[:, b + V_BADA:b + V_BADA + 24] = _colvec(b_ada[l], 24)
        vecs_base[:, b + V_CONVB:b + V_CONVB + 2] = _colvec(conv_b[l], 2)
        vecs_base[:, b + V_LNG:b + V_LNG + 2] = _colvec(conv_ln_g[l], 2)
        vecs_base[:, b + V_LNB:b + V_LNB + 2] = _colvec(conv_ln_b[l], 2)
        vecs_base[:, b + V_PSCALE:b + V_PSCALE + 2] = _colvec(pool_scale[l], 2)
        for k in range(31):
            vecs_base[:, b + V_DW + k * 2:b + V_DW + k * 2 + 2] = _colvec(conv_dw[l, k], 2)
    vecs_base[:, V_FNW:V_FNW + 8] = _colvec(final_norm_w, 8)
    vecs_base[:, V_SINK:V_SINK + 32] = np.broadcast_to(attn_sink.reshape(1, 32), (128, 32))
    vecs_base[:, V_PTAB:V_PTAB + 34] = ptab

    cpw = np.zeros((128, NL, 2, 256), np.float32)
    plw = np.zeros((128, NL, 2, 128), np.float32)
    for l in range(NL):
        for ji in range(2):
            cpw[:, l, ji, :] = conv_pw[l, ji * 128:(ji + 1) * 128, :]
        for g in range(4):
            j, hf = g // 2, g % 2
            plw[hf * 64:(hf + 1) * 64, l, j, hf * 64:(hf + 1) * 64] = pool_w[l, g]
    cpw = cpw.reshape(128, -1)
    plw = plw.reshape(128, -1)

    in_maps = []
    for i in range(NCORES):
        xt = np.concatenate([x_prompt[2 * i], x_prompt[2 * i + 1], x_sample[i]], axis=0)
        cc = np.stack([_colvec(c_ctx, 8), _colvec(c[i], 8)], axis=2).reshape(128, 16)
        in_maps.append({
            "xT": np.ascontiguousarray(xt.T),
            "cc": np.ascontiguousarray(cc),
            "vecs": vecs_base,
            "wslab": wslab,
            "cpw": cpw,
            "plw": plw,
            "ck": np.ascontiguousarray(cache_k[i].reshape(NL, 512, 128)),
            "cv": np.ascontiguousarray(cache_v[i].reshape(NL, 512, 128)),
            "rope": rope,
            "cst": cst,
        })
    if (nl, _stop) not in _PROG:
        _PROG[(nl, _stop)] = build_program(nl, _stop)
    res = run_bass_kernel_spmd(_PROG[(nl, _stop)], in_maps, core_ids=list(range(NCORES)))
    y_prompt = np.empty((16, 256, D), np.float32)
    y_sample = np.empty((8, 1024, D), np.float32)
    nk = np.empty((16, NL, 256, 2, 64), np.float32)
    nv = np.empty((16, NL, 256, 2, 64), np.float32)
    for i in range(NCORES):
        r = res.results[i]
        yt = np.asarray(r["yT"]).T
        y_prompt[2 * i] = yt[0:256]
        y_prompt[2 * i + 1] = yt[256:512]
        y_sample[i] = yt[512:1536]
        nk[2 * i:2 * i + 2] = np.asarray(r["nk"]).reshape(2, NL, 256, 2, 64)
        nv[2 * i:2 * i + 2] = np.asarray(r["nv"]).reshape(2, NL, 256, 2, 64)
    return (y_prompt, y_sample, nk, nv)
```

```python
import os
import numpy as np
import concourse.bass as bass
import concourse.mybir as mybir
from concourse.bass_utils import run_bass_kernel_spmd
from contextlib import ExitStack

F32 = mybir.dt.float32
BF16 = mybir.dt.bfloat16
AF = mybir.ActivationFunctionType
ALU = mybir.AluOpType

NCORES = 8
D = 1024
NL = 4
T = 1536
TW = 512
EPS = 1e-6
PA = 15
UA_BASE = [15, 301, 587]
WA = 587 + 1024 + 15
PC = 8
UC_BASE = [8, 280, 552]
WC = 552 + 1024 + 8
POOL_WINDOWS = (2, 4, 8, 16)

LV = 102
V_NORMW, V_BADA, V_CONVB, V_LNG, V_LNB, V_PSCALE, V_DW = 0, 8, 32, 34, 36, 38, 40
V_FNW = NL * LV
V_SINK = V_FNW + 8
V_PTAB = V_SINK + 32
NV = V_PTAB + 34
C_ID, C_RT, C_O1024, C_O256, C_MNEXT, C_MPREV = 0, 128, 256, 384, 512, 640
NCST = 768

INPROJ = ([("glu", 0), ("val", 0), ("glu", 1), ("val", 1), ("agate", 0), ("agate", 1)]
          + [("q", c) for c in range(4)] + [("kd", 0), ("kd", 1), ("ktok", 0), ("vtok", 0)]
          + [("bgate", c) for c in range(4)] + [("cin", 0), ("cin", 1), ("cgate", 0), ("cgate", 1)])
SLABS_PER_LAYER = 24 + len(INPROJ) + 8
NSL = 6
NDUMMY = int(os.environ.get('NDUMMY', '0'))
NBURST = int(os.environ.get('NBURST', '20'))
BURST_AT = [int(v) for v in os.environ.get('BURST_AT', '0').split(',') if v != '']
NDG = 8


def _slab_cols(kind, idx):
    if kind == "glu":
        return np.arange(256 + idx * 128, 256 + idx * 128 + 128)
    if kind == "val":
        return np.arange(idx * 128, idx * 128 + 128)
    if kind == "agate":
        return np.arange(512 + idx * 128, 512 + idx * 128 + 128)
    if kind == "q":
        return np.arange(768 + idx * 128, 768 + idx * 128 + 128)
    if kind == "kd":
        a = np.arange(1280 + idx * 64, 1280 + idx * 64 + 64)
        return np.concatenate([a, a])
    if kind == "ktok":
        return np.arange(1280, 1408)
    if kind == "vtok":
        return np.arange(1408, 1536)
    if kind == "bgate":
        return np.arange(1536 + idx * 128, 1536 + idx * 128 + 128)
    if kind == "cin":
        return np.arange(2048 + idx * 128, 2048 + idx * 128 + 128)
    if kind == "cgate":
        return np.arange(2304 + idx * 128, 2304 + idx * 128 + 128)
    raise ValueError(kind)


class Buf:
    __slots__ = ("name", "w", "r", "dsem", "dcnt")

    def __init__(self, name):
        self.name = name
        self.w = None
        self.r = {}
        self.dsem = None
        self.dcnt = 0


class Sched:
    def __init__(self, nc, es):
        self.nc = nc
        self.es = es
        self.eng = {"pe": nc.tensor, "act": nc.scalar, "dve": nc.vector, "pool": nc.gpsimd, "sp": nc.sync}
        self.sem = {e: es.enter_context(nc.semaphore("s_" + e)) for e in self.eng}
        self.cnt = {e: 0 for e in self.eng}
        self.seen = {e: {} for e in self.eng}
        self.out_events = {}

    def _deps(self, e, reads, writes, skip_own):
        deps = {}
        own = self.sem[e]

        def add(ev):
            if ev is None:
                return
            s, v = ev
            if deps.get(s, 0) < v:
                deps[s] = v

        for b in reads:
            add(b.w)
        for b in writes:
            if b.w is not None and not (skip_own and b.w[0] is own):
                add(b.w)
            for s, v in b.r.items():
                if not (skip_own and s is own):
                    add((s, v))
        return deps

    def _wait(self, e, deps):
        for s, v in deps.items():
            if self.seen[e].get(s, 0) >= v:
                continue
            self.eng[e].wait_ge(s, v)
            self.seen[e][s] = v

    @staticmethod
    def _record(ev, reads, writes):
        for b in reads:
            if b.r.get(ev[0], 0) < ev[1]:
                b.r[ev[0]] = ev[1]
        for b in writes:
            b.w = ev
            b.r = {}

    def op(self, e, fns, reads=(), writes=()):
        if callable(fns):
            fns = [fns]
        self._wait(e, self._deps(e, reads, writes, e == "pe"))
        ins = None
        for f in fns:
            ins = f(self.eng[e])
        ins.then_inc(self.sem[e], 1)
        self.cnt[e] += 1
        ev = (self.sem[e], self.cnt[e])
        self._record(ev, reads, writes)
        return ev

    def dma(self, q, out, in_, reads=(), writes=(), is_output=False):
        self._wait(q, self._deps(q, reads, writes, False))
        b = (list(writes) or list(reads))[0]
        if b.dsem is None:
            b.dsem = self.es.enter_context(self.nc.semaphore("d_" + b.name))
        b.dcnt += 16
        self.eng[q].dma_start(out=out, in_=in_).then_inc(b.dsem, 16)
        ev = (b.dsem, b.dcnt)
        self._record(ev, reads, writes)
        if is_output:
            self.out_events[b.dsem] = max(self.out_events.get(b.dsem, 0), b.dcnt)
        return ev

    def finish(self):
        for s, v in self.out_events.items():
            self.eng["sp"].wait_ge(s, v)


def transfer(src, dst):
    evs = {}
    for b in src:
        if b.w is not None:
            evs[b.w[0]] = max(evs.get(b.w[0], 0), b.w[1])
        for s, v in b.r.items():
            evs[s] = max(evs.get(s, 0), v)
    for b in dst:
        for s, v in evs.items():
            if b.r.get(s, 0) < v:
                b.r[s] = v


def build_program(nl=NL, stop=None):
    nc = bass.Bass("TRN2", target_bir_lowering=False)

    def dram(n, s, kind="ExternalInput"):
        return nc.dram_tensor(n, s, F32, kind=kind).ap()

    xT_d = dram("xT", [D, T])
    cc_d = dram("cc", [128, 16])
    vecs_d = dram("vecs", [128, NV])
    wslab_d = dram("wslab", [NL * SLABS_PER_LAYER, 128, 1024])
    cpw_d = dram("cpw", [128, NL * 2 * 256])
    plw_d = dram("plw", [128, NL * 2 * 128])
    ck_d = dram("ck", [NL, 512, 128])
    cv_d = dram("cv", [NL, 512, 128])
    rope_d = dram("rope", [128, 2048])
    cst_d = dram("cst", [128, NCST])
    yT_d = dram("yT", [D, T], "ExternalOutput")
    nk_d = dram("nk", [2, NL, 256, 128], "ExternalOutput")
    nv_d = dram("nv", [2, NL, 256, 128], "ExternalOutput")

    with ExitStack() as es:
        S = Sched(nc, es)

        def sb(n, s, d):
            return es.enter_context(nc.sbuf_tensor("sb_" + n, s, d))

        x = sb("x", [128, 8, T], F32)
        xb = [[Buf("x%d_%d" % (k, t)) for t in range(3)] for k in range(8)]
        hraw = sb("hraw", [128, 8 * T], BF16)
        hb = [[Buf("h%d_%d" % (k, t)) for t in range(3)] for k in range(8)]
        h_all = [b for row in hb for b in row]

        def hap(k, c0, c1):
            return hraw[:, k * T + c0:k * T + c1]

        m = sb("m", [128, 8, T], BF16)
        mb = [[Buf("m%d_%d" % (k, t)) for t in range(3)] for k in range(8)]
        uA = sb("uA", [128, 2, WA], BF16)
        uAb = [Buf("uA0"), Buf("uA1")]
        QT = sb("QT", [128, 4, T], BF16)
        qb_ = [[Buf("q%d_%d" % (c, t)) for t in range(3)] for c in range(4)]
        KT = sb("KT", [128, 2, T], BF16)
        kb_ = [[Buf("k%d_%d" % (c, t)) for t in range(3)] for c in range(2)]
        Vaug = sb("Vaug", [128, 12, 2, 128], BF16)
        vb = [Buf("v%d" % g) for g in range(3)]
        uC = sb("uC", [128, 2, WC], F32)
        uCb = [Buf("uC0"), Buf("uC1")]
        slabs = sb("slabs", [128, NSL, 8, 128], BF16)
        slb = [Buf("slab%d" % i) for i in range(NSL)]
        diag = sb("diag", [128, NDG, 128], BF16)
        dgb = [Buf("dg%d" % i) for i in range(NDG)]
        rope = sb("rope", [128, 2048], F32)
        ropeb = Buf("rope")
        cst = sb("cst", [128, NCST], BF16)
        cstb = Buf("cst")
        vec = sb("vec", [128, NV], F32)
        vecb = Buf("vec")
        ccs = sb("ccs", [128, 16], F32)
        ccb = Buf("cc")
        csil = sb("csil", [128, 8, 2], BF16)
        csilb = Buf("csil")
        modT2 = sb("modT", [128, 2, 24, 2], F32)
        modb2 = [Buf("mod0"), Buf("mod1")]
        gmod2 = sb("gmod", [128, 2, 8, 2], F32)
        gmodb2 = [Buf("gmod0"), Buf("gmod1")]
        esink = sb("esink", [128, 32], F32)
        esinkb = Buf("esink")
        cpw = sb("cpw", [128, NL * 2 * 256], BF16)
        cpwb = Buf("cpw")
        plw = sb("plw", [128, NL * 2 * 128], BF16)
        plwb = Buf("plw")
        cksrc = sb("cksrc", [128, 4, 2, 2, 64], BF16)
        cksrcb = [Buf("cksrc%d" % i) for i in range(4)]
        ckT = sb("ckT", [128, 2, 512], BF16)
        ckTb = [Buf("ckT0"), Buf("ckT1")]
        cvaug = sb("cvaug", [128, 4, 2, 128], BF16)
        cvb = [Buf("cvaug0"), Buf("cvaug1")]
        PT = sb("PT", [128, 4, 512], BF16)
        PTb = [Buf("pt%d" % i) for i in range(4)]
        Rt3 = sb("Rt3", [128, 3, 512], F32)
        Rb3 = [Buf("R0"), Buf("R1"), Buf("R2")]
        g32 = sb("g32", [128, 4, 512], F32)
        g32b = [Buf("g32_%d" % i) for i in range(4)]
        g16 = sb("g16", [128, 4, 512], BF16)
        g16b = [Buf("g16_%d" % i) for i in range(4)]
        edg = sb("edg", [128, 4, 8], F32)
        edb = [Buf("ed%d" % i) for i in range(4)]
        poolA = hraw[:, 0:2 * WC].bitcast(F32)
        poolB = hraw[:, 2 * WC:4 * WC].bitcast(F32)
        pAb, pBb = Buf("poolA"), Buf("poolB")
        NA32 = 5
        a32 = [hraw[:, 4 * WC + i * 1024:4 * WC + (i + 1) * 1024].bitcast(F32) for i in range(NA32)]
        a32b = [Buf("a32_%d" % i) for i in range(NA32)]
        s16 = sb("s16", [128, 4, 512], BF16)
        s16b = [Buf("s16_%d" % i) for i in range(4)]
        alias_all = [pAb, pBb] + a32b

        pbank = [es.enter_context(nc.psum_tensor("pb%d" % i, [128, 512], F32)) for i in range(8)]
        psb = [Buf("psb%d" % i) for i in range(8)]

        st = {"g32": 0, "g16": 0, "br32": 0, "rd": 0, "ed": 0, "dg": 0, "pt": 0, "sbr": 0, "s16": 0, "ms32": 0, "sq8": 0}

        def t32():
            i = st["g32"] % 4
            st["g32"] += 1
            return g32[:, i, :], g32b[i]

        def b32():
            i = st["br32"] % (NA32 + 1)
            st["br32"] += 1
            if i < NA32:
                return a32[i], a32b[i]
            return g32[:, 3, :], g32b[3]

        def ms32():
            i = st["ms32"] % 3
            st["ms32"] += 1
            return g32[:, i, :], g32b[i]

        def t16():
            i = st["g16"] % 4
            st["g16"] += 1
            return g16[:, i, :], g16b[i]

        def ring(name, n):
            i = st[name] % n
            st[name] += 1
            return i

        def act(out, in_, func, reads, writes, bias=None, scale=None):
            kw = {}
            if bias is not None:
                kw["bias"] = bias
            if scale is not None:
                kw["scale"] = scale
            return S.op("act", lambda e: e.activation(out=out, in_=in_, func=func, **kw), reads, writes)

        def tt(eng, out, in0, in1, op, reads, writes):
            return S.op(eng, lambda e: e.tensor_tensor(out=out, in0=in0, in1=in1, op=op), reads, writes)

        def ts(eng, out, in0, s1, s2, op0, op1, reads, writes):
            if s2 is None:
                return S.op(eng, lambda e: e.tensor_scalar(out=out, in0=in0, scalar1=s1, scalar2=None, op0=op0), reads, writes)
            return S.op(eng, lambda e: e.tensor_scalar(out=out, in0=in0, scalar1=s1, scalar2=s2, op0=op0, op1=op1), reads, writes)

        def stt(eng, out, in0, scalar, in1, op0, op1, reads, writes):
            return S.op(eng, lambda e: e.scalar_tensor_tensor(out=out, in0=in0, scalar=scalar, in1=in1, op0=op0, op1=op1), reads, writes)

        def cp(eng, out, in_, reads, writes):
            return S.op(eng, lambda e: e.tensor_copy(out=out, in_=in_), reads, writes)

        def mm(specs, reads, writes):
            fns = []
            for sp_ in specs:
                o, l, r, a, z = sp_[:5]
                if len(sp_) > 5 and sp_[5]:
                    fns.append(lambda e, o=o, l=l, r=r, a=a, z=z: e.matmul(o, l, r, start=a, stop=z, skip_group_check=True))
                else:
                    fns.append(lambda e, o=o, l=l, r=r, a=a, z=z: e.matmul(o, l, r, start=a, stop=z))
            return S.op("pe", fns, reads, writes)

        def vcol(c, rows=None):
            if rows is None:
                return vec[:, c:c + 1]
            return vec[rows, c:c + 1]

        ident = cst[:, C_ID:C_ID + 128]
        epsc = sb("epsc", [128, 1], F32)
        epsb = Buf("eps")
        S.op("dve", lambda e: e.memset(epsc[:, :], EPS), writes=[epsb])

        sl = {"issued": 0, "next": 0}
        total_slabs = 24 + nl * (SLABS_PER_LAYER - 24) + (nl - 1) * 24

        def slab_issue(i):
            s = i % NSL
            S.dma("pool", slabs[:, s, :, :], wslab_d[i].rearrange("p (k c) -> p k c", k=8), reads=(), writes=[slb[s]])

        def get_slab():
            i = sl["next"]
            sl["next"] += 1
            while sl["issued"] < min(total_slabs, i + NSL):
                slab_issue(sl["issued"])
                sl["issued"] += 1
            s = i % NSL
            return slabs[:, s, :, :], slb[s]

        S.dma("sp", vec[:, :], vecs_d[:, :], writes=[vecb])
        S.dma("sp", ccs[:, :], cc_d[:, :], writes=[ccb])
        S.dma("pool", cst[:, :], cst_d[:, :], writes=[cstb])
        for k in range(8):
            S.dma("sp", x[:, k, :], xT_d[k * 128:(k + 1) * 128, :], writes=xb[k])
        S.dma("sp", rope[:, :], rope_d[:, :], writes=[ropeb])
        S.dma("pool", cpw[:, :], cpw_d[:, :], writes=[cpwb])
        S.dma("pool", plw[:, :], plw_d[:, :], writes=[plwb])
        S.op("dve", lambda e: e.memset(uA[:, :, :], 0.0), writes=uAb)
        S.op("pool", lambda e: e.memset(uC[:, :, :], 0.0), writes=uCb)
        S.op("dve", lambda e: e.memset(Vaug[:, :, :, 64:128], 1.0), writes=vb)
        S.op("dve", lambda e: e.memset(cvaug[:, :, :, 64:128], 1.0), writes=cvb)
        act(csil[:, :, :], ccs[:, :].rearrange("p (k t) -> p k t", t=2), AF.Silu, [ccb], [csilb])
        act(esink[:, :], vec[:, V_SINK:V_SINK + 32], AF.Exp, [vecb], [esinkb])

        def tile_cols(t):
            return t * TW, (t + 1) * TW

        def sq_tile():
            i = st["sq8"] % 8
            st["sq8"] += 1
            if i < 4:
                return g16[:, i, :], g16b[i]
            return s16[:, i - 4, :], s16b[i - 4]

        def stats_act(t, k):
            c0, c1 = tile_cols(t)
            sq, sqb = sq_tile()
            act(sq, x[:, k, c0:c1], AF.Square, [xb[k][t]], [sqb])
            return sq, sqb

        def stats_mm(t, sq, sqb, first, last):
            mm([(pbank[5 + t][:, :], cst[:, C_O1024:C_O1024 + 128], sq, first, last)], [cstb, sqb], [psb[5 + t]])

        def stats_sq(t, k, first, last):
            sq, sqb = stats_act(t, k)
            stats_mm(t, sq, sqb, first, last)

        def stats_fin(t, part=None):
            if part in (None, 0):
                act(Rt3[:, t, :], pbank[5 + t][:, :], AF.Ln, [psb[5 + t]], [Rb3[t]], bias=EPS)
            if part in (None, 1):
                act(Rt3[:, t, :], Rt3[:, t, :], AF.Exp, [Rb3[t]], [Rb3[t]], scale=-0.5)

        def ada_slab(l, c, ab=7):
            sap, sbuf_ = get_slab()
            mm([(pbank[ab][:, 2 * c:2 * c + 2], sap[:, k, :], csil[:, k, :], k == 0, k == 7) for k in range(8)],
               [sbuf_, csilb], [psb[ab]])

        def ada_finish(l, ab=7):
            vbase = l * LV
            modT = modT2[:, l % 2]
            modb = modb2[l % 2]
            gmod = gmod2[:, l % 2]
            gmodb = gmodb2[l % 2]
            tt("dve", modT, pbank[ab][:, 0:48].rearrange("p (c t) -> p c t", t=2),
               vec[:, vbase + V_BADA:vbase + V_BADA + 24].unsqueeze(2).to_broadcast([128, 24, 2]), ALU.add,
               [psb[ab], vecb], [modb])
            ts("dve", gmod, modT[:, 8:16, :], 1.0, None, ALU.add, None, [modb], [gmodb])
            tt("dve", gmod, gmod,
               vec[:, vbase + V_NORMW:vbase + V_NORMW + 8].unsqueeze(2).to_broadcast([128, 8, 2]), ALU.mult,
               [gmodb, vecb], [gmodb])

        def layer(l):
            vbase = l * LV
            modT = modT2[:, l % 2]
            modb = modb2[l % 2]
            gmod = gmod2[:, l % 2]
            gmodb = gmodb2[l % 2]
            if stop == 'ada':
                return
            transfer(alias_all, h_all)
            for part_ in range(2):
                for t in range(3):
                    stats_fin(t, part_)
            cnt_n = 0
            for k in range(8):
                for t in range(3):
                    c0, c1 = tile_cols(t)
                    col = 0 if t == 0 else 1
                    xr, xrb = t32()
                    eng_ = "pool" if (k + t) % 3 == 2 else "dve"
                    cnt_n += 1
                    tt(eng_, xr, x[:, k, c0:c1], Rt3[:, t, :], ALU.mult, [xb[k][t], Rb3[t]], [xrb])
                    if k % 2 == 0:
                        act(hap(k, c0, c1), xr, AF.Identity, [xrb, gmodb, modb], [hb[k][t]],
                            bias=modT[:, k, col:col + 1], scale=gmod[:, k, col:col + 1])
                    else:
                        ts("dve", hap(k, c0, c1), xr, gmod[:, k, col:col + 1], modT[:, k, col:col + 1], ALU.mult, ALU.add,
                           [xrb, gmodb, modb], [hb[k][t]])

            if stop == 'norm':
                return
            for si, (kind, idx) in enumerate(INPROJ):
                if stop is not None and stop.startswith('inproj:') and si >= int(stop.split(':')[1]):
                    return
                sap, sbuf_ = get_slab()
                base = 0 if si % 2 == 0 else 3
                if kind == "ktok":
                    bank, bb = pbank[base], psb[base]
                    mm([(bank[:, tb * 128:(tb + 1) * 128], hap(k, tb * 128, (tb + 1) * 128), sap[:, k, :], k == 0, k == 7)
                        for tb in range(4) for k in range(8)],
                       [sbuf_] + [hb[k][0] for k in range(8)], [bb])
                    ko, kob = t32()
                    cp("dve", ko, bank[:, :], [bb], [kob])
                    for s_ in range(2):
                        S.dma("sp", nk_d[s_, l].rearrange("(r p) c -> p r c", p=128),
                              ko[:, s_ * 256:(s_ + 1) * 256].rearrange("p (r c) -> p r c", r=2), reads=[kob], is_output=True)
                    continue
                if kind == "vtok":
                    for g in range(3):
                        bank, bb = pbank[base + g], psb[base + g]
                        mm([(bank[:, tb * 128:(tb + 1) * 128], hap(k, (g * 4 + tb) * 128, (g * 4 + tb + 1) * 128), sap[:, k, :], k == 0, k == 7)
                            for tb in range(4) for k in range(8)],
                           [sbuf_] + [hb[k][g] for k in range(8)], [bb])
                        if g == 0:
                            vo, vob = t32()
                            cp("dve", vo, bank[:, :], [bb], [vob])
                            for s_ in range(2):
                                S.dma("sp", nv_d[s_, l].rearrange("(r p) c -> p r c", p=128),
                                      vo[:, s_ * 256:(s_ + 1) * 256].rearrange("p (r c) -> p r c", r=2), reads=[vob], is_output=True)
                        for kv in range(2):
                            if os.environ.get("VT_SKIP") == "act":
                                continue
                            cp("dve", Vaug[:, g * 4:(g + 1) * 4, kv, 0:64],
                               bank[:, :].rearrange("p (b v d) -> p b v d", b=4, v=2)[:, :, kv, :], [bb], [vb[g]])
                    continue
                banks = [pbank[base + t] for t in range(3)]
                bbs = [psb[base + t] for t in range(3)]
                if si == 0:
                    for k in range(8):
                        mm([(banks[t][:, :], sap[:, k, :], hap(k, t * TW, (t + 1) * TW), k == 0, k == 7) for t in range(3)],
                           [sbuf_] + hb[k], bbs)
                else:
                    mm([(banks[t][:, :], sap[:, k, :], hap(k, t * TW, (t + 1) * TW), k == 0, k == 7)
                        for k in range(8) for t in range(3)],
                       [sbuf_] + h_all, bbs)
                for t in range(3):
                    c0, c1 = tile_cols(t)
                    ps = banks[t][:, :]
                    pb_ = bbs[t]
                    if kind in ("glu", "val"):
                        if t == 0:
                            dst = uA[:, idx, PA:PA + 2 * 286].rearrange("p (s w) -> p s w", s=2)[:, :, 0:256]
                            src = ps.rearrange("p (s w) -> p s w", s=2)
                        else:
                            dst = uA[:, idx, UA_BASE[2] + (t - 1) * TW:UA_BASE[2] + t * TW]
                            src = ps
                        if kind == "glu":
                            act(dst, src, AF.Sigmoid, [pb_], [uAb[idx]])
                        else:
                            tt("dve", dst, src, dst, ALU.mult, [pb_, uAb[idx]], [uAb[idx]])
                    elif kind == "agate":
                        act(m[:, idx, c0:c1], ps, AF.Silu, [pb_], [mb[idx][t]])
                    elif kind == "bgate":
                        act(m[:, 2 + idx, c0:c1], ps, AF.Silu, [pb_], [mb[2 + idx][t]])
                    elif kind == "cgate":
                        act(m[:, 6 + idx, c0:c1], ps, AF.Silu, [pb_], [mb[6 + idx][t]])
                    elif kind == "cin":
                        if t == 0:
                            dst = uC[:, idx, PC:PC + 2 * 272].rearrange("p (s w) -> p s w", s=2)[:, :, 0:256]
                            src = ps.rearrange("p (s w) -> p s w", s=2)
                        else:
                            dst = uC[:, idx, UC_BASE[2] + (t - 1) * TW:UC_BASE[2] + t * TW]
                            src = ps
                        cp("dve", dst, src, [pb_], [uCb[idx]])
                    elif kind in ("q", "kd"):
                        if kind == "q":
                            dst, dstb = QT[:, idx, c0:c1], qb_[idx][t]
                        else:
                            dst, dstb = KT[:, idx, c0:c1], kb_[idx][t]
                        if t == 0:
                            cp("dve", dst, ps, [pb_], [dstb])
                        else:
                            r0 = (t - 1) * TW
                            qraw, qrb = t16()
                            act(qraw, ps, AF.Copy, [pb_], [qrb])
                            t1, t1b = t32()
                            tt("dve", t1, ps, rope[:, r0:r0 + TW], ALU.mult, [pb_, ropeb], [t1b])
                            rb_i = 6 + (t % 2)
                            mm([(pbank[rb_i][:, :], cst[:, C_RT:C_RT + 128], qraw, True, True)], [cstb, qrb], [psb[rb_i]])
                            t2, t2b = t32()
                            tt("dve", t2, pbank[rb_i][:, :], rope[:, 1024 + r0:1024 + r0 + TW], ALU.mult, [psb[rb_i], ropeb], [t2b])
                            tt("pool", dst, t1, t2, ALU.add, [t1b, t2b], [dstb])

            if stop == 'inproj':
                return
            transfer(h_all, alias_all)

            for kv in range(2):
                for dup in range(2):
                    S.dma("pool", cksrc[:, :, kv, dup, :], ck_d[l][:, kv * 64:(kv + 1) * 64].rearrange("(ct p) d -> p ct d", p=128),
                          writes=[cksrcb[kv * 2 + dup]])
                S.dma("pool", cvaug[:, :, kv, 0:64], cv_d[l][:, kv * 64:(kv + 1) * 64].rearrange("(ct p) d -> p ct d", p=128),
                      writes=[cvb[kv]])

            if stop == 'ctxdma':
                return
            thunks = []

            def lvl(eng, dst, src, sh_lo, sh_hi, lo, hi, rows, reads, writes):
                tt(eng, dst[rows, lo:hi], src[rows, lo + sh_lo:hi + sh_lo], src[rows, lo + sh_hi:hi + sh_hi], ALU.add, reads, writes)

            lo_r, hi_r = slice(0, 64), slice(64, 128)

            def pool_tile(j, t):
                u = uC[:, j, :]
                ub = uCb[j]
                tabc = V_PTAB + j * 17
                c0, c1 = tile_cols(t)
                d16, d16b = t16()
                if t == 0:
                    dst = d16.rearrange("p (s w) -> p s w", s=2)
                    sS = poolB[:, PC:PC + 2 * 272].rearrange("p (s w) -> p s w", s=2)[:, :, 0:256]
                    sU = uC[:, j, PC:PC + 2 * 272].rearrange("p (s w) -> p s w", s=2)[:, :, 0:256]
                else:
                    o = UC_BASE[2] + (t - 1) * TW
                    dst, sS, sU = d16, poolB[:, o:o + TW], uC[:, j, o:o + TW]
                stt("dve", dst, sS, vcol(tabc), sU, ALU.mult, ALU.subtract, [pBb, ub, vecb], [d16b])
                fixes = []
                if t == 0:
                    for s_ in range(2):
                        fixes.append((s_ * 256, UC_BASE[s_], tabc + 1))
                        fixes.append((s_ * 256 + 248, UC_BASE[s_] + 248, tabc + 9))
                elif t == 1:
                    fixes.append((0, UC_BASE[2], tabc + 1))
                else:
                    fixes.append((504, UC_BASE[2] + 1016, tabc + 9))
                for (dc, pc_, tc) in fixes:
                    ei = ring("ed", 4)
                    tt("dve", edg[:, ei, :], poolB[:, pc_:pc_ + 8], vec[:, tc:tc + 8], ALU.mult, [pBb, vecb], [edb[ei]])
                    tt("dve", d16[:, dc:dc + 8], edg[:, ei, :], uC[:, j, pc_:pc_ + 8], ALU.subtract, [edb[ei], ub, d16b], [d16b])
                ob = ring("sbr", 3)
                mm([(pbank[ob][:, :], plw[:, (l * 2 + j) * 128:(l * 2 + j + 1) * 128], d16, True, True)], [plwb, d16b], [psb[ob]])
                stt("dve", m[:, 6 + j, c0:c1], pbank[ob][:, :], vcol(vbase + V_PSCALE + j), m[:, 6 + j, c0:c1],
                    ALU.mult, ALU.mult, [psb[ob], vecb, mb[6 + j][t]], [mb[6 + j][t]])

            for j in range(2):
                u = uC[:, j, :]
                ub = uCb[j]
                L_ = lambda *a: thunks.append(lambda a=a: lvl("pool", *a))
                if j == 0:
                    L_(poolB, u, -1, 0, 1, WC, lo_r, [ub], [pBb])
                    L_(poolA, u, -1, 0, 1, WC, hi_r, [ub], [pAb])
                    L_(poolB, poolA, -1, 1, 2, WC - 1, hi_r, [pAb], [pBb])
                else:
                    L_(poolB, u, -1, 0, 1, WC, lo_r, [ub], [pBb])
                    L_(poolA, u, -1, 0, 1, WC, hi_r, [ub], [pAb])
                    L_(poolA, poolB, -1, 1, 2, WC - 1, lo_r, [pBb], [pAb])
                    L_(poolB, poolA, -1, 1, 2, WC - 1, hi_r, [pAb], [pBb])
                    L_(poolB, poolA, -2, 2, 4, WC - 3, lo_r, [pAb], [pBb])
                    L_(poolA, poolB, -2, 2, 4, WC - 3, hi_r, [pBb], [pAb])
                    L_(poolB, poolA, -4, 4, 8, WC - 7, hi_r, [pAb], [pBb])
                for t in range(3):
                    thunks.append(lambda j=j, t=t: pool_tile(j, t))

            work = []
            lev0, til0, lev1, til1 = thunks[0:3], thunks[3:6], thunks[6:13], thunks[13:16]
            thunks = []
            for f_ in lev0:
                f_()
            CB = (5, 6)

            conv_list = [(t, j, k) for t in range(3) for j in range(2) for k in range(31)]
            conv_dg = {}
            DG_AHEAD = 6

            def build_diag(n):
                if n >= len(conv_list):
                    return
                t, j, k = conv_list[n]
                di = ring("dg", NDG)
                ts("dve", diag[:, di, :], ident, vcol(vbase + V_DW + k * 2 + j), None, ALU.mult, None,
                   [cstb, vecb], [dgb[di]])
                conv_dg[n] = di

            def conv_item(n):
                t, j, k = conv_list[n]
                di = conv_dg.pop(n)
                bank, bb = pbank[CB[j]], psb[CB[j]]
                if t == 0:
                    spec = (bank[:, :].rearrange("p (s w) -> p s w", s=2), diag[:, di, :],
                            uA[:, j, k:k + 2 * 286].rearrange("p (s w) -> p s w", s=2)[:, :, 0:256], k == 0, k == 30)
                else:
                    o = UA_BASE[2] - PA + (t - 1) * TW + k
                    spec = (bank[:, :], diag[:, di, :], uA[:, j, o:o + TW], k == 0, k == 30)
                mm([spec], [dgb[di], uAb[j]], [bb])
                build_diag(n + DG_AHEAD)

            for n_ in range(DG_AHEAD):
                build_diag(n_)

            LN = {}

            def ln_A1(t):
                y32 = []
                for j in range(2):
                    bank, bb = pbank[CB[j]], psb[CB[j]]
                    y, yb = b32()
                    act(y, bank[:, :], AF.Identity, [bb, vecb], [yb], bias=vcol(vbase + V_CONVB + j))
                    y32.append((y, yb))
                LN[t] = {"y": y32}

            def ln_A2(t):
                aux = []
                for j in range(2):
                    y, yb = LN[t]["y"][j]
                    ysq, ysqb = t16()
                    act(ysq, y, AF.Square, [yb], [ysqb])
                    y16, y16b = t16()
                    act(y16, y, AF.Copy, [yb], [y16b])
                    aux.append((ysq, ysqb, y16, y16b))
                LN[t]["aux"] = aux

            def ln_A3(t):
                r6, r7 = ring("sbr", 3), ring("sbr", 3)
                for j in range(2):
                    ysq, ysqb, y16, y16b = LN[t]["aux"][j]
                    mm([(pbank[r6][:, :], cst[:, C_O256:C_O256 + 128], y16, j == 0, j == 1)], [cstb, y16b], [psb[r6]])
                    mm([(pbank[r7][:, :], cst[:, C_O256:C_O256 + 128], ysq, j == 0, j == 1)], [cstb, ysqb], [psb[r7]])
                mean, meanb = ms32()
                act(mean, pbank[r6][:, :], AF.Copy, [psb[r6]], [meanb])
                msq, msqb = ms32()
                act(msq, pbank[r6][:, :], AF.Square, [psb[r6]], [msqb])
                tt("dve", msq, pbank[r7][:, :], msq, ALU.subtract, [psb[r7], msqb], [msqb])
                act(msq, msq, AF.Ln, [msqb, epsb], [msqb], bias=epsc[:, 0:1])
                act(msq, msq, AF.Exp, [msqb], [msqb], scale=-0.5)
                for j in range(2):
                    y, yb = LN[t]["y"][j]
                    tt("dve", y, y, mean, ALU.subtract, [yb, meanb], [yb])
                    tt("dve", y, y, msq, ALU.mult, [yb, msqb], [yb])

            def ln_C(t):
                sj = []
                for j in range(2):
                    y, yb = LN[t]["y"][j]
                    si_ = ring("s16", 4)
                    act(s16[:, si_, :], y, AF.Silu, [yb, vecb], [s16b[si_]], bias=vcol(vbase + V_LNB + j), scale=vcol(vbase + V_LNG + j))
                    sj.append((s16[:, si_, :], s16b[si_]))
                LN[t]["s"] = sj

            def ln_D(t):
                c0, c1 = tile_cols(t)
                sj = LN[t]["s"]
                for jo in range(2):
                    ob = ring("sbr", 3)
                    mm([(pbank[ob][:, :], cpw[:, (l * 2 + ji) * 256 + jo * 128:(l * 2 + ji) * 256 + jo * 128 + 128], sj[ji][0], ji == 0, ji == 1)
                        for ji in range(2)], [cpwb, sj[0][1], sj[1][1]], [psb[ob]])
                    tt("dve", m[:, jo, c0:c1], pbank[ob][:, :], m[:, jo, c0:c1], ALU.mult, [psb[ob], mb[jo][t]], [mb[jo][t]])

            lnq = []
            for t in range(3):
                for j in range(2):
                    for k in range(31):
                        work.append(lambda n=(t * 2 + j) * 31 + k: conv_item(n))
                        if lnq and (j * 31 + k) % 3 == 2:
                            work.append(lnq.pop(0))
                lnq = [lambda t=t: ln_A1(t), lambda t=t: ln_A2(t), lambda t=t: ln_A3(t), lambda t=t: ln_C(t), lambda t=t: ln_D(t)]
                work.append(lnq.pop(0))
            work.extend(lnq)
            work[40:40] = til0 + lev1
            work.extend(til1)
            ln_thunks = []

            if stop == 'pool':
                return
            for kv in range(2):
                bi = 3 + kv
                mm([(pbank[bi][:, ct * 128:(ct + 1) * 128], cksrc[:, ct, kv, :, :].rearrange("p a d -> p (a d)"), ident, True, True) for ct in range(4)],
                   [cksrcb[kv * 2], cksrcb[kv * 2 + 1], cstb], [psb[bi]])
                cp("dve", ckT[:, kv, :], pbank[bi][:, :], [psb[bi]], [ckTb[kv]])

            rws = [slice(0, 64), slice(64, 128)]

            def unit_tiles(unit):
                tiles = []
                if unit[0] == "p":
                    _, s_, hd = unit
                    c, half, kvh = hd // 2, hd % 2, hd // 4
                    rows = rws[half]
                    tiles.append(dict(
                        smm=[(KT[rows, kvh, (s_ * 2 + kt) * 128:(s_ * 2 + kt + 1) * 128], QT[rows, c, s_ * 256:(s_ + 1) * 256], kt * 256, 256)
                             for kt in range(2)],
                        ncols=512, masks=[], srd=[kb_[kvh][0], qb_[c][0]],
                        pv=[(Vaug[:, s_ * 2 + kt, kvh, :], kt * 256, 256, 0, 0) for kt in range(2)], prd=[vb[0]]))
                    return tiles
                _, hd = unit
                c, half, kvh = hd // 2, hd % 2, hd // 4
                rows = rws[half]
                loc, ctx = [], []
                for j in range(8):
                    lo, hi = max(0, j - 1), min(8, j + 2)
                    n = (hi - lo) * 128
                    masks = []
                    for qbk in range(lo, hi):
                        if qbk == j - 1:
                            masks.append(((qbk - lo) * 128, C_MNEXT))
                        elif qbk == j + 1:
                            masks.append(((qbk - lo) * 128, C_MPREV))
                    pv = []
                    for b_ in range(2):
                        a0, a1 = max(lo * 128, b_ * 512), min(hi * 128, (b_ + 1) * 512)
                        if a1 > a0:
                            pv.append((Vaug[:, 4 + j, kvh, :], a0 - lo * 128, a1 - a0, b_, a0 - b_ * 512))
                    loc.append(dict(
                        smm=[(KT[rows, kvh, 512 + j * 128:512 + (j + 1) * 128], QT[rows, c, 512 + lo * 128:512 + hi * 128], 0, n)],
                        ncols=n, masks=masks, srd=[kb_[kvh][1 + j // 4], qb_[c][1], qb_[c][2]],
                        pv=pv, prd=[vb[1 + j // 4]]))
                for qh in range(2):
                    for ct in range(4):
                        ctx.append(dict(
                            smm=[(ckT[rows, kvh, ct * 128:(ct + 1) * 128], QT[rows, c, 512 + qh * 512:512 + (qh + 1) * 512], 0, 512)],
                            ncols=512, masks=[], srd=[ckTb[kvh], qb_[c][1 + qh]],
                            pv=[(cvaug[:, ct, kvh, :], 0, 512, qh, 0)], prd=[cvb[kvh]]))
                for i in range(8):
                    tiles.append(ctx[i])
                    tiles.append(loc[i])
                return tiles

            units = [("p", s_, hd) for s_ in range(2) for hd in range(8)] + [("s", hd) for hd in range(8)]
            flat = []
            for ui, unit in enumerate(units):
                tl = unit_tiles(unit)
                for ti, tile in enumerate(tl):
                    flat.append((ui, unit, tile, ti == len(tl) - 1))
            ostart = {}
            unit_first = {}
            for (ui_, unit_, tile_, last_) in flat:
                unit_first.setdefault(ui_, tile_)

            def emit_S(idx, tile):
                sbi = ring("sbr", 3)
                fill = []
                mm(fill + [(pbank[sbi][:, oc:oc + n], kt_, q_, True, True) for (kt_, q_, oc, n) in tile["smm"]],
                   tile["srd"] + [cstb, mb[0][0]], [psb[sbi]])
                pti = ring("pt", 4)
                tile["pti"] = pti
                nco = tile["ncols"]
                act(PT[:, pti, 0:nco], pbank[sbi][:, 0:nco], AF.Exp, [psb[sbi]], [PTb[pti]], scale=0.125)
                for (c0_, mk) in tile["masks"]:
                    tt("dve", PT[:, pti, c0_:c0_ + 128], PT[:, pti, c0_:c0_ + 128], cst[:, mk:mk + 128], ALU.mult,
                       [PTb[pti], cstb], [PTb[pti]])

            def norm_piece(Ob, Obb, n, hd, mcol0, qt):
                c, half = hd // 2, hd % 2
                ro = rws[half]
                rd, rdb_ = ms32()
                tmp, tmpb = ms32()
                act(rd[64:128, 0:n], Ob[64:128, 0:n], AF.Ln, [Obb, esinkb], [rdb_], bias=esink[64:128, l * 8 + hd:l * 8 + hd + 1])
                act(rd[64:128, 0:n], rd[64:128, 0:n], AF.Exp, [rdb_], [rdb_], scale=-1.0)
                tt("dve", tmp[ro, 0:n], Ob[0:64, 0:n], rd[64:128, 0:n], ALU.mult, [Obb, rdb_], [tmpb])
                tt("dve", m[ro, 2 + c, mcol0:mcol0 + n], tmp[ro, 0:n], m[ro, 2 + c, mcol0:mcol0 + n], ALU.mult,
                   [tmpb, mb[2 + c][qt]], [mb[2 + c][qt]])

            def emit_PV(idx, ui, unit, tile, last):
                pti = tile["pti"]
                oset = (3, 4) if (unit[0] == 's' or ui % 2 == 0) else (4, 3)
                for (va, pc0, n, ob_, oc0) in tile["pv"]:
                    bi = oset[ob_]
                    first = (ui, ob_) not in ostart
                    ostart[(ui, ob_)] = True
                    mm([(pbank[bi][:, oc0:oc0 + n], va, PT[:, pti, pc0:pc0 + n], first, True, True)],
                       [PTb[pti]] + tile["prd"], [psb[bi]])
                if last:
                    if unit[0] == "p":
                        _, s_, hd = unit
                        norm_piece(pbank[oset[0]], psb[oset[0]], 256, hd, s_ * 256, 0)
                    else:
                        _, hd = unit
                        for b_ in range(2):
                            norm_piece(pbank[oset[b_]], psb[oset[b_]], 512, hd, 512 + b_ * 512, 1 + b_)
                    if l + 1 < nl:
                        ada_slab(l + 1, ui)
                    if ln_thunks:
                        ln_thunks.pop(0)()

            pend = []
            wacc = [0.0]
            wrate = (len(work) + 4.0) / max(1, len(flat) - 8)
            for idx, (ui, unit, tile, last) in enumerate(flat):
                emit_S(idx, tile)
                wacc[0] += wrate
                while wacc[0] >= 1.0 and work:
                    work.pop(0)()
                    wacc[0] -= 1.0
                pend.append((idx, ui, unit, tile, last))
                if len(pend) > 2:
                    emit_PV(*pend.pop(0))
            while pend:
                emit_PV(*pend.pop(0))
            while work:
                work.pop(0)()

            if l + 1 < nl:
                ada_finish(l + 1)

            if stop == 'attn':
                return
            m_all = [b for row in mb for b in row]
            pend_stats = []
            for mo in range(8):
                sap, sbuf_ = get_slab()
                bidx = [(3 * mo + t) % 5 for t in range(3)]
                banks = [pbank[i] for i in bidx]
                bbs = [psb[i] for i in bidx]
                mm([(banks[t][:, :], sap[:, k, :], m[:, k, t * TW:(t + 1) * TW], k == 0, k == 7)
                    for k in range(8) for t in range(3)], [sbuf_] + m_all, bbs)
                for (t_, sq_, sqb_, mo_) in pend_stats:
                    stats_mm(t_, sq_, sqb_, mo_ == 0, mo_ == 7)
                pend_stats = []
                for t in range(3):
                    c0, c1 = tile_cols(t)
                    col = 0 if t == 0 else 1
                    stt("dve", x[:, mo, c0:c1], banks[t][:, :], modT[:, 16 + mo, col:col + 1], x[:, mo, c0:c1],
                        ALU.mult, ALU.add, [bbs[t], modb, xb[mo][t]], [xb[mo][t]])
                    sq_, sqb_ = stats_act(t, mo)
                    pend_stats.append((t, sq_, sqb_, mo))
            for (t_, sq_, sqb_, mo_) in pend_stats:
                stats_mm(t_, sq_, sqb_, mo_ == 0, mo_ == 7)

        for t in range(3):
            for k in range(8):
                stats_sq(t, k, k == 0, k == 7)
        for c in range(24):
            ada_slab(0, c, 0)
        ada_finish(0, 0)
        for l in range(nl):
            layer(l)

        for part_ in range(2):
            for t in range(3):
                stats_fin(t, part_)
        for t in range(3):
            c0, c1 = tile_cols(t)
            for k in range(8):
                yo, yob = t32()
                stt("dve", yo, x[:, k, c0:c1], vcol(V_FNW + k), Rt3[:, t, :], ALU.mult, ALU.mult, [xb[k][t], Rb3[t], vecb], [yob])
                S.dma("sp", yT_d[k * 128:(k + 1) * 128, c0:c1], yo, reads=[yob], is_output=True)
        S.finish()
    return nc


def _const_tables():
    cst = np.zeros((128, NCST), np.float32)
    cst[:, C_ID:C_ID + 128] = np.eye(128, dtype=np.float32)
    R = np.zeros((64, 64), np.float32)
    for d in range(16):
        R[d, d + 16] = -1.0
        R[d + 16, d] = 1.0
        R[32 + d, 48 + d] = -1.0
        R[48 + d, 32 + d] = 1.0
    Rt = np.zeros((128, 128), np.float32)
    Rt[0:64, 0:64] = R.T
    Rt[64:128, 64:128] = R.T
    cst[:, C_RT:C_RT + 128] = Rt
    cst[:, C_O1024:C_O1024 + 128] = 1.0 / 1024.0
    cst[:, C_O256:C_O256 + 128] = 1.0 / 256.0
    jk = np.arange(128)[:, None]
    r = np.arange(128)[None, :]
    cst[:, C_MNEXT:C_MNEXT + 128] = (jk <= r).astype(np.float32)
    cst[:, C_MPREV:C_MPREV + 128] = (jk >= r).astype(np.float32)
    quarter = 16
    freqs = 10000.0 ** (-np.arange(quarter, dtype=np.float64) / quarter)
    tpos = np.arange(1024)
    rows = (tpos // 64).astype(np.float64)
    cols = (tpos % 64).astype(np.float64)
    rope = np.zeros((128, 2048), np.float32)
    for p in range(128):
        d = p % 64
        pos = rows if d < 32 else cols
        ang = pos * freqs[d % 16]
        rope[p, 0:1024] = np.cos(ang).astype(np.float32)
        rope[p, 1024:2048] = np.sin(ang).astype(np.float32)
    ptab = np.zeros((128, 34), np.float32)
    for j in range(2):
        for p in range(128):
            w = POOL_WINDOWS[2 * j + p // 64]
            ptab[p, j * 17] = 1.0 / w
            for t in range(8):
                cnt = t + w // 2 if t < w // 2 else w
                ptab[p, j * 17 + 1 + t] = 1.0 / cnt
            for c in range(8):
                rr = 7 - c
                cnt = rr + 1 + w // 2 if w // 2 >= rr + 1 else w
                ptab[p, j * 17 + 9 + c] = 1.0 / cnt
    return cst, rope, ptab


def _colvec(v, n):
    return np.ascontiguousarray(np.asarray(v, np.float32).reshape(n, 128).T)


_PROG = {}


def kernel(x_prompt, x_sample, c, cache_k, cache_v, c_ctx, w_ada, b_ada, norm_w, w_in,
           conv_dw, conv_b, conv_ln_g, conv_ln_b, conv_pw, attn_sink, pool_w, pool_scale,
           w_out, final_norm_w, _nl=NL, _stop=None):
    f = lambda a: np.asarray(a, np.float32)
    x_prompt, x_sample, c, cache_k, cache_v, c_ctx = map(f, (x_prompt, x_sample, c, cache_k, cache_v, c_ctx))
    w_ada, b_ada, norm_w, w_in, conv_dw, conv_b = map(f, (w_ada, b_ada, norm_w, w_in, conv_dw, conv_b))
    conv_ln_g, conv_ln_b, conv_pw, attn_sink, pool_w, pool_scale = map(f, (conv_ln_g, conv_ln_b, conv_pw, attn_sink, pool_w, pool_scale))
    w_out, final_norm_w = f(w_out), f(final_norm_w)
    nl = _nl
    cst, rope, ptab = _const_tables()

    wslab = np.empty((NL * SLABS_PER_LAYER, 128, 1024), np.float32)

    def put(i, W, cols):
        wslab[i] = W[:, cols].reshape(8, 128, 128).transpose(1, 0, 2).reshape(128, 1024)

    i = 0
    for cc_ in range(24):
        put(i, w_ada[0], np.arange(cc_ * 128, cc_ * 128 + 128))
        i += 1
    for l in range(NL):
        for kind, idx in INPROJ:
            put(i, w_in[l], _slab_cols(kind, idx))
            i += 1
        if l + 1 < NL:
            for cc_ in range(24):
                put(i, w_ada[l + 1], np.arange(cc_ * 128, cc_ * 128 + 128))
                i += 1
        for mo in range(8):
            put(i, w_out[l], np.arange(mo * 128, mo * 128 + 128))
            i += 1

    vecs_base = np.zeros((128, NV), np.float32)
    for l in range(NL):
        b = l * LV
        vecs_base[:, b + V_NORMW:b + V_NORMW + 8] = _colvec(norm_w[l], 8)
        vecs_base[:, b + V_BADA:b + V_BADA + 24] = _colvec(b_ada[l], 24)
        vecs_base[:, b + V_CONVB:b + V_CONVB + 2] = _colvec(conv_b[l], 2)
        vecs_base[:, b + V_LNG:b + V_LNG + 2] = _colvec(conv_ln_g[l], 2)
        vecs_base[:, b + V_LNB:b + V_LNB + 2] = _colvec(conv_ln_b[l], 2)
        vecs_base[:, b + V_PSCALE:b + V_PSCALE + 2] = _colvec(pool_scale[l], 2)
        for k in range(31):
            vecs_base[:, b + V_DW + k * 2:b + V_DW + k * 2 + 2] = _colvec(conv_dw[l, k], 2)
    vecs_base[:, V_FNW:V_FNW + 8] = _colvec(final_norm_w, 8)
    vecs_base[:, V_SINK:V_SINK + 32] = np.broadcast_to(attn_sink.reshape(1, 32), (128, 32))
    vecs_base[:, V_PTAB:V_PTAB + 34] = ptab

    cpw = np.zeros((128, NL, 2, 256), np.float32)
    plw = np.zeros((128, NL, 2, 128), np.float32)
    for l in range(NL):
        for ji in range(2):
            cpw[:, l, ji, :] = conv_pw[l, ji * 128:(ji + 1) * 128, :]
        for g in range(4):
            j, hf = g // 2, g % 2
            plw[hf * 64:(hf + 1) * 64, l, j, hf * 64:(hf + 1) * 64] = pool_w[l, g]
    cpw = cpw.reshape(128, -1)
    plw = plw.reshape(128, -1)

    in_maps = []
    for i in range(NCORES):
        xt = np.concatenate([x_prompt[2 * i], x_prompt[2 * i + 1], x_sample[i]], axis=0)
        cc = np.stack([_colvec(c_ctx, 8), _colvec(c[i], 8)], axis=2).reshape(128, 16)
        in_maps.append({
            "xT": np.ascontiguousarray(xt.T),
            "cc": np.ascontiguousarray(cc),
            "vecs": vecs_base,
            "wslab": wslab,
            "cpw": cpw,
            "plw": plw,
            "ck": np.ascontiguousarray(cache_k[i].reshape(NL, 512, 128)),
            "cv": np.ascontiguousarray(cache_v[i].reshape(NL, 512, 128)),
            "rope": rope,
            "cst": cst,
        })
    if (nl, _stop) not in _PROG:
        _PROG[(nl, _stop)] = build_program(nl, _stop)
    res = run_bass_kernel_spmd(_PROG[(nl, _stop)], in_maps, core_ids=list(range(NCORES)))
    y_prompt = np.empty((16, 256, D), np.float32)
    y_sample = np.empty((8, 1024, D), np.float32)
    nk = np.empty((16, NL, 256, 2, 64), np.float32)
    nv = np.empty((16, NL, 256, 2, 64), np.float32)
    for i in range(NCORES):
        r = res.results[i]
        yt = np.asarray(r["yT"]).T
        y_prompt[2 * i] = yt[0:256]
        y_prompt[2 * i + 1] = yt[256:512]
        y_sample[i] = yt[512:1536]
        nk[2 * i:2 * i + 2] = np.asarray(r["nk"]).reshape(2, NL, 256, 2, 64)
        nv[2 * i:2 * i + 2] = np.asarray(r["nv"]).reshape(2, NL, 256, 2, 64)
    return (y_prompt, y_sample, nk, nv)
```

```python
import os
import numpy as np
import concourse.bass as bass
import concourse.mybir as mybir
from concourse.bass_utils import run_bass_kernel_spmd
from contextlib import ExitStack

F32 = mybir.dt.float32
BF16 = mybir.dt.bfloat16
AF = mybir.ActivationFunctionType
ALU = mybir.AluOpType

NCORES = 8
D = 1024
NL = 4
T = 1536
TW = 512
EPS = 1e-6
PA = 15
UA_BASE = [15, 301, 587]
WA = 587 + 1024 + 15
PC = 8
UC_BASE = [8, 280, 552]
WC = 552 + 1024 + 8
POOL_WINDOWS = (2, 4, 8, 16)

LV = 102
V_NORMW, V_BADA, V_CONVB, V_LNG, V_LNB, V_PSCALE, V_DW = 0, 8, 32, 34, 36, 38, 40
V_FNW = NL * LV
V_SINK = V_FNW + 8
V_PTAB = V_SINK + 32
NV = V_PTAB + 34
C_ID, C_RT, C_O1024, C_O256, C_MNEXT, C_MPREV = 0, 128, 256, 384, 512, 640
NCST = 768

INPROJ = ([("glu", 0), ("val", 0), ("glu", 1), ("val", 1), ("agate", 0), ("agate", 1)]
          + [("q", c) for c in range(4)] + [("kd", 0), ("kd", 1), ("ktok", 0), ("vtok", 0)]
          + [("bgate", c) for c in range(4)] + [("cin", 0), ("cin", 1), ("cgate", 0), ("cgate", 1)])
SLABS_PER_LAYER = 24 + len(INPROJ) + 8
NSL = 6
NDUMMY = int(os.environ.get('NDUMMY', '0'))
NBURST = int(os.environ.get('NBURST', '20'))
BURST_AT = [int(v) for v in os.environ.get('BURST_AT', '0').split(',') if v != '']
NDG = 8


def _slab_cols(kind, idx):
    if kind == "glu":
        return np.arange(256 + idx * 128, 256 + idx * 128 + 128)
    if kind == "val":
        return np.arange(idx * 128, idx * 128 + 128)
    if kind == "agate":
        return np.arange(512 + idx * 128, 512 + idx * 128 + 128)
    if kind == "q":
        return np.arange(768 + idx * 128, 768 + idx * 128 + 128)
    if kind == "kd":
        a = np.arange(1280 + idx * 64, 1280 + idx * 64 + 64)
        return np.concatenate([a, a])
    if kind == "ktok":
        return np.arange(1280, 1408)
    if kind == "vtok":
        return np.arange(1408, 1536)
    if kind == "bgate":
        return np.arange(1536 + idx * 128, 1536 + idx * 128 + 128)
    if kind == "cin":
        return np.arange(2048 + idx * 128, 2048 + idx * 128 + 128)
    if kind == "cgate":
        return np.arange(2304 + idx * 128, 2304 + idx * 128 + 128)
    raise ValueError(kind)


class Buf:
    __slots__ = ("name", "w", "r", "dsem", "dcnt")

    def __init__(self, name):
        self.name = name
        self.w = None
        self.r = {}
        self.dsem = None
        self.dcnt = 0


class Sched:
    def __init__(self, nc, es):
        self.nc = nc
        self.es = es
        self.eng = {"pe": nc.tensor, "act": nc.scalar, "dve": nc.vector, "pool": nc.gpsimd, "sp": nc.sync}
        self.sem = {e: es.enter_context(nc.semaphore("s_" + e)) for e in self.eng}
        self.cnt = {e: 0 for e in self.eng}
        self.seen = {e: {} for e in self.eng}
        self.out_events = {}

    def _deps(self, e, reads, writes, skip_own):
        deps = {}
        own = self.sem[e]

        def add(ev):
            if ev is None:
                return
            s, v = ev
            if deps.get(s, 0) < v:
                deps[s] = v

        for b in reads:
            add(b.w)
        for b in writes:
            if b.w is not None and not (skip_own and b.w[0] is own):
                add(b.w)
            for s, v in b.r.items():
                if not (skip_own and s is own):
                    add((s, v))
        return deps

    def _wait(self, e, deps):
        for s, v in deps.items():
            if self.seen[e].get(s, 0) >= v:
                continue
            self.eng[e].wait_ge(s, v)
            self.seen[e][s] = v

    @staticmethod
    def _record(ev, reads, writes):
        for b in reads:
            if b.r.get(ev[0], 0) < ev[1]:
                b.r[ev[0]] = ev[1]
        for b in writes:
            b.w = ev
            b.r = {}

    def op(self, e, fns, reads=(), writes=()):
        if callable(fns):
            fns = [fns]
        self._wait(e, self._deps(e, reads, writes, e == "pe"))
        ins = None
        for f in fns:
            ins = f(self.eng[e])
        ins.then_inc(self.sem[e], 1)
        self.cnt[e] += 1
        ev = (self.sem[e], self.cnt[e])
        self._record(ev, reads, writes)
        return ev

    def dma(self, q, out, in_, reads=(), writes=(), is_output=False):
        self._wait(q, self._deps(q, reads, writes, False))
        b = (list(writes) or list(reads))[0]
        if b.dsem is None:
            b.dsem = self.es.enter_context(self.nc.semaphore("d_" + b.name))
        b.dcnt += 16
        self.eng[q].dma_start(out=out, in_=in_).then_inc(b.dsem, 16)
        ev = (b.dsem, b.dcnt)
        self._record(ev, reads, writes)
        if is_output:
            self.out_events[b.dsem] = max(self.out_events.get(b.dsem, 0), b.dcnt)
        return ev

    def finish(self):
        for s, v in self.out_events.items():
            self.eng["sp"].wait_ge(s, v)


def transfer(src, dst):
    evs = {}
    for b in src:
        if b.w is not None:
            evs[b.w[0]] = max(evs.get(b.w[0], 0), b.w[1])
        for s, v in b.r.items():
            evs[s] = max(evs.get(s, 0), v)
    for b in dst:
        for s, v in evs.items():
            if b.r.get(s, 0) < v:
                b.r[s] = v


def build_program(nl=NL, stop=None):
    nc = bass.Bass("TRN2", target_bir_lowering=False)

    def dram(n, s, kind="ExternalInput"):
        return nc.dram_tensor(n, s, F32, kind=kind).ap()

    xT_d = dram("xT", [D, T])
    cc_d = dram("cc", [128, 16])
    vecs_d = dram("vecs", [128, NV])
    wslab_d = dram("wslab", [NL * SLABS_PER_LAYER, 128, 1024])
    cpw_d = dram("cpw", [128, NL * 2 * 256])
    plw_d = dram("plw", [128, NL * 2 * 128])
    ck_d = dram("ck", [NL, 512, 128])
    cv_d = dram("cv", [NL, 512, 128])
    rope_d = dram("rope", [128, 2048])
    cst_d = dram("cst", [128, NCST])
    yT_d = dram("yT", [D, T], "ExternalOutput")
    nk_d = dram("nk", [2, NL, 256, 128], "ExternalOutput")
    nv_d = dram("nv", [2, NL, 256, 128], "ExternalOutput")

    with ExitStack() as es:
        S = Sched(nc, es)

        def sb(n, s, d):
            return es.enter_context(nc.sbuf_tensor("sb_" + n, s, d))

        x = sb("x", [128, 8, T], F32)
        xb = [[Buf("x%d_%d" % (k, t)) for t in range(3)] for k in range(8)]
        hraw = sb("hraw", [128, 8 * T], BF16)
        hb = [[Buf("h%d_%d" % (k, t)) for t in range(3)] for k in range(8)]
        h_all = [b for row in hb for b in row]

        def hap(k, c0, c1):
            return hraw[:, k * T + c0:k * T + c1]

        m = sb("m", [128, 8, T], BF16)
        mb = [[Buf("m%d_%d" % (k, t)) for t in range(3)] for k in range(8)]
        uA = sb("uA", [128, 2, WA], BF16)
        uAb = [Buf("uA0"), Buf("uA1")]
        QT = sb("QT", [128, 4, T], BF16)
        qb_ = [[Buf("q%d_%d" % (c, t)) for t in range(3)] for c in range(4)]
        KT = sb("KT", [128, 2, T], BF16)
        kb_ = [[Buf("k%d_%d" % (c, t)) for t in range(3)] for c in range(2)]
        Vaug = sb("Vaug", [128, 12, 2, 128], BF16)
        vb = [Buf("v%d" % g) for g in range(3)]
        uC = sb("uC", [128, 2, WC], F32)
        uCb = [Buf("uC0"), Buf("uC1")]
        slabs = sb("slabs", [128, NSL, 8, 128], BF16)
        slb = [Buf("slab%d" % i) for i in range(NSL)]
        diag = sb("diag", [128, NDG, 128], BF16)
        dgb = [Buf("dg%d" % i) for i in range(NDG)]
        rope = sb("rope", [128, 2048], F32)
        ropeb = Buf("rope")
        cst = sb("cst", [128, NCST], BF16)
        cstb = Buf("cst")
        vec = sb("vec", [128, NV], F32)
        vecb = Buf("vec")
        ccs = sb("ccs", [128, 16], F32)
        ccb = Buf("cc")
        csil = sb("csil", [128, 8, 2], BF16)
        csilb = Buf("csil")
        modT2 = sb("modT", [128, 2, 24, 2], F32)
        modb2 = [Buf("mod0"), Buf("mod1")]
        gmod2 = sb("gmod", [128, 2, 8, 2], F32)
        gmodb2 = [Buf("gmod0"), Buf("gmod1")]
        esink = sb("esink", [128, 32], F32)
        esinkb = Buf("esink")
        cpw = sb("cpw", [128, NL * 2 * 256], BF16)
        cpwb = Buf("cpw")
        plw = sb("plw", [128, NL * 2 * 128], BF16)
        plwb = Buf("plw")
        cksrc = sb("cksrc", [128, 4, 2, 2, 64], BF16)
        cksrcb = [Buf("cksrc%d" % i) for i in range(4)]
        ckT = sb("ckT", [128, 2, 512], BF16)
        ckTb = [Buf("ckT0"), Buf("ckT1")]
        cvaug = sb("cvaug", [128, 4, 2, 128], BF16)
        cvb = [Buf("cvaug0"), Buf("cvaug1")]
        PT = sb("PT", [128, 4, 512], BF16)
        PTb = [Buf("pt%d" % i) for i in range(4)]
        Rt3 = sb("Rt3", [128, 3, 512], F32)
        Rb3 = [Buf("R0"), Buf("R1"), Buf("R2")]
        g32 = sb("g32", [128, 4, 512], F32)
        g32b = [Buf("g32_%d" % i) for i in range(4)]
        g16 = sb("g16", [128, 4, 512], BF16)
        g16b = [Buf("g16_%d" % i) for i in range(4)]
        edg = sb("edg", [128, 4, 8], F32)
        edb = [Buf("ed%d" % i) for i in range(4)]
        poolA = hraw[:, 0:2 * WC].bitcast(F32)
        poolB = hraw[:, 2 * WC:4 * WC].bitcast(F32)
        pAb, pBb = Buf("poolA"), Buf("poolB")
        NA32 = 5
        a32 = [hraw[:, 4 * WC + i * 1024:4 * WC + (i + 1) * 1024].bitcast(F32) for i in range(NA32)]
        a32b = [Buf("a32_%d" % i) for i in range(NA32)]
        s16 = sb("s16", [128, 4, 512], BF16)
        s16b = [Buf("s16_%d" % i) for i in range(4)]
        alias_all = [pAb, pBb] + a32b

        pbank = [es.enter_context(nc.psum_tensor("pb%d" % i, [128, 512], F32)) for i in range(8)]
        psb = [Buf("psb%d" % i) for i in range(8)]

        st = {"g32": 0, "g16": 0, "br32": 0, "rd": 0, "ed": 0, "dg": 0, "pt": 0, "sbr": 0, "s16": 0, "ms32": 0, "sq8": 0}

        def t32():
            i = st["g32"] % 4
            st["g32"] += 1
            return g32[:, i, :], g32b[i]

        def b32():
            i = st["br32"] % (NA32 + 1)
            st["br32"] += 1
            if i < NA32:
                return a32[i], a32b[i]
            return g32[:, 3, :], g32b[3]

        def ms32():
            i = st["ms32"] % 3
            st["ms32"] += 1
            return g32[:, i, :], g32b[i]

        def t16():
            i = st["g16"] % 4
            st["g16"] += 1
            return g16[:, i, :], g16b[i]

        def ring(name, n):
            i = st[name] % n
            st[name] += 1
            return i

        def act(out, in_, func, reads, writes, bias=None, scale=None):
            kw = {}
            if bias is not None:
                kw["bias"] = bias
            if scale is not None:
                kw["scale"] = scale
            return S.op("act", lambda e: e.activation(out=out, in_=in_, func=func, **kw), reads, writes)

        def tt(eng, out, in0, in1, op, reads, writes):
            return S.op(eng, lambda e: e.tensor_tensor(out=out, in0=in0, in1=in1, op=op), reads, writes)

        def ts(eng, out, in0, s1, s2, op0, op1, reads, writes):
            if s2 is None:
                return S.op(eng, lambda e: e.tensor_scalar(out=out, in0=in0, scalar1=s1, scalar2=None, op0=op0), reads, writes)
            return S.op(eng, lambda e: e.tensor_scalar(out=out, in0=in0, scalar1=s1, scalar2=s2, op0=op0, op1=op1), reads, writes)

        def stt(eng, out, in0, scalar, in1, op0, op1, reads, writes):
            return S.op(eng, lambda e: e.scalar_tensor_tensor(out=out, in0=in0, scalar=scalar, in1=in1, op0=op0, op1=op1), reads, writes)

        def cp(eng, out, in_, reads, writes):
            return S.op(eng, lambda e: e.tensor_copy(out=out, in_=in_), reads, writes)

        def mm(specs, reads, writes):
            fns = []
            for sp_ in specs:
                o, l, r, a, z = sp_[:5]
                if len(sp_) > 5 and sp_[5]:
                    fns.append(lambda e, o=o, l=l, r=r, a=a, z=z: e.matmul(o, l, r, start=a, stop=z, skip_group_check=True))
                else:
                    fns.append(lambda e, o=o, l=l, r=r, a=a, z=z: e.matmul(o, l, r, start=a, stop=z))
            return S.op("pe", fns, reads, writes)

        def vcol(c, rows=None):
            if rows is None:
                return vec[:, c:c + 1]
            return vec[rows, c:c + 1]

        ident = cst[:, C_ID:C_ID + 128]
        epsc = sb("epsc", [128, 1], F32)
        epsb = Buf("eps")
        S.op("dve", lambda e: e.memset(epsc[:, :], EPS), writes=[epsb])

        sl = {"issued": 0, "next": 0}
        total_slabs = 24 + nl * (SLABS_PER_LAYER - 24) + (nl - 1) * 24

        def slab_issue(i):
            s = i % NSL
            S.dma("pool", slabs[:, s, :, :], wslab_d[i].rearrange("p (k c) -> p k c", k=8), reads=(), writes=[slb[s]])

        def get_slab():
            i = sl["next"]
            sl["next"] += 1
            while sl["issued"] < min(total_slabs, i + NSL):
                slab_issue(sl["issued"])
                sl["issued"] += 1
            s = i % NSL
            return slabs[:, s, :, :], slb[s]

        S.dma("sp", vec[:, :], vecs_d[:, :], writes=[vecb])
        S.dma("sp", ccs[:, :], cc_d[:, :], writes=[ccb])
        S.dma("pool", cst[:, :], cst_d[:, :], writes=[cstb])
        for k in range(8):
            S.dma("sp", x[:, k, :], xT_d[k * 128:(k + 1) * 128, :], writes=xb[k])
        S.op("dve", lambda e: e.memset(uA[:, :, :], 0.0), writes=uAb)
        S.op("pool", lambda e: e.memset(uC[:, :, :], 0.0), writes=uCb)
        S.op("dve", lambda e: e.memset(Vaug[:, :, :, 64:128], 1.0), writes=vb)
        S.op("dve", lambda e: e.memset(cvaug[:, :, :, 64:128], 1.0), writes=cvb)
        act(csil[:, :, :], ccs[:, :].rearrange("p (k t) -> p k t", t=2), AF.Silu, [ccb], [csilb])
        act(esink[:, :], vec[:, V_SINK:V_SINK + 32], AF.Exp, [vecb], [esinkb])

        def tile_cols(t):
            return t * TW, (t + 1) * TW

        def sq_tile():
            i = st["sq8"] % 8
            st["sq8"] += 1
            if i < 4:
                return g16[:, i, :], g16b[i]
            return s16[:, i - 4, :], s16b[i - 4]

        def stats_act(t, k):
            c0, c1 = tile_cols(t)
            sq, sqb = sq_tile()
            act(sq, x[:, k, c0:c1], AF.Square, [xb[k][t]], [sqb])
            return sq, sqb

        def stats_mm(t, sq, sqb, first, last):
            mm([(pbank[5 + t][:, :], cst[:, C_O1024:C_O1024 + 128], sq, first, last)], [cstb, sqb], [psb[5 + t]])

        def stats_sq(t, k, first, last):
            sq, sqb = stats_act(t, k)
            stats_mm(t, sq, sqb, first, last)

        def stats_fin(t, part=None):
            if part in (None, 0):
                act(Rt3[:, t, :], pbank[5 + t][:, :], AF.Ln, [psb[5 + t]], [Rb3[t]], bias=EPS)
            if part in (None, 1):
                act(Rt3[:, t, :], Rt3[:, t, :], AF.Exp, [Rb3[t]], [Rb3[t]], scale=-0.5)

        def ada_slab(l, c, ab=7):
            sap, sbuf_ = get_slab()
            mm([(pbank[ab][:, 2 * c:2 * c + 2], sap[:, k, :], csil[:, k, :], k == 0, k == 7) for k in range(8)],
               [sbuf_, csilb], [psb[ab]])

        def ada_finish(l, ab=7):
            vbase = l * LV
            modT = modT2[:, l % 2]
            modb = modb2[l % 2]
            gmod = gmod2[:, l % 2]
            gmodb = gmodb2[l % 2]
            tt("dve", modT, pbank[ab][:, 0:48].rearrange("p (c t) -> p c t", t=2),
               vec[:, vbase + V_BADA:vbase + V_BADA + 24].unsqueeze(2).to_broadcast([128, 24, 2]), ALU.add,
               [psb[ab], vecb], [modb])
            ts("dve", gmod, modT[:, 8:16, :], 1.0, None, ALU.add, None, [modb], [gmodb])
            tt("dve", gmod, gmod,
               vec[:, vbase + V_NORMW:vbase + V_NORMW + 8].unsqueeze(2).to_broadcast([128, 8, 2]), ALU.mult,
               [gmodb, vecb], [gmodb])

        def layer(l):
            vbase = l * LV
            modT = modT2[:, l % 2]
            modb = modb2[l % 2]
            gmod = gmod2[:, l % 2]
            gmodb = gmodb2[l % 2]
            if stop == 'ada':
                return
            transfer(alias_all, h_all)
            for part_ in range(2):
                for t in range(3):
                    stats_fin(t, part_)
            cnt_n = 0
            for k in range(8):
                for t in range(3):
                    c0, c1 = tile_cols(t)
                    col = 0 if t == 0 else 1
                    xr, xrb = t32()
                    eng_ = "pool" if (k + t) % 3 == 2 else "dve"
                    cnt_n += 1
                    tt(eng_, xr, x[:, k, c0:c1], Rt3[:, t, :], ALU.mult, [xb[k][t], Rb3[t]], [xrb])
                    if k % 2 == 0:
                        act(hap(k, c0, c1), xr, AF.Identity, [xrb, gmodb, modb], [hb[k][t]],
                            bias=modT[:, k, col:col + 1], scale=gmod[:, k, col:col + 1])
                    else:
                        ts("dve", hap(k, c0, c1), xr, gmod[:, k, col:col + 1], modT[:, k, col:col + 1], ALU.mult, ALU.add,
                           [xrb, gmodb, modb], [hb[k][t]])

            if stop == 'norm':
                return
            for si, (kind, idx) in enumerate(INPROJ):
                if stop is not None and stop.startswith('inproj:') and si >= int(stop.split(':')[1]):
                    return
                sap, sbuf_ = get_slab()
                base = 0 if si % 2 == 0 else 3
                if kind == "ktok":
                    bank, bb = pbank[base], psb[base]
                    mm([(bank[:, tb * 128:(tb + 1) * 128], hap(k, tb * 128, (tb + 1) * 128), sap[:, k, :], k == 0, k == 7)
                        for tb in range(4) for k in range(8)],
                       [sbuf_] + [hb[k][0] for k in range(8)], [bb])
                    ko, kob = t32()
                    cp("dve", ko, bank[:, :], [bb], [kob])
                    for s_ in range(2):
                        S.dma("sp", nk_d[s_, l].rearrange("(r p) c -> p r c", p=128),
                              ko[:, s_ * 256:(s_ + 1) * 256].rearrange("p (r c) -> p r c", r=2), reads=[kob], is_output=True)
                    continue
                if kind == "vtok":
                    for g in range(3):
                        bank, bb = pbank[base + g], psb[base + g]
                        mm([(bank[:, tb * 128:(tb + 1) * 128], hap(k, (g * 4 + tb) * 128, (g * 4 + tb + 1) * 128), sap[:, k, :], k == 0, k == 7)
                            for tb in range(4) for k in range(8)],
                           [sbuf_] + [hb[k][g] for k in range(8)], [bb])
                        if g == 0:
                            vo, vob = t32()
                            cp("dve", vo, bank[:, :], [bb], [vob])
                            for s_ in range(2):
                                S.dma("sp", nv_d[s_, l].rearrange("(r p) c -> p r c", p=128),
                                      vo[:, s_ * 256:(s_ + 1) * 256].rearrange("p (r c) -> p r c", r=2), reads=[vob], is_output=True)
                        for kv in range(2):
                            if os.environ.get("VT_SKIP") == "act":
                                continue
                            cp("dve", Vaug[:, g * 4:(g + 1) * 4, kv, 0:64],
                               bank[:, :].rearrange("p (b v d) -> p b v d", b=4, v=2)[:, :, kv, :], [bb], [vb[g]])
                    continue
                banks = [pbank[base + t] for t in range(3)]
                bbs = [psb[base + t] for t in range(3)]
                if si == 0:
                    for k in range(8):
                        mm([(banks[t][:, :], sap[:, k, :], hap(k, t * TW, (t + 1) * TW), k == 0, k == 7) for t in range(3)],
                           [sbuf_] + hb[k], bbs)
                else:
                    mm([(banks[t][:, :], sap[:, k, :], hap(k, t * TW, (t + 1) * TW), k == 0, k == 7)
                        for k in range(8) for t in range(3)],
                       [sbuf_] + h_all, bbs)
                for t in range(3):
                    c0, c1 = tile_cols(t)
                    ps = banks[t][:, :]
                    pb_ = bbs[t]
                    if kind in ("glu", "val"):
                        if t == 0:
                            dst = uA[:, idx, PA:PA + 2 * 286].rearrange("p (s w) -> p s w", s=2)[:, :, 0:256]
                            src = ps.rearrange("p (s w) -> p s w", s=2)
                        else:
                            dst = uA[:, idx, UA_BASE[2] + (t - 1) * TW:UA_BASE[2] + t * TW]
                            src = ps
                        if kind == "glu":
                            act(dst, src, AF.Sigmoid, [pb_], [uAb[idx]])
                        else:
                            tt("dve", dst, src, dst, ALU.mult, [pb_, uAb[idx]], [uAb[idx]])
                    elif kind == "agate":
                        act(m[:, idx, c0:c1], ps, AF.Silu, [pb_], [mb[idx][t]])
                    elif kind == "bgate":
                        act(m[:, 2 + idx, c0:c1], ps, AF.Silu, [pb_], [mb[2 + idx][t]])
                    elif kind == "cgate":
                        act(m[:, 6 + idx, c0:c1], ps, AF.Silu, [pb_], [mb[6 + idx][t]])
                    elif kind == "cin":
                        if t == 0:
                            dst = uC[:, idx, PC:PC + 2 * 272].rearrange("p (s w) -> p s w", s=2)[:, :, 0:256]
                            src = ps.rearrange("p (s w) -> p s w", s=2)
                        else:
                            dst = uC[:, idx, UC_BASE[2] + (t - 1) * TW:UC_BASE[2] + t * TW]
                            src = ps
                        cp("dve", dst, src, [pb_], [uCb[idx]])
                    elif kind in ("q", "kd"):
                        if kind == "q":
                            dst, dstb = QT[:, idx, c0:c1], qb_[idx][t]
                        else:
                            dst, dstb = KT[:, idx, c0:c1], kb_[idx][t]
                        if t == 0:
                            cp("dve", dst, ps, [pb_], [dstb])
                        else:
                            r0 = (t - 1) * TW
                            qraw, qrb = t16()
                            act(qraw, ps, AF.Copy, [pb_], [qrb])
                            t1, t1b = t32()
                            tt("dve", t1, ps, rope[:, r0:r0 + TW], ALU.mult, [pb_, ropeb], [t1b])
                            rb_i = 6 + (t % 2)
                            mm([(pbank[rb_i][:, :], cst[:, C_RT:C_RT + 128], qraw, True, True)], [cstb, qrb], [psb[rb_i]])
                            t2, t2b = t32()
                            tt("dve", t2, pbank[rb_i][:, :], rope[:, 1024 + r0:1024 + r0 + TW], ALU.mult, [psb[rb_i], ropeb], [t2b])
                            tt("pool", dst, t1, t2, ALU.add, [t1b, t2b], [dstb])

            if stop == 'inproj':
                return
            transfer(h_all, alias_all)

            for kv in range(2):
                for dup in range(2):
                    S.dma("pool", cksrc[:, :, kv, dup, :], ck_d[l][:, kv * 64:(kv + 1) * 64].rearrange("(ct p) d -> p ct d", p=128),
                          writes=[cksrcb[kv * 2 + dup]])
                S.dma("pool", cvaug[:, :, kv, 0:64], cv_d[l][:, kv * 64:(kv + 1) * 64].rearrange("(ct p) d -> p ct d", p=128),
                      writes=[cvb[kv]])

            if stop == 'ctxdma':
                return
            thunks = []

            def lvl(eng, dst, src, sh_lo, sh_hi, lo, hi, rows, reads, writes):
                tt(eng, dst[rows, lo:hi], src[rows, lo + sh_lo:hi + sh_lo], src[rows, lo + sh_hi:hi + sh_hi], ALU.add, reads, writes)

            lo_r, hi_r = slice(0, 64), slice(64, 128)

            def pool_tile(j, t):
                u = uC[:, j, :]
                ub = uCb[j]
                tabc = V_PTAB + j * 17
                c0, c1 = tile_cols(t)
                d16, d16b = t16()
                if t == 0:
                    dst = d16.rearrange("p (s w) -> p s w", s=2)
                    sS = poolB[:, PC:PC + 2 * 272].rearrange("p (s w) -> p s w", s=2)[:, :, 0:256]
                    sU = uC[:, j, PC:PC + 2 * 272].rearrange("p (s w) -> p s w", s=2)[:, :, 0:256]
                else:
                    o = UC_BASE[2] + (t - 1) * TW
                    dst, sS, sU = d16, poolB[:, o:o + TW], uC[:, j, o:o + TW]
                stt("dve", dst, sS, vcol(tabc), sU, ALU.mult, ALU.subtract, [pBb, ub, vecb], [d16b])
                fixes = []
                if t == 0:
                    for s_ in range(2):
                        fixes.append((s_ * 256, UC_BASE[s_], tabc + 1))
                        fixes.append((s_ * 256 + 248, UC_BASE[s_] + 248, tabc + 9))
                elif t == 1:
                    fixes.append((0, UC_BASE[2], tabc + 1))
                else:
                    fixes.append((504, UC_BASE[2] + 1016, tabc + 9))
                for (dc, pc_, tc) in fixes:
                    ei = ring("ed", 4)
                    tt("dve", edg[:, ei, :], poolB[:, pc_:pc_ + 8], vec[:, tc:tc + 8], ALU.mult, [pBb, vecb], [edb[ei]])
                    tt("dve", d16[:, dc:dc + 8], edg[:, ei, :], uC[:, j, pc_:pc_ + 8], ALU.subtract, [edb[ei], ub, d16b], [d16b])
                ob = ring("sbr", 3)
                mm([(pbank[ob][:, :], plw[:, (l * 2 + j) * 128:(l * 2 + j + 1) * 128], d16, True, True)], [plwb, d16b], [psb[ob]])
                stt("dve", m[:, 6 + j, c0:c1], pbank[ob][:, :], vcol(vbase + V_PSCALE + j), m[:, 6 + j, c0:c1],
                    ALU.mult, ALU.mult, [psb[ob], vecb, mb[6 + j][t]], [mb[6 + j][t]])

            for j in range(2):
                u = uC[:, j, :]
                ub = uCb[j]
                L_ = lambda *a: thunks.append(lambda a=a: lvl("pool", *a))
                if j == 0:
                    L_(poolB, u, -1, 0, 1, WC, lo_r, [ub], [pBb])
                    L_(poolA, u, -1, 0, 1, WC, hi_r, [ub], [pAb])
                    L_(poolB, poolA, -1, 1, 2, WC - 1, hi_r, [pAb], [pBb])
                else:
                    L_(poolB, u, -1, 0, 1, WC, lo_r, [ub], [pBb])
                    L_(poolA, u, -1, 0, 1, WC, hi_r, [ub], [pAb])
                    L_(poolA, poolB, -1, 1, 2, WC - 1, lo_r, [pBb], [pAb])
                    L_(poolB, poolA, -1, 1, 2, WC - 1, hi_r, [pAb], [pBb])
                    L_(poolB, poolA, -2, 2, 4, WC - 3, lo_r, [pAb], [pBb])
                    L_(poolA, poolB, -2, 2, 4, WC - 3, hi_r, [pBb], [pAb])
                    L_(poolB, poolA, -4, 4, 8, WC - 7, hi_r, [pAb], [pBb])
                for t in range(3):
                    thunks.append(lambda j=j, t=t: pool_tile(j, t))

            work = []
            lev0, til0, lev1, til1 = thunks[0:3], thunks[3:6], thunks[6:13], thunks[13:16]
            thunks = []
            for f_ in lev0:
                f_()
            CB = (5, 6)

            conv_list = [(t, j, k) for t in range(3) for j in range(2) for k in range(31)]
            conv_dg = {}
            DG_AHEAD = 6

            def build_diag(n):
                if n >= len(conv_list):
                    return
                t, j, k = conv_list[n]
                di = ring("dg", NDG)
                ts("dve", diag[:, di, :], ident, vcol(vbase + V_DW + k * 2 + j), None, ALU.mult, None,
                   [cstb, vecb], [dgb[di]])
                conv_dg[n] = di

            def conv_item(n):
                t, j, k = conv_list[n]
                di = conv_dg.pop(n)
                bank, bb = pbank[CB[j]], psb[CB[j]]
                if t == 0:
                    spec = (bank[:, :].rearrange("p (s w) -> p s w", s=2), diag[:, di, :],
                            uA[:, j, k:k + 2 * 286].rearrange("p (s w) -> p s w", s=2)[:, :, 0:256], k == 0, k == 30)
                else:
                    o = UA_BASE[2] - PA + (t - 1) * TW + k
                    spec = (bank[:, :], diag[:, di, :], uA[:, j, o:o + TW], k == 0, k == 30)
                mm([spec], [dgb[di], uAb[j]], [bb])
                build_diag(n + DG_AHEAD)

            for n_ in range(DG_AHEAD):
                build_diag(n_)

            LN = {}

            def ln_A1(t):
                y32 = []
                for j in range(2):
                    bank, bb = pbank[CB[j]], psb[CB[j]]
                    y, yb = b32()
                    act(y, bank[:, :], AF.Identity, [bb, vecb], [yb], bias=vcol(vbase + V_CONVB + j))
                    y32.append((y, yb))
                LN[t] = {"y": y32}

            def ln_A2(t):
                aux = []
                for j in range(2):
                    y, yb = LN[t]["y"][j]
                    ysq, ysqb = t16()
                    act(ysq, y, AF.Square, [yb], [ysqb])
                    y16, y16b = t16()
                    act(y16, y, AF.Copy, [yb], [y16b])
                    aux.append((ysq, ysqb, y16, y16b))
                LN[t]["aux"] = aux

            def ln_A3(t):
                r6, r7 = ring("sbr", 3), ring("sbr", 3)
                for j in range(2):
                    ysq, ysqb, y16, y16b = LN[t]["aux"][j]
                    mm([(pbank[r6][:, :], cst[:, C_O256:C_O256 + 128], y16, j == 0, j == 1)], [cstb, y16b], [psb[r6]])
                    mm([(pbank[r7][:, :], cst[:, C_O256:C_O256 + 128], ysq, j == 0, j == 1)], [cstb, ysqb], [psb[r7]])
                mean, meanb = ms32()
                act(mean, pbank[r6][:, :], AF.Copy, [psb[r6]], [meanb])
                msq, msqb = ms32()
                act(msq, pbank[r6][:, :], AF.Square, [psb[r6]], [msqb])
                tt("dve", msq, pbank[r7][:, :], msq, ALU.subtract, [psb[r7], msqb], [msqb])
                act(msq, msq, AF.Ln, [msqb, epsb], [msqb], bias=epsc[:, 0:1])
                act(msq, msq, AF.Exp, [msqb], [msqb], scale=-0.5)
                for j in range(2):
                    y, yb = LN[t]["y"][j]
                    tt("dve", y, y, mean, ALU.subtract, [yb, meanb], [yb])
                    tt("dve", y, y, msq, ALU.mult, [yb, msqb], [yb])

            def ln_C(t):
                sj = []
                for j in range(2):
                    y, yb = LN[t]["y"][j]
                    si_ = ring("s16", 4)
                    act(s16[:, si_, :], y, AF.Silu, [yb, vecb], [s16b[si_]], bias=vcol(vbase + V_LNB + j), scale=vcol(vbase + V_LNG + j))
                    sj.append((s16[:, si_, :], s16b[si_]))
                LN[t]["s"] = sj

            def ln_D(t):
                c0, c1 = tile_cols(t)
                sj = LN[t]["s"]
                for jo in range(2):
                    ob = ring("sbr", 3)
                    mm([(pbank[ob][:, :], cpw[:, (l * 2 + ji) * 256 + jo * 128:(l * 2 + ji) * 256 + jo * 128 + 128], sj[ji][0], ji == 0, ji == 1)
                        for ji in range(2)], [cpwb, sj[0][1], sj[1][1]], [psb[ob]])
                    tt("dve", m[:, jo, c0:c1], pbank[ob][:, :], m[:, jo, c0:c1], ALU.mult, [psb[ob], mb[jo][t]], [mb[jo][t]])

            lnq = []
            for t in range(3):
                for j in range(2):
                    for k in range(31):
                        work.append(lambda n=(t * 2 + j) * 31 + k: conv_item(n))
                        if lnq and (j * 31 + k) % 3 == 2:
                            work.append(lnq.pop(0))
                lnq = [lambda t=t: ln_A1(t), lambda t=t: ln_A2(t), lambda t=t: ln_A3(t), lambda t=t: ln_C(t), lambda t=t: ln_D(t)]
                work.append(lnq.pop(0))
            work.extend(lnq)
            work[40:40] = til0 + lev1
            work.extend(til1)
            ln_thunks = []

            if stop == 'pool':
                return
            for kv in range(2):
                bi = 3 + kv
                mm([(pbank[bi][:, ct * 128:(ct + 1) * 128], cksrc[:, ct, kv, :, :].rearrange("p a d -> p (a d)"), ident, True, True) for ct in range(4)],
                   [cksrcb[kv * 2], cksrcb[kv * 2 + 1], cstb], [psb[bi]])
                cp("dve", ckT[:, kv, :], pbank[bi][:, :], [psb[bi]], [ckTb[kv]])

            rws = [slice(0, 64), slice(64, 128)]

            def unit_tiles(unit):
                tiles = []
                if unit[0] == "p":
                    _, s_, hd = unit
                    c, half, kvh = hd // 2, hd % 2, hd // 4
                    rows = rws[half]
                    tiles.append(dict(
                        smm=[(KT[rows, kvh, (s_ * 2 + kt) * 128:(s_ * 2 + kt + 1) * 128], QT[rows, c, s_ * 256:(s_ + 1) * 256], kt * 256, 256)
                             for kt in range(2)],
                        ncols=512, masks=[], srd=[kb_[kvh][0], qb_[c][0]],
                        pv=[(Vaug[:, s_ * 2 + kt, kvh, :], kt * 256, 256, 0, 0) for kt in range(2)], prd=[vb[0]]))
                    return tiles
                _, hd = unit
                c, half, kvh = hd // 2, hd % 2, hd // 4
                rows = rws[half]
                loc, ctx = [], []
                for j in range(8):
                    lo, hi = max(0, j - 1), min(8, j + 2)
                    n = (hi - lo) * 128
                    masks = []
                    for qbk in range(lo, hi):
                        if qbk == j - 1:
                            masks.append(((qbk - lo) * 128, C_MNEXT))
                        elif qbk == j + 1:
                            masks.append(((qbk - lo) * 128, C_MPREV))
                    pv = []
                    for b_ in range(2):
                        a0, a1 = max(lo * 128, b_ * 512), min(hi * 128, (b_ + 1) * 512)
                        if a1 > a0:
                            pv.append((Vaug[:, 4 + j, kvh, :], a0 - lo * 128, a1 - a0, b_, a0 - b_ * 512))
                    loc.append(dict(
                        smm=[(KT[rows, kvh, 512 + j * 128:512 + (j + 1) * 128], QT[rows, c, 512 + lo * 128:512 + hi * 128], 0, n)],
                        ncols=n, masks=masks, srd=[kb_[kvh][1 + j // 4], qb_[c][1], qb_[c][2]],
                        pv=pv, prd=[vb[1 + j // 4]]))
                for qh in range(2):
                    for ct in range(4):
                        ctx.append(dict(
                            smm=[(ckT[rows, kvh, ct * 128:(ct + 1) * 128], QT[rows, c, 512 + qh * 512:512 + (qh + 1) * 512], 0, 512)],
                            ncols=512, masks=[], srd=[ckTb[kvh], qb_[c][1 + qh]],
                            pv=[(cvaug[:, ct, kvh, :], 0, 512, qh, 0)], prd=[cvb[kvh]]))
                for i in range(8):
                    tiles.append(ctx[i])
                    tiles.append(loc[i])
                return tiles

            units = [("p", s_, hd) for s_ in range(2) for hd in range(8)] + [("s", hd) for hd in range(8)]
            flat = []
            for ui, unit in enumerate(units):
                tl = unit_tiles(unit)
                for ti, tile in enumerate(tl):
                    flat.append((ui, unit, tile, ti == len(tl) - 1))
            ostart = {}
            unit_first = {}
            for (ui_, unit_, tile_, last_) in flat:
                unit_first.setdefault(ui_, tile_)

            def emit_S(idx, tile):
                sbi = ring("sbr", 3)
                fill = []
                mm(fill + [(pbank[sbi][:, oc:oc + n], kt_, q_, True, True) for (kt_, q_, oc, n) in tile["smm"]],
                   tile["srd"] + [cstb, mb[0][0]], [psb[sbi]])
                pti = ring("pt", 4)
                tile["pti"] = pti
                nco = tile["ncols"]
                act(PT[:, pti, 0:nco], pbank[sbi][:, 0:nco], AF.Exp, [psb[sbi]], [PTb[pti]], scale=0.125)
                for (c0_, mk) in tile["masks"]:
                    tt("dve", PT[:, pti, c0_:c0_ + 128], PT[:, pti, c0_:c0_ + 128], cst[:, mk:mk + 128], ALU.mult,
                       [PTb[pti], cstb], [PTb[pti]])

            def norm_piece(Ob, Obb, n, hd, mcol0, qt):
                c, half = hd // 2, hd % 2
                ro = rws[half]
                rd, rdb_ = ms32()
                tmp, tmpb = ms32()
                act(rd[64:128, 0:n], Ob[64:128, 0:n], AF.Ln, [Obb, esinkb], [rdb_], bias=esink[64:128, l * 8 + hd:l * 8 + hd + 1])
                act(rd[64:128, 0:n], rd[64:128, 0:n], AF.Exp, [rdb_], [rdb_], scale=-1.0)
                tt("dve", tmp[ro, 0:n], Ob[0:64, 0:n], rd[64:128, 0:n], ALU.mult, [Obb, rdb_], [tmpb])
                tt("dve", m[ro, 2 + c, mcol0:mcol0 + n], tmp[ro, 0:n], m[ro, 2 + c, mcol0:mcol0 + n], ALU.mult,
                   [tmpb, mb[2 + c][qt]], [mb[2 + c][qt]])

            def emit_PV(idx, ui, unit, tile, last):
                pti = tile["pti"]
                oset = (3, 4) if (unit[0] == 's' or ui % 2 == 0) else (4, 3)
                for (va, pc0, n, ob_, oc0) in tile["pv"]:
                    bi = oset[ob_]
                    first = (ui, ob_) not in ostart
                    ostart[(ui, ob_)] = True
                    mm([(pbank[bi][:, oc0:oc0 + n], va, PT[:, pti, pc0:pc0 + n], first, True, True)],
                       [PTb[pti]] + tile["prd"], [psb[bi]])
                if last:
                    if unit[0] == "p":
                        _, s_, hd = unit
                        norm_piece(pbank[oset[0]], psb[oset[0]], 256, hd, s_ * 256, 0)
                    else:
                        _, hd = unit
                        for b_ in range(2):
                            norm_piece(pbank[oset[b_]], psb[oset[b_]], 512, hd, 512 + b_ * 512, 1 + b_)
                    if l + 1 < nl:
                        ada_slab(l + 1, ui)
                    if ln_thunks:
                        ln_thunks.pop(0)()

            pend = []
            wacc = [0.0]
            wrate = (len(work) + 4.0) / max(1, len(flat) - 8)
            for idx, (ui, unit, tile, last) in enumerate(flat):
                emit_S(idx, tile)
                wacc[0] += wrate
                while wacc[0] >= 1.0 and work:
                    work.pop(0)()
                    wacc[0] -= 1.0
                pend.append((idx, ui, unit, tile, last))
                if len(pend) > 2:
                    emit_PV(*pend.pop(0))
            while pend:
                emit_PV(*pend.pop(0))
            while work:
                work.pop(0)()

            if l + 1 < nl:
                ada_finish(l + 1)

            if stop == 'attn':
                return
            m_all = [b for row in mb for b in row]
            pend_stats = []
            for mo in range(8):
                sap, sbuf_ = get_slab()
                bidx = [(3 * mo + t) % 5 for t in range(3)]
                banks = [pbank[i] for i in bidx]
                bbs = [psb[i] for i in bidx]
                mm([(banks[t][:, :], sap[:, k, :], m[:, k, t * TW:(t + 1) * TW], k == 0, k == 7)
                    for k in range(8) for t in range(3)], [sbuf_] + m_all, bbs)
                for (t_, sq_, sqb_, mo_) in pend_stats:
                    stats_mm(t_, sq_, sqb_, mo_ == 0, mo_ == 7)
                pend_stats = []
                for t in range(3):
                    c0, c1 = tile_cols(t)
                    col = 0 if t == 0 else 1
                    stt("dve", x[:, mo, c0:c1], banks[t][:, :], modT[:, 16 + mo, col:col + 1], x[:, mo, c0:c1],
                        ALU.mult, ALU.add, [bbs[t], modb, xb[mo][t]], [xb[mo][t]])
                    sq_, sqb_ = stats_act(t, mo)
                    pend_stats.append((t, sq_, sqb_, mo))
            for (t_, sq_, sqb_, mo_) in pend_stats:
                stats_mm(t_, sq_, sqb_, mo_ == 0, mo_ == 7)

        for t in range(3):
            for k in range(8):
                stats_sq(t, k, k == 0, k == 7)
        for c in range(24):
            ada_slab(0, c, 0)
        ada_finish(0, 0)
        S.dma("sp", rope[:, :], rope_d[:, :], writes=[ropeb])
        S.dma("pool", cpw[:, :], cpw_d[:, :], writes=[cpwb])
        S.dma("pool", plw[:, :], plw_d[:, :], writes=[plwb])
        for l in range(nl):
            layer(l)

        for part_ in range(2):
            for t in range(3):
                stats_fin(t, part_)
        for t in range(3):
            c0, c1 = tile_cols(t)
            for k in range(8):
                yo, yob = t32()
                stt("dve", yo, x[:, k, c0:c1], vcol(V_FNW + k), Rt3[:, t, :], ALU.mult, ALU.mult, [xb[k][t], Rb3[t], vecb], [yob])
                S.dma("sp", yT_d[k * 128:(k + 1) * 128, c0:c1], yo, reads=[yob], is_output=True)
        S.finish()
    return nc


def _const_tables():
    cst = np.zeros((128, NCST), np.float32)
    cst[:, C_ID:C_ID + 128] = np.eye(128, dtype=np.float32)
    R = np.zeros((64, 64), np.float32)
    for d in range(16):
        R[d, d + 16] = -1.0
        R[d + 16, d] = 1.0
        R[32 + d, 48 + d] = -1.0
        R[48 + d, 32 + d] = 1.0
    Rt = np.zeros((128, 128), np.float32)
    Rt[0:64, 0:64] = R.T
    Rt[64:128, 64:128] = R.T
    cst[:, C_RT:C_RT + 128] = Rt
    cst[:, C_O1024:C_O1024 + 128] = 1.0 / 1024.0
    cst[:, C_O256:C_O256 + 128] = 1.0 / 256.0
    jk = np.arange(128)[:, None]
    r = np.arange(128)[None, :]
    cst[:, C_MNEXT:C_MNEXT + 128] = (jk <= r).astype(np.float32)
    cst[:, C_MPREV:C_MPREV + 128] = (jk >= r).astype(np.float32)
    quarter = 16
    freqs = 10000.0 ** (-np.arange(quarter, dtype=np.float64) / quarter)
    tpos = np.arange(1024)
    rows = (tpos // 64).astype(np.float64)
    cols = (tpos % 64).astype(np.float64)
    rope = np.zeros((128, 2048), np.float32)
    for p in range(128):
        d = p % 64
        pos = rows if d < 32 else cols
        ang = pos * freqs[d % 16]
        rope[p, 0:1024] = np.cos(ang).astype(np.float32)
        rope[p, 1024:2048] = np.sin(ang).astype(np.float32)
    ptab = np.zeros((128, 34), np.float32)
    for j in range(2):
        for p in range(128):
            w = POOL_WINDOWS[2 * j + p // 64]
            ptab[p, j * 17] = 1.0 / w
            for t in range(8):
                cnt = t + w // 2 if t < w // 2 else w
                ptab[p, j * 17 + 1 + t] = 1.0 / cnt
            for c in range(8):
                rr = 7 - c
                cnt = rr + 1 + w // 2 if w // 2 >= rr + 1 else w
                ptab[p, j * 17 + 9 + c] = 1.0 / cnt
    return cst, rope, ptab


def _colvec(v, n):
    return np.ascontiguousarray(np.asarray(v, np.float32).reshape(n, 128).T)


_PROG = {}


def kernel(x_prompt, x_sample, c, cache_k, cache_v, c_ctx, w_ada, b_ada, norm_w, w_in,
           conv_dw, conv_b, conv_ln_g, conv_ln_b, conv_pw, attn_sink, pool_w, pool_scale,
           w_out, final_norm_w, _nl=NL, _stop=None):
    f = lambda a: np.asarray(a, np.float32)
    x_prompt, x_sample, c, cache_k, cache_v, c_ctx = map(f, (x_prompt, x_sample, c, cache_k, cache_v, c_ctx))
    w_ada, b_ada, norm_w, w_in, conv_dw, conv_b = map(f, (w_ada, b_ada, norm_w, w_in, conv_dw, conv_b))
    conv_ln_g, conv_ln_b, conv_pw, attn_sink, pool_w, pool_scale = map(f, (conv_ln_g, conv_ln_b, conv_pw, attn_sink, pool_w, pool_scale))
    w_out, final_norm_w = f(w_out), f(final_norm_w)
    nl = _nl
    cst, rope, ptab = _const_tables()

    wslab = np.empty((NL * SLABS_PER_LAYER, 128, 1024), np.float32)

    def put(i, W, cols):
        wslab[i] = W[:, cols].reshape(8, 128, 128).transpose(1, 0, 2).reshape(128, 1024)

    i = 0
    for cc_ in range(24):
        put(i, w_ada[0], np.arange(cc_ * 128, cc_ * 128 + 128))
        i += 1
    for l in range(NL):
        for kind, idx in INPROJ:
            put(i, w_in[l], _slab_cols(kind, idx))
            i += 1
        if l + 1 < NL:
            for cc_ in range(24):
                put(i, w_ada[l + 1], np.arange(cc_ * 128, cc_ * 128 + 128))
                i += 1
        for mo in range(8):
            put(i, w_out[l], np.arange(mo * 128, mo * 128 + 128))
            i += 1

    vecs_base = np.zeros((128, NV), np.float32)
    for l in range(NL):
        b = l * LV
        vecs_base[:, b + V_NORMW:b + V_NORMW + 8] = _colvec(norm_w[l], 8)
        vecs_base[:, b + V_BADA:b + V_BADA + 24] = _colvec(b_ada[l], 24)
        vecs_base[:, b + V_CONVB:b + V_CONVB + 2] = _colvec(conv_b[l], 2)
        vecs_base[:, b + V_LNG:b + V_LNG + 2] = _colvec(conv_ln_g[l], 2)
        vecs_base[:, b + V_LNB:b + V_LNB + 2] = _colvec(conv_ln_b[l], 2)
        vecs_base[:, b + V_PSCALE:b + V_PSCALE + 2] = _colvec(pool_scale[l], 2)
        for k in range(31):
            vecs_base[:, b + V_DW + k * 2:b + V_DW + k * 2 + 2] = _colvec(conv_dw[l, k], 2)
    vecs_base[:, V_FNW:V_FNW + 8] = _colvec(final_norm_w, 8)
    vecs_base[:, V_SINK:V_SINK + 32] = np.broadcast_to(attn_sink.reshape(1, 32), (128, 32))
    vecs_base[:, V_PTAB:V_PTAB + 34] = ptab

    cpw = np.zeros((128, NL, 2, 256), np.float32)
    plw = np.zeros((128, NL, 2, 128), np.float32)
    for l in range(NL):
        for ji in range(2):
            cpw[:, l, ji, :] = conv_pw[l, ji * 128:(ji + 1) * 128, :]
        for g in range(4):
            j, hf = g // 2, g % 2
            plw[hf * 64:(hf + 1) * 64, l, j, hf * 64:(hf + 1) * 64] = pool_w[l, g]
    cpw = cpw.reshape(128, -1)
    plw = plw.reshape(128, -1)

    in_maps = []
    for i in range(NCORES):
        xt = np.concatenate([x_prompt[2 * i], x_prompt[2 * i + 1], x_sample[i]], axis=0)
        cc = np.stack([_colvec(c_ctx, 8), _colvec(c[i], 8)], axis=2).reshape(128, 16)
        in_maps.append({
            "xT": np.ascontiguousarray(xt.T),
            "cc": np.ascontiguousarray(cc),
            "vecs": vecs_base,
            "wslab": wslab,
            "cpw": cpw,
            "plw": plw,
            "ck": np.ascontiguousarray(cache_k[i].reshape(NL, 512, 128)),
            "cv": np.ascontiguousarray(cache_v[i].reshape(NL, 512, 128)),
            "rope": rope,
            "cst": cst,
        })
    if (nl, _stop) not in _PROG:
        _PROG[(nl, _stop)] = build_program(nl, _stop)
    res = run_bass_kernel_spmd(_PROG[(nl, _stop)], in_maps, core_ids=list(range(NCORES)))
    y_prompt = np.empty((16, 256, D), np.float32)
    y_sample = np.empty((8, 1024, D), np.float32)
    nk = np.empty((16, NL, 256, 2, 64), np.float32)
    nv = np.empty((16, NL, 256, 2, 64), np.float32)
    for i in range(NCORES):
        r = res.results[i]
        yt = np.asarray(r["yT"]).T
        y_prompt[2 * i] = yt[0:256]
        y_prompt[2 * i + 1] = yt[256:512]
        y_sample[i] = yt[512:1536]
        nk[2 * i:2 * i + 2] = np.asarray(r["nk"]).reshape(2, NL, 256, 2, 64)
        nv[2 * i:2 * i + 2] = np.asarray(r["nv"]).reshape(2, NL, 256, 2, 64)
    return (y_prompt, y_sample, nk, nv)
```
